# Optimizing a Trainium2 kernel written in Bass

```python
import jax, jax.numpy as jnp
from jax import lax
import numpy as np

D_MODEL = 2048
BATCH = 32
SEQ = 256
DEPTH = 4
DEC_BATCH = 4
DEC_SEQ = 1024
PAST_LEN = 256

GRID_W = 64
WA = 512
WB = 512
N_POOL_GROUPS = 4
POOL_GC = WB // N_POOL_GROUPS
POOL_WINDOWS = (2, 4, 8, 16)
WC = 1024
N_HEADS = 4
HEAD_DIM = WC // N_HEADS
CONV_A_WIDTH = 31
FFN_CONV_WIDTH = 3
D_FF = 5632
CHUNK = 64
N_BRANCH = 3
N_MOD = 6
EPS = 1e-6
FORGET_BIAS = 3.0

OFF_GLU = 0
OFF_POOL = OFF_GLU + 2 * WA
OFF_QKV = OFF_POOL + WB
OFF_OG = OFF_QKV + 3 * WC
OFF_GATES = OFF_OG + WC
OFF_MERGE = OFF_GATES + 4 * N_HEADS
N_IN = OFF_MERGE + N_BRANCH * D_MODEL

kernel_name = 'hybrid_diffusion_conv_pool_mlstm_step'


def rmsnorm(x, w):
    xf = x.astype(jnp.float32)
    y = xf * lax.rsqrt(jnp.mean(xf * xf, axis=-1, keepdims=True) + EPS)
    return y.astype(x.dtype) * w


def layernorm(x, w, b):
    xf = x.astype(jnp.float32)
    mu = jnp.mean(xf, axis=-1, keepdims=True)
    var = jnp.mean(jnp.square(xf - mu), axis=-1, keepdims=True)
    return ((xf - mu) * lax.rsqrt(var + EPS)).astype(x.dtype) * w + b


def dwconv(x, w, b):
    y = lax.conv_general_dilated(x, w[:, None, :].astype(x.dtype), (1,), 'SAME',
                                 dimension_numbers=('NWC', 'WIO', 'NWC'),
                                 feature_group_count=x.shape[-1])
    return y + b


def along_rows(fn, x, rows):
    if rows is None:
        return fn(x)
    B, L, C = x.shape
    return fn(x.reshape(B * rows, GRID_W, C)).reshape(B, L, -1)


def along_cols(fn, x, rows):
    if rows is None:
        return fn(x)
    B, L, C = x.shape
    xc = x.reshape(B, rows, GRID_W, C).transpose(0, 2, 1, 3).reshape(B * GRID_W, rows, C)
    y = fn(xc)
    return y.reshape(B, GRID_W, rows, -1).transpose(0, 2, 1, 3).reshape(B, L, -1)


def centred_pool_minus_self(x, window):
    L = x.shape[1]
    xf = x.astype(jnp.float32)
    csum = jnp.concatenate([jnp.zeros_like(xf[:, :1]), jnp.cumsum(xf, axis=1)], axis=1)
    t = jnp.arange(L)
    lo = jnp.clip(t - window // 2, 0, L)
    hi = jnp.clip(t - window // 2 + window, 0, L)
    mean = (csum[:, hi] - csum[:, lo]) / (hi - lo).astype(jnp.float32)[None, :, None]
    return (mean - xf).astype(x.dtype)


def pool_groups(x):
    return jnp.concatenate([centred_pool_minus_self(x[..., g * POOL_GC:(g + 1) * POOL_GC], POOL_WINDOWS[g])
                            for g in range(N_POOL_GROUPS)], axis=-1)


def mlstm_scan(q, k, v, log_i, log_f, C0, n0, m0):
    B, L, H, Dh = q.shape
    nc = L // CHUNK

    def chunks(a):
        return a.reshape(B, nc, CHUNK, H, Dh).transpose(1, 0, 3, 2, 4)

    def gchunks(a):
        return a.reshape(B, nc, CHUNK, H).transpose(1, 0, 3, 2)

    causal = jnp.tril(jnp.ones((CHUNK, CHUNK), dtype=bool))

    def step(carry, inp):
        C, n, m = carry
        qc, kc, vc, li, lf = inp
        b = jnp.cumsum(lf, axis=-1)
        logd = jnp.where(causal, b[..., :, None] - b[..., None, :] + li[..., None, :], -jnp.inf)
        inter = b + m[..., None]
        m_t = jnp.maximum(inter, jnp.max(logd, axis=-1))
        w_int = jnp.exp(inter - m_t)
        s = jnp.einsum('bhtd,bhsd->bhts', qc, kc) * jnp.exp(logd - m_t[..., None])
        num = w_int[..., None] * jnp.einsum('bhtd,bhde->bhte', qc, C) + jnp.einsum('bhts,bhse->bhte', s, vc)
        den = w_int * jnp.einsum('bhtd,bhd->bht', qc, n) + jnp.sum(s, axis=-1)
        h = num / jnp.maximum(jnp.abs(den), jnp.exp(-m_t))[..., None]
        b_last = b[..., -1]
        g = b_last[..., None] - b + li
        m_new = jnp.maximum(b_last + m, jnp.max(g, axis=-1))
        decay = jnp.exp(b_last + m - m_new)
        wk = kc * jnp.exp(g - m_new[..., None])[..., None]
        C_new = decay[..., None, None] * C + jnp.einsum('bhsd,bhse->bhde', wk, vc)
        n_new = decay[..., None] * n + jnp.sum(wk, axis=2)
        return (C_new, n_new, m_new), h

    carry0 = (C0.astype(jnp.float32), n0.astype(jnp.float32), m0.astype(jnp.float32))
    (C, n, m), h = lax.scan(step, carry0, (chunks(q), chunks(k), chunks(v), gchunks(log_i), gchunks(log_f)))
    h = h.transpose(1, 0, 3, 2, 4).reshape(B, L, H, Dh)
    return h, (C, n, m)


def mlstm_bidir(q, k, v, gates, C0, n0, m0):
    i_f, f_f, i_b, f_b = jnp.split(gates, 4, axis=-1)
    h_f, (Cf, nf, mf) = mlstm_scan(q, k, v, i_f, jax.nn.log_sigmoid(f_f), C0[:, 0], n0[:, 0], m0[:, 0])
    flip = lambda a: jnp.flip(a, axis=1)
    h_b, (Cb, nb, mb) = mlstm_scan(flip(q), flip(k), flip(v), flip(i_b), flip(jax.nn.log_sigmoid(f_b)),
                                   C0[:, 1], n0[:, 1], m0[:, 1])
    h = h_f + flip(h_b)
    return h, (jnp.stack([Cf, Cb], axis=1), jnp.stack([nf, nb], axis=1), jnp.stack([mf, mb], axis=1))


def trunk_layer(x, cond, C0, n0, m0, rows, w_ada, b_ada, norm_mix_pre, norm_mix_post, norm_ffn_pre,
                norm_ffn_post, w_in, b_in, conv_a_w, conv_a_b, ln_a_w, ln_a_b, w_a_out, w_pool, pool_scale,
                mlstm_norm_w, w_c_out, w_out, w_ffn_up, ffn_conv_w, ffn_conv_b, w_ffn_down):
    B, L, _ = x.shape
    mod = (jax.nn.silu(cond) @ w_ada + b_ada)[:, None, :]
    sh1, sc1, g1, sh2, sc2, g2 = jnp.split(mod, N_MOD, axis=-1)

    h = rmsnorm(x, norm_mix_pre) * (1 + sc1) + sh1
    z = h @ w_in + b_in

    glu = z[..., OFF_GLU:OFF_GLU + WA] * jax.nn.sigmoid(z[..., OFF_GLU + WA:OFF_POOL])
    a = along_rows(lambda t: dwconv(t, conv_a_w, conv_a_b), glu, rows)
    y_a = jax.nn.silu(layernorm(a, ln_a_w, ln_a_b)) @ w_a_out

    p = along_cols(pool_groups, z[..., OFF_POOL:OFF_QKV], rows)
    y_b = jnp.einsum('blgc,gcd->blgd', p.reshape(B, L, N_POOL_GROUPS, POOL_GC), w_pool).reshape(B, L, D_MODEL) * pool_scale

    q, k, v = [z[..., OFF_QKV + j * WC:OFF_QKV + (j + 1) * WC].astype(jnp.float32).reshape(B, L, N_HEADS, HEAD_DIM)
               for j in range(3)]
    q = q * (HEAD_DIM ** -0.5)
    hc, new_state = mlstm_bidir(q, k, v, z[..., OFF_GATES:OFF_MERGE].astype(jnp.float32), C0, n0, m0)
    hc = hc * lax.rsqrt(jnp.mean(hc * hc, axis=-1, keepdims=True) + EPS)
    hc = hc.reshape(B, L, WC).astype(x.dtype) * mlstm_norm_w * jax.nn.sigmoid(z[..., OFF_OG:OFF_GATES])
    y_c = hc @ w_c_out

    ga, gb, gc = jnp.split(jax.nn.sigmoid(z[..., OFF_MERGE:]), N_BRANCH, axis=-1)
    mix = (ga * y_a + gb * y_b + gc * y_c) @ w_out
    x = x + g1 * rmsnorm(mix, norm_mix_post)

    h2 = rmsnorm(x, norm_ffn_pre) * (1 + sc2) + sh2
    u, g = jnp.split(h2 @ w_ffn_up, 2, axis=-1)
    g = along_cols(lambda t: dwconv(t, ffn_conv_w, ffn_conv_b), g, rows)
    f = (jax.nn.gelu(g) * u) @ w_ffn_down
    x = x + g2 * rmsnorm(f, norm_ffn_post)
    return x, new_state


def setup_inputs(seed: int = 0) -> dict:
    key = jax.random.key(seed)
    keys = jax.random.split(key, 32)
    counter = [0]

    def nrm(shape, scale):
        kk = keys[counter[0]]
        counter[0] += 1
        return scale * jax.random.normal(kk, shape, jnp.float32)

    D = D_MODEL
    x_prompt = nrm((BATCH, SEQ, D), 1.0)
    x_sample = nrm((DEC_BATCH, DEC_SEQ, D), 1.0)
    state_C = nrm((DEC_BATCH, DEPTH, 2, N_HEADS, HEAD_DIM, HEAD_DIM), 0.05)
    state_n = nrm((DEC_BATCH, DEPTH, 2, N_HEADS, HEAD_DIM), 0.5)
    state_m = nrm((DEC_BATCH, DEPTH, 2, N_HEADS), 0.5)
    c = nrm((DEC_BATCH, D), 1.0)
    c_ctx = nrm((D,), 1.0)
    w_ada = nrm((DEPTH, D, N_MOD * D), 0.5 * D ** -0.5)
    b_ada = nrm((DEPTH, N_MOD * D), 0.01)
    norm_mix_pre = 1.0 + nrm((DEPTH, D), 0.05)
    norm_mix_post = 1.0 + nrm((DEPTH, D), 0.05)
    norm_ffn_pre = 1.0 + nrm((DEPTH, D), 0.05)
    norm_ffn_post = 1.0 + nrm((DEPTH, D), 0.05)
    w_in = nrm((DEPTH, D, N_IN), D ** -0.5)
    b_in = nrm((DEPTH, N_IN), 0.01)
    forget_cols = jnp.concatenate([jnp.arange(OFF_GATES + N_HEADS, OFF_GATES + 2 * N_HEADS),
                                   jnp.arange(OFF_GATES + 3 * N_HEADS, OFF_GATES + 4 * N_HEADS)])
    b_in = b_in.at[:, forget_cols].add(FORGET_BIAS)
    conv_a_w = nrm((DEPTH, CONV_A_WIDTH, WA), CONV_A_WIDTH ** -0.5)
    conv_a_b = nrm((DEPTH, WA), 0.01)
    ln_a_w = 1.0 + nrm((DEPTH, WA), 0.05)
    ln_a_b = nrm((DEPTH, WA), 0.01)
    w_a_out = nrm((DEPTH, WA, D), WA ** -0.5)
    w_pool = nrm((DEPTH, N_POOL_GROUPS, POOL_GC, D // N_POOL_GROUPS), POOL_GC ** -0.5)
    pool_scale = 1.0 + nrm((DEPTH, D), 0.05)
    mlstm_norm_w = 1.0 + nrm((DEPTH, WC), 0.05)
    w_c_out = nrm((DEPTH, WC, D), WC ** -0.5)
    w_out = nrm((DEPTH, D, D), D ** -0.5)
    w_ffn_up = nrm((DEPTH, D, 2 * D_FF), D ** -0.5)
    ffn_conv_w = nrm((DEPTH, FFN_CONV_WIDTH, D_FF), FFN_CONV_WIDTH ** -0.5)
    ffn_conv_b = nrm((DEPTH, D_FF), 0.01)
    w_ffn_down = nrm((DEPTH, D_FF, D), D_FF ** -0.5)
    return {'x_prompt': x_prompt, 'x_sample': x_sample, 'state_C': state_C, 'state_n': state_n,
            'state_m': state_m, 'c': c, 'c_ctx': c_ctx, 'w_ada': w_ada, 'b_ada': b_ada,
            'norm_mix_pre': norm_mix_pre, 'norm_mix_post': norm_mix_post, 'norm_ffn_pre': norm_ffn_pre,
            'norm_ffn_post': norm_ffn_post, 'w_in': w_in, 'b_in': b_in, 'conv_a_w': conv_a_w,
            'conv_a_b': conv_a_b, 'ln_a_w': ln_a_w, 'ln_a_b': ln_a_b, 'w_a_out': w_a_out, 'w_pool': w_pool,
            'pool_scale': pool_scale, 'mlstm_norm_w': mlstm_norm_w, 'w_c_out': w_c_out, 'w_out': w_out,
            'w_ffn_up': w_ffn_up, 'ffn_conv_w': ffn_conv_w, 'ffn_conv_b': ffn_conv_b, 'w_ffn_down': w_ffn_down}


def reference(x_prompt, x_sample, state_C, state_n, state_m, c, c_ctx, w_ada, b_ada, norm_mix_pre,
              norm_mix_post, norm_ffn_pre, norm_ffn_post, w_in, b_in, conv_a_w, conv_a_b, ln_a_w, ln_a_b,
              w_a_out, w_pool, pool_scale, mlstm_norm_w, w_c_out, w_out, w_ffn_up, ffn_conv_w, ffn_conv_b,
              w_ffn_down):
    rows = x_sample.shape[1] // GRID_W
    bp = x_prompt.shape[0]
    C_zero = jnp.zeros((bp, 2, N_HEADS, HEAD_DIM, HEAD_DIM), jnp.float32)
    n_zero = jnp.zeros((bp, 2, N_HEADS, HEAD_DIM), jnp.float32)
    m_zero = jnp.zeros((bp, 2, N_HEADS), jnp.float32)
    cond_ctx = c_ctx[None, :]
    yp = x_prompt
    ys = x_sample
    Cs, ns, ms = [], [], []
    for l in range(DEPTH):
        lw = (w_ada[l], b_ada[l], norm_mix_pre[l], norm_mix_post[l], norm_ffn_pre[l], norm_ffn_post[l],
              w_in[l], b_in[l], conv_a_w[l], conv_a_b[l], ln_a_w[l], ln_a_b[l], w_a_out[l], w_pool[l],
              pool_scale[l], mlstm_norm_w[l], w_c_out[l], w_out[l], w_ffn_up[l], ffn_conv_w[l],
              ffn_conv_b[l], w_ffn_down[l])
        yp, (C1, n1, m1) = trunk_layer(yp, cond_ctx, C_zero, n_zero, m_zero, None, *lw)
        Cs.append(C1)
        ns.append(n1)
        ms.append(m1)
        ys, _ = trunk_layer(ys, c, state_C[:, l], state_n[:, l], state_m[:, l], rows, *lw)
    new_state_C = jnp.stack(Cs, axis=1)
    new_state_n = jnp.stack(ns, axis=1)
    new_state_m = jnp.stack(ms, axis=1)
    return (yp, ys, new_state_C, new_state_n, new_state_m)
```

```python
import numpy as np
from contextlib import ExitStack
import concourse.bass as bass
import concourse.mybir as mybir
from concourse.bass_utils import run_bass_kernel_spmd

F32 = mybir.dt.float32
BF16 = mybir.dt.bfloat16
AF = mybir.ActivationFunctionType
ALU = mybir.AluOpType
AX = mybir.AxisListType

D = 2048
T = 1024
NT = 8
WA = 512
WB = 512
WC = 1024
NH = 4
DH = 256
DFF = 5632
NIN = 11792
OFF_POOL = 1024
OFF_QKV = 1536
OFF_OG = 4608
OFF_GATES = 5632
OFF_MERGE = 5648
EPS = 1e-6
POOLW = (2, 4, 8, 16)
BIGN = 4112


class Res:
    __slots__ = ("lw", "rd", "dr")

    def __init__(self):
        self.lw = None
        self.rd = {}
        self.dr = []


class Op:
    __slots__ = ("eng", "fn", "deps", "dma", "sig", "sigval", "semk")

    def __init__(self, eng, fn, deps, dma):
        self.eng = eng
        self.fn = fn
        self.deps = deps
        self.dma = dma
        self.sig = False
        self.sigval = 0
        self.semk = None


class Sched:
    ENGS = ("pe", "act", "dve", "pool", "sp")
    NDS = 12

    def __init__(self):
        self.ops = []

    def add(self, eng, fn, reads=(), writes=(), dma=False):
        oid = len(self.ops)
        deps = set()
        for r in reads:
            if r.lw is not None:
                deps.add(r.lw)
        for w in writes:
            if w.lw is not None:
                deps.add(w.lw)
            deps.update(w.rd.values())
            deps.update(w.dr)
        self.ops.append(Op(eng, fn, deps, dma))
        for r in reads:
            if dma:
                r.dr.append(oid)
            else:
                r.rd[eng] = oid
        for w in writes:
            w.lw = oid
            w.rd = {}
            w.dr = []
        return oid

    def emit(self, nc, es):
        ops = self.ops
        for o in ops:
            if o.dma:
                o.sig = True
            for d in o.deps:
                od = ops[d]
                if od.dma:
                    continue
                if od.eng == o.eng and o.eng == "pe" and not o.dma:
                    continue
                od.sig = True
        cnt = {e: 0 for e in self.ENGS}
        dcnt = {}
        dn = {e: 0 for e in self.ENGS}
        for o in ops:
            if o.dma:
                k = (o.eng, dn[o.eng] % self.NDS)
                dn[o.eng] += 1
                dcnt[k] = dcnt.get(k, 0) + 1
                o.semk = k
                o.sigval = 16 * dcnt[k]
            elif o.sig:
                cnt[o.eng] += 1
                o.sigval = cnt[o.eng]
        sems = {}
        for e in self.ENGS:
            sems[e] = es.enter_context(nc.semaphore("s_" + e))
        for k in dcnt:
            sems[k] = es.enter_context(nc.semaphore("d_%s_%d" % k))
        block = es.enter_context(nc.Block())
        handles = {"pe": block.tensor, "act": block.scalar, "dve": block.vector, "pool": block.gpsimd,
                   "sp": block.sync}

        def run_engine(ename):
            def body(h):
                seen = {}
                for o in ops:
                    if o.eng != ename:
                        continue
                    need = {}
                    for d in o.deps:
                        od = ops[d]
                        if od.dma:
                            key = od.semk
                        else:
                            if od.eng == ename and ename == "pe" and not o.dma:
                                continue
                            key = od.eng
                        if need.get(key, 0) < od.sigval:
                            need[key] = od.sigval
                    if o.dma and o.sigval > 16:
                        if need.get(o.semk, 0) < o.sigval - 16:
                            need[o.semk] = o.sigval - 16
                    for key, v in need.items():
                        if seen.get(key, 0) < v:
                            h.wait_ge(sems[key], v)
                            seen[key] = v
                    ins = o.fn(h)
                    if o.dma:
                        ins.then_inc(sems[o.semk], 16)
                    elif o.sig:
                        ins.then_inc(sems[ename], 1)
                for k, c in dcnt.items():
                    if k[0] == ename:
                        h.wait_ge(sems[k], 16 * c)
            return body

        for e in self.ENGS:
            handles[e](run_engine(e))


def build(depth=4):
    nc = bass.Bass("TRN2", target_bir_lowering=False)
    S = Sched()
    es = ExitStack()

    def din(name, shape):
        return nc.dram_tensor(name, list(shape), F32, kind="ExternalInput").ap()

    def dout(name, shape):
        return nc.dram_tensor(name, list(shape), F32, kind="ExternalOutput").ap()

    xs = din("xs", [2 * T, D])
    cond = din("cond", [2, D])
    C0 = din("C0", [depth, 2, NH, DH, DH])
    n0 = din("n0", [depth, 2, NH, DH])
    m0 = din("m0", [depth, 8])
    w_ada = din("w_ada", [depth, D, 6 * D])
    b_ada = din("b_ada", [depth, 6 * D])
    nrm = {k: din(k, [depth, D]) for k in ("norm_mix_pre", "norm_mix_post", "norm_ffn_pre", "norm_ffn_post")}
    w_in = din("w_in", [depth, D, NIN])
    b_in = din("b_in", [depth, NIN])
    conv_a_w = din("conv_a_w", [depth, 31, WA])
    conv_a_b = din("conv_a_b", [depth, WA])
    ln_a_w = din("ln_a_w", [depth, WA])
    ln_a_b = din("ln_a_b", [depth, WA])
    w_a_out = din("w_a_out", [depth, WA, D])
    w_pool = din("w_pool", [depth, 4, 128, 512])
    pool_scale = din("pool_scale", [depth, D])
    mlstm_norm_w = din("mlstm_norm_w", [depth, WC])
    w_c_out = din("w_c_out", [depth, WC, D])
    w_out = din("w_out", [depth, D, D])
    w_ffn_up = din("w_ffn_up", [depth, D, 2 * DFF])
    ffn_conv_w = din("ffn_conv_w", [depth, 3, DFF])
    ffn_conv_b = din("ffn_conv_b", [depth, DFF])
    w_ffn_down = din("w_ffn_down", [depth, DFF, D])
    c_ident = din("c_ident", [128, 128])
    c_triU = din("c_triU", [64, 64])
    c_triL = din("c_triL", [64, 64])
    c_invcnt = din("c_invcnt", [2, 4, T])

    y = dout("y", [2 * T, D])
    stC = dout("stC", [4, depth, 2, NH, DH, DH])
    stn = dout("stn", [4, depth, 2, NH, DH])
    stm = dout("stm", [4, depth, 8])
    mod_d = dout("mod_d", [2, 6 * D])
    f_d = dout("f_d", [T, D])
    R_y = [[Res() for _ in range(NT)] for _ in range(2)]
    R_f = [Res() for _ in range(NT)]
    R_mod = Res()
    R_st = Res()

    def sb(name, shape, dt=F32):
        return es.enter_context(nc.sbuf_tensor(name, list(shape), dt))

    BIG = [sb("big%d" % i, [128, BIGN]) for i in range(8)]
    RB = [Res() for _ in range(8)]
    WBUF = [sb("wb%d" % i, [128, 8192], BF16) for i in range(2)]
    RW = [Res() for _ in range(2)]
    SC = [sb("sc%d" % i, [128, T]) for i in range(4)]
    RS = [Res() for _ in range(4)]
    junk = sb("junk", [128, 2048], BF16)
    R_junk = Res()
    PS = [es.enter_context(nc.psum_tensor("ps%d" % i, [128, 512], F32)) for i in range(8)]
    RP = [Res() for _ in range(8)]
    psn = [0]

    def psum():
        i = psn[0] % 8
        psn[0] += 1
        return PS[i], RP[i]

    wbn = [0]

    def bf(t, n=None):
        a = t[:].bitcast(BF16)
        return a

    def mm(out, lhsT, rhs, start, stop, R, W):
        S.add("pe", lambda e: e.matmul(out, lhsT, rhs, start=start, stop=stop), R, W)

    def tr(out, in_, ident, R, W):
        S.add("pe", lambda e: e.transpose(out, in_, ident), R, W)

    def act(out, in_, func, R, W, bias=None, scale=None, accum=None):
        kw = {}
        if bias is not None:
            kw["bias"] = bias
        if scale is not None:
            kw["scale"] = scale
        if accum is not None:
            kw["accum_out"] = accum
        S.add("act", lambda e: e.activation(out, in_, func, **kw), R, W)

    def tt(out, a, b, op, R, W, eng="dve"):
        S.add(eng, lambda e: e.tensor_tensor(out, a, b, op), R, W)

    def ts(out, a, s1, s2, op0, op1, R, W, eng="dve"):
        if op1 is None:
            S.add(eng, lambda e: e.tensor_scalar(out, a, s1, None, op0), R, W)
        else:
            S.add(eng, lambda e: e.tensor_scalar(out, a, s1, s2, op0, op1), R, W)

    def stt(out, a, s, b, op0, op1, R, W):
        S.add("dve", lambda e: e.scalar_tensor_tensor(out, a, s, b, op0, op1), R, W)

    def cpy(out, in_, R, W, eng="dve"):
        S.add(eng, lambda e: e.tensor_copy(out, in_), R, W)

    def mset(ap, v, W, eng="dve"):
        S.add(eng, lambda e: e.memset(ap, v), (), W)

    def dma(q, out, in_, R, W, nonc=False):
        if nonc:
            S.add(q, lambda e: e.dma_start(out=out, in_=in_, allow_slow_non_contiguous=True), R, W, dma=True)
        else:
            S.add(q, lambda e: e.dma_start(out=out, in_=in_), R, W, dma=True)

    gm = sb("gm", [128, 14, 128]); rgm = Res()
    G = sb("G", [64, 256]); rG = Res()
    acol = sb("acol", [128, 1])
    mcur_all = sb("mcur", [128, 2, 4, 4])
    rsm_t = sb("rsm_t", [64, 8])
    hn_s = sb("hn_s", [64, 16, 4]); R_hn = Res()
    wpool_sb = sb("wpool", [128, 4, 512], BF16); r_wp = Res()
    small = sb("small", [128, 8, 4]); rsm = Res()
    R_kp, R_va, R_sp, R_cb, R_rs = Res(), Res(), Res(), Res(), Res()
    ident = sb("ident", [128, 128])
    identb = sb("identb", [128, 128], BF16)
    ones = sb("ones", [128, 128])
    triU = sb("triU", [64, 64])
    triL = sb("triL", [64, 64])
    R_c = Res()
    dma("sp", ident[:], c_ident, (), [R_c])
    dma("sp", triU[:], c_triU, (), [R_c])
    dma("sp", triL[:], c_triL, (), [R_c])
    cpy(identb[:], ident[:], [R_c], [R_c])
    mset(ones[:], 1.0, [R_c])

    scT = sb("scT", [128, 16, 64], BF16)
    R_scT = Res()
    cnd = SC[0]
    mset(BIG[0][0:64, 0:D], 0.0, [RB[0]])
    dma("sp", BIG[0][0:1, 0:D], cond[0:1, :], (), [RB[0]])
    dma("sp", BIG[0][32:33, 0:D], cond[1:2, :], (), [RB[0]])
    act(BIG[0][0:64, 0:D], BIG[0][0:64, 0:D], AF.Silu, [RB[0]], [RB[0]])
    for kt in range(16):
        p, rp = psum()
        tr(p[:, 0:64], BIG[0][0:64, kt * 128:(kt + 1) * 128], ident[0:64, 0:64], [RB[0], R_c], [rp])
        cpy(scT[:, kt, :], p[:, 0:64], [rp], [R_scT])

    prm = sb("prm", [128, 640])
    R_prm = Res()
    P_BIN, P_CAW, P_CAB, P_LNW, P_LNB, P_PSC, P_MNW, P_FCW, P_FCB, P_BQ = 0, 92, 216, 220, 224, 228, 244, 252, 384, 428
    gbias = sb("gbias", [16, 1])

    def load_rows_T(dst_ap_fn, src2d, nrows, ncolt, stage, rstage):
        dma("sp", stage[0:nrows, 0:ncolt * 128], src2d, (), [rstage])
        for c in range(ncolt):
            p, rp = psum()
            tr(p[:, 0:nrows], stage[0:nrows, c * 128:(c + 1) * 128], ident[0:nrows, 0:nrows], [rstage, R_c], [rp])
            cpy(dst_ap_fn(c), p[:, 0:nrows], [rp], [R_prm])

    def load_params(l):
        st, rs = BIG[7], RB[7]
        load_rows_T(lambda c: prm[:, P_BIN:P_BIN + 44], b_in[l, 0:5632].rearrange("(r c) -> r c", c=128), 44, 1, st, rs)
        load_rows_T(lambda c: prm[:, P_BIN + 44:P_BIN + 92], b_in[l, OFF_MERGE:NIN].rearrange("(r c) -> r c", c=128), 48, 1, st, rs)
        dma("sp", gbias[:], b_in[l, OFF_GATES:OFF_MERGE].rearrange("(p o) -> p o", o=1), (), [R_prm])
        pv = prm[:, P_CAW:P_CAW + 124].rearrange("p (f k) -> p f k", k=31)
        load_rows_T(lambda c: pv[:, c, :], conv_a_w[l], 31, 4, st, rs)
        load_rows_T(lambda c: prm[:, P_CAB:P_CAB + 4], conv_a_b[l].rearrange("(r c) -> r c", c=128), 4, 1, st, rs)
        load_rows_T(lambda c: prm[:, P_LNW:P_LNW + 4], ln_a_w[l].rearrange("(r c) -> r c", c=128), 4, 1, st, rs)
        load_rows_T(lambda c: prm[:, P_LNB:P_LNB + 4], ln_a_b[l].rearrange("(r c) -> r c", c=128), 4, 1, st, rs)
        load_rows_T(lambda c: prm[:, P_PSC:P_PSC + 16], pool_scale[l].rearrange("(r c) -> r c", c=128), 16, 1, st, rs)
        load_rows_T(lambda c: prm[:, P_MNW:P_MNW + 8], mlstm_norm_w[l].rearrange("(r c) -> r c", c=128), 8, 1, st, rs)
        fv = prm[:, P_FCW:P_FCW + 132].rearrange("p (f k) -> p f k", k=3)
        load_rows_T(lambda c: fv[:, c, :], ffn_conv_w[l][:, 0:2816], 3, 22, st, rs)
        load_rows_T(lambda c: fv[:, 22 + c, :], ffn_conv_w[l][:, 2816:5632], 3, 22, st, rs)
        load_rows_T(lambda c: prm[:, P_FCB:P_FCB + 44], ffn_conv_b[l].rearrange("(r c) -> r c", c=128), 44, 1, st, rs)
        ts(prm[:, P_BQ:P_BQ + 8], prm[:, P_BIN + 12:P_BIN + 20], 1.0 / 16.0, None, ALU.mult, None, [R_prm], [R_prm])

    def load_panel(src, KT, ncols):
        i = wbn[0] % 2
        wbn[0] += 1
        v = WBUF[i][:, 0:KT * ncols].rearrange("p (k n) -> p k n", n=ncols)
        dma("pool", v, src.rearrange("(k p) n -> p k n", p=128), (), [RW[i]])
        return v, RW[i]

    def linear(src, KT, ncols, rhs_fn, evac_fn, ft0=0):
        v, rw = load_panel(src, KT, ncols)
        for ft in range(ncols // 128):
            for tb in range(2):
                p, rp = psum()
                for kt in range(KT):
                    ra, rr = rhs_fn(kt, tb)
                    mm(p[:, :], v[:, kt, ft * 128:(ft + 1) * 128], ra, kt == 0, kt == KT - 1, [rw] + rr, [rp])
                evac_fn(ft0 + ft, tb, p, rp)

    def mod_phase(l):
        stg, rsg = SC[0], RS[0]
        bst, rbs = SC[1], RS[1]
        for pn in range(24):
            v, rw = load_panel(w_ada[l, :, pn * 512:(pn + 1) * 512], 16, 512)
            p, rp = psum()
            for kt in range(16):
                mm(p[0:64, :], scT[:, kt, :], v[:, kt, :], kt == 0, kt == 15, [rw, R_scT], [rp])
            dma("sp", bst[0:1, 0:512], b_ada[l, pn * 512:(pn + 1) * 512].rearrange("(o n) -> o n", o=1), (), [rbs])
            dma("sp", bst[32:33, 0:512], b_ada[l, pn * 512:(pn + 1) * 512].rearrange("(o n) -> o n", o=1), (), [rbs])
            tt(stg[0:1, 0:512], p[0:1, :], bst[0:1, 0:512], ALU.add, [rp, rbs], [rsg])
            tt(stg[32:33, 0:512], p[32:33, :], bst[32:33, 0:512], ALU.add, [rp, rbs], [rsg])
            dma("sp", mod_d[0:1, pn * 512:(pn + 1) * 512], stg[0:1, 0:512], [rsg], [R_mod])
            dma("sp", mod_d[1:2, pn * 512:(pn + 1) * 512], stg[32:33, 0:512], [rsg], [R_mod])

    def bcast_load(dst, rdst, vec):
        dma("sp", dst, vec.partition_broadcast(128), [R_mod], [rdst])

    def norm_pass(l, g, which, xsrc, hviews, hres, bcb, rbcb, xb, rxb):
        gam = BIG[bcb][:, 0:D]
        shf = BIG[bcb][:, D:2 * D]
        rb = RB[bcb]
        si, ci = (0, 1) if which == 0 else (3, 4)
        nw = nrm["norm_mix_pre" if which == 0 else "norm_ffn_pre"]
        bcast_load(gam, rb, mod_d[g, ci * D:(ci + 1) * D])
        bcast_load(shf, rb, nw[l])
        stt(gam, gam, 1.0, shf, ALU.add, ALU.mult, [rb], [rb])
        bcast_load(shf, rb, mod_d[g, si * D:(si + 1) * D])
        xt = [BIG[xb][:, 0:D], BIG[xb][:, D:2 * D]]
        for i in range(NT):
            x_t = xt[i % 2]
            dma("sp", x_t, xsrc[i][0], xsrc[i][1], [RB[xb]])
            act(junk[:, :], x_t, AF.Square, [RB[xb]], [R_junk, rsm], accum=small[:, i, 0:1])
            ts(small[:, i, 1:2], small[:, i, 0:1], 1.0 / D, EPS, ALU.mult, ALU.add, [rsm], [rsm])
            act(small[:, i, 2:3], small[:, i, 1:2], AF.Sqrt, [rsm], [rsm])
            S.add("dve", (lambda o, a: (lambda e: e.reciprocal(o, a)))(small[:, i, 3:4], small[:, i, 2:3]), [rsm], [rsm])
            stt(x_t, x_t, small[:, i, 3:4], gam, ALU.mult, ALU.mult, [RB[xb], rsm, rb], [RB[xb]])
            tt(x_t, x_t, shf, ALU.add, [RB[xb], rb], [RB[xb]])
            for q4 in range(4):
                p, rp = psum()
                for j in range(4):
                    kt = q4 * 4 + j
                    tr(p[:, j * 128:(j + 1) * 128], x_t[:, kt * 128:(kt + 1) * 128], ident[:, :], [RB[xb], R_c], [rp])
                for j in range(4):
                    kt = q4 * 4 + j
                    if j % 2 == 0:
                        cpy(hviews[kt][:, i * 128:(i + 1) * 128], p[:, j * 128:(j + 1) * 128], [rp], [hres[kt]])
                    else:
                        act(hviews[kt][:, i * 128:(i + 1) * 128], p[:, j * 128:(j + 1) * 128], AF.Copy, [rp], [hres[kt]])

    def final_pass(l, g, which, xsrc, last):
        bcb = 7
        rb = RB[7]
        pg = BIG[7][:, 0:D]
        tmpb = BIG[7][:, D:2 * D]
        gi = 2 if which == 0 else 5
        nw = nrm["norm_mix_post" if which == 0 else "norm_ffn_post"]
        bcast_load(pg, rb, mod_d[g, gi * D:(gi + 1) * D])
        bcast_load(tmpb, rb, nw[l])
        tt(pg, pg, tmpb, ALU.mult, [rb], [rb])
        xt = [BIG[6][:, 0:D], BIG[6][:, D:2 * D]]
        ft_ = [BIG[5][:, 0:D], BIG[5][:, D:2 * D]]
        for i in range(NT):
            x_t = xt[i % 2]
            f_t = ft_[i % 2]
            dma("sp", x_t, xsrc[i][0], xsrc[i][1], [RB[6]])
            dma("sp", f_t, f_d[i * 128:(i + 1) * 128, :], [R_f[i]], [RB[5]])
            act(junk[:, :], f_t, AF.Square, [RB[5]], [R_junk, rsm], accum=small[:, i, 0:1])
            ts(small[:, i, 1:2], small[:, i, 0:1], 1.0 / D, EPS, ALU.mult, ALU.add, [rsm], [rsm])
            act(small[:, i, 2:3], small[:, i, 1:2], AF.Sqrt, [rsm], [rsm])
            S.add("dve", (lambda o, a: (lambda e: e.reciprocal(o, a)))(small[:, i, 3:4], small[:, i, 2:3]), [rsm], [rsm])
            stt(f_t, f_t, small[:, i, 3:4], pg, ALU.mult, ALU.mult, [RB[5], rsm, rb], [RB[5]])
            tt(f_t, f_t, x_t, ALU.add, [RB[5], RB[6]], [RB[5]])
            dma("sp", y[g * T + i * 128:g * T + (i + 1) * 128, :], f_t, [RB[5]], [R_y[g][i]])

    def out_block_to_fd(stage, rstage, tstage, rtstage, d0):
        sv = stage
        tv = tstage
        for i in range(NT):
            p, rp = psum()
            for j in range(4):
                tr(p[:, j * 128:(j + 1) * 128], sv[:, j, i * 128:(i + 1) * 128], ident[:, :], [rstage, R_c], [rp])
            if i % 2 == 0:
                cpy(tv[:, i, :], p[:, :], [rp], [rtstage])
            else:
                act(tv[:, i, :], p[:, :], AF.Copy, [rp], [rtstage])
            dma("sp", f_d[i * 128:(i + 1) * 128, d0 * 128:d0 * 128 + 512], tv[:, i, :], [rtstage], [R_f[i]])

    def shifted(a, g, kind, d):
        if g == 0:
            a3 = a.rearrange("p (r c) -> p r c", c=64)
            if kind == "row":
                lo, hi = max(0, -d), 64 - max(0, d)
                return a3[:, :, lo:hi], a3[:, :, lo + d:hi + d]
            lo, hi = max(0, -d), 16 - max(0, d)
            return a3[:, lo:hi, :], a3[:, lo + d:hi + d, :]
        a3 = a.rearrange("p (r c) -> p r c", c=256)
        lo, hi = max(0, -d), 256 - max(0, d)
        return a3[:, :, lo:hi], a3[:, :, lo + d:hi + d]

    def mixer(l, g):
        xsrc = []
        for i in range(NT):
            if l == 0:
                xsrc.append((xs[g * T + i * 128:g * T + (i + 1) * 128, :], []))
            else:
                xsrc.append((y[g * T + i * 128:g * T + (i + 1) * 128, :], [R_y[g][i]]))
        wl = w_in[l]

        def hT_views(b0, b1):
            hv, hr = [], []
            for kt in range(16):
                b = b0 if kt < 8 else b1
                hv.append(bf(BIG[b])[:, (kt % 8) * T:(kt % 8 + 1) * T])
                hr.append(RB[b])
            return hv, hr

        hv, hr = hT_views(0, 1)
        norm_pass(l, g, 0, xsrc, hv, hr, 7, RB[7], 6, RB[6])

        def rhs_h(hv, hr):
            return lambda kt, tb: (hv[kt][:, tb * 512:(tb + 1) * 512], [hr[kt]])

        qkvT = [bf(BIG[2 + j])[:, 0:8 * T].rearrange("p (f t) -> p f t", t=T) for j in range(3)]
        for j in range(3):
            for pn in range(2):
                c0 = OFF_QKV + j * WC + pn * 512

                def ev(ft, tb, p, rp, j=j):
                    bt = OFF_QKV // 128 + j * 8 + ft
                    dst = qkvT[j][:, ft, tb * 512:(tb + 1) * 512]
                    if j == 0:
                        act(dst, p[:, :], AF.Identity, [rp, R_prm], [RB[2]], bias=prm[:, P_BQ + ft:P_BQ + ft + 1], scale=1.0 / 16.0)
                    elif (ft + tb) % 2 == 0:
                        act(dst, p[:, :], AF.Identity, [rp, R_prm], [RB[2 + j]], bias=prm[:, P_BIN + bt:P_BIN + bt + 1])
                    else:
                        ts(dst, p[:, :], prm[:, P_BIN + bt:P_BIN + bt + 1], None, ALU.add, None, [rp, R_prm], [RB[2 + j]])
                linear(wl[:, c0:c0 + 512], 16, 512, rhs_h(hv, hr), ev, ft0=pn * 4)
        gT = SC[3]
        i = wbn[0] % 2
        wbn[0] += 1
        gv = WBUF[i][:, 0:256].rearrange("p (k n) -> p k n", n=16)
        dma("pool", gv, wl[:, OFF_GATES:OFF_MERGE].rearrange("(k p) n -> p k n", p=128), (), [RW[i]], nonc=True)
        for tb in range(2):
            p, rp = psum()
            for kt in range(16):
                mm(p[0:16, :], gv[:, kt, :], hv[kt][:, tb * 512:(tb + 1) * 512], kt == 0, kt == 15, [RW[i], hr[kt]], [rp])
            act(gT[0:16, tb * 512:(tb + 1) * 512], p[0:16, :], AF.Identity, [rp, R_prm], [RS[3]], bias=gbias[:, 0:1])

        mlstm(l, g, qkvT, gT)

        hv, hr = hT_views(1, 2)
        norm_pass(l, g, 0, xsrc, hv, hr, 7, RB[7], 6, RB[6])
        hcT = bf(BIG[0])[:, 0:8 * T].rearrange("p (f t) -> p f t", t=T)

        for pn in range(2):
            c0 = OFF_OG + pn * 512

            def ev(ft, tb, p, rp):
                bt = OFF_OG // 128 + ft
                tmp = SC[(ft + tb) % 2][:, 0:512]
                rt = RS[(ft + tb) % 2]
                act(tmp, p[:, :], AF.Sigmoid, [rp, R_prm], [rt], bias=prm[:, P_BIN + bt:P_BIN + bt + 1])
                tt(hcT[:, ft, tb * 512:(tb + 1) * 512], hcT[:, ft, tb * 512:(tb + 1) * 512], tmp, ALU.mult, [rt, RB[0]], [RB[0]])
            linear(wl[:, c0:c0 + 512], 16, 512, rhs_h(hv, hr), ev, ft0=pn * 4)

        ga = BIG[3][:, 0:4 * T].rearrange("p (f t) -> p f t", t=T)
        gb = BIG[4][:, 0:4 * T].rearrange("p (f t) -> p f t", t=T)
        acc = BIG[5][:, 0:4 * T].rearrange("p (f t) -> p f t", t=T)
        sq = BIG[6][:, 0:4 * T].rearrange("p (f t) -> p f t", t=T)
        aT = bf(BIG[7])[:, 0:4 * T].rearrange("p (f t) -> p f t", t=T)
        pT = bf(BIG[7])[:, 4 * T:8 * T].rearrange("p (f t) -> p f t", t=T)
        for pn in range(2):
            def ev(ft, tb, p, rp):
                if ft < 4:
                    act(ga[:, ft, tb * 512:(tb + 1) * 512], p[:, :], AF.Identity, [rp, R_prm], [RB[3]], bias=prm[:, P_BIN + ft:P_BIN + ft + 1])
                else:
                    act(gb[:, ft - 4, tb * 512:(tb + 1) * 512], p[:, :], AF.Sigmoid, [rp, R_prm], [RB[4]], bias=prm[:, P_BIN + ft:P_BIN + ft + 1])
            linear(wl[:, pn * 512:(pn + 1) * 512], 16, 512, rhs_h(hv, hr), ev, ft0=pn * 4)
        caw = prm[:, P_CAW:P_CAW + 124].rearrange("p (f k) -> p f k", k=31)
        for ft in range(4):
            tt(ga[:, ft, :], ga[:, ft, :], gb[:, ft, :], ALU.mult, [RB[3], RB[4]], [RB[3]])
            ts(acc[:, ft, :], ga[:, ft, :], caw[:, ft, 15:16], prm[:, P_CAB + ft:P_CAB + ft + 1], ALU.mult, ALU.add, [RB[3], R_prm], [RB[5]])
            for k in range(31):
                d = k - 15
                if d == 0:
                    continue
                dv, _ = shifted(acc[:, ft, :], g, "row", d)
                _, sv = shifted(ga[:, ft, :], g, "row", d)
                stt(dv, sv, caw[:, ft, k:k + 1], dv, ALU.mult, ALU.add, [RB[3], RB[5], R_prm], [RB[5]])
            act(sq[:, ft, :], acc[:, ft, :], AF.Square, [RB[5]], [RB[6]])
        for tb in range(2):
            p1, r1 = psum()
            p2, r2 = psum()
            for ft in range(4):
                mm(p1[:, :], ones[:, :], acc[:, ft, tb * 512:(tb + 1) * 512], ft == 0, ft == 3, [R_c, RB[5]], [r1])
            for ft in range(4):
                mm(p2[:, :], ones[:, :], sq[:, ft, tb * 512:(tb + 1) * 512], ft == 0, ft == 3, [R_c, RB[6]], [r2])
            mean = SC[0][:, 0:512]
            var = SC[1][:, 0:512]
            act(mean, p1[:, :], AF.Copy, [r1], [RS[0]], scale=1.0 / WA)
            tt(var, mean, mean, ALU.mult, [RS[0]], [RS[1]])
            stt(var, p2[:, :], 1.0 / WA, var, ALU.mult, ALU.subtract, [r2, RS[1]], [RS[1]])
            ts(var, var, EPS, None, ALU.add, None, [RS[1]], [RS[1]])
            act(var, var, AF.Sqrt, [RS[1]], [RS[1]])
            S.add("dve", (lambda o: (lambda e: e.reciprocal(o, o)))(var), [RS[1]], [RS[1]])
            for ft in range(4):
                tmp = SC[2][:, 0:512]
                tt(tmp, acc[:, ft, tb * 512:(tb + 1) * 512], mean, ALU.subtract, [RB[5], RS[0]], [RS[2]])
                tt(tmp, tmp, var, ALU.mult, [RS[2], RS[1]], [RS[2]])
                act(aT[:, ft, tb * 512:(tb + 1) * 512], tmp, AF.Silu, [RS[2], R_prm], [RB[7]],
                    bias=prm[:, P_LNB + ft:P_LNB + ft + 1], scale=prm[:, P_LNW + ft:P_LNW + ft + 1])

        zp = BIG[3][:, 0:4 * T].rearrange("p (f t) -> p f t", t=T)
        pacc = BIG[4][:, 0:4 * T].rearrange("p (f t) -> p f t", t=T)

        def ev(ft, tb, p, rp):
            bt = OFF_POOL // 128 + ft
            act(zp[:, ft, tb * 512:(tb + 1) * 512], p[:, :], AF.Identity, [rp, R_prm], [RB[3]], bias=prm[:, P_BIN + bt:P_BIN + bt + 1])
        linear(wl[:, OFF_POOL:OFF_POOL + 512], 16, 512, rhs_h(hv, hr), ev)
        for ft in range(4):
            w = POOLW[ft]
            cpy(pacc[:, ft, :], zp[:, ft, :], [RB[3]], [RB[4]])
            for d in range(-(w // 2), w // 2):
                if d == 0:
                    continue
                dv, _ = shifted(pacc[:, ft, :], g, "col", d)
                _, sv = shifted(zp[:, ft, :], g, "col", d)
                tt(dv, dv, sv, ALU.add, [RB[3], RB[4]], [RB[4]])
            ic = SC[0]
            dma("sp", ic[:, :], c_invcnt[g, ft].partition_broadcast(128), (), [RS[0]])
            tt(pacc[:, ft, :], pacc[:, ft, :], ic[:, :], ALU.mult, [RB[4], RS[0]], [RB[4]])
            tt(pT[:, ft, :], pacc[:, ft, :], zp[:, ft, :], ALU.subtract, [RB[4], RB[3]], [RB[7]])

        def mixv(dt_):
            b = 3 if dt_ < 8 else 4
            return bf(BIG[b])[:, (dt_ % 8) * T:(dt_ % 8 + 1) * T], RB[b]
        dma("pool", wpool_sb[:, :, :], w_pool[l].rearrange("g c d -> c g d"), (), [r_wp])
        for db in range(4):
            wa_v, wa_r = load_panel(w_a_out[l][:, db * 512:(db + 1) * 512], 4, 512)
            wc_v, wc_r = load_panel(w_c_out[l][:, db * 512:(db + 1) * 512], 8, 512)
            ya = BIG[5][:, 0:4 * T].rearrange("p (f t) -> p f t", t=T)
            yc = BIG[6][:, 0:4 * T].rearrange("p (f t) -> p f t", t=T)
            for dj in range(4):
                for tb in range(2):
                    p, rp = psum()
                    for kt in range(4):
                        mm(p[:, :], wa_v[:, kt, dj * 128:(dj + 1) * 128], aT[:, kt, tb * 512:(tb + 1) * 512], kt == 0, kt == 3, [wa_r, RB[7]], [rp])
                    act(ya[:, dj, tb * 512:(tb + 1) * 512], p[:, :], AF.Copy, [rp], [RB[5]])
                    p, rp = psum()
                    for kt in range(8):
                        mm(p[:, :], wc_v[:, kt, dj * 128:(dj + 1) * 128], hcT[:, kt, tb * 512:(tb + 1) * 512], kt == 0, kt == 7, [wc_r, RB[0]], [rp])
                    cpy(yc[:, dj, tb * 512:(tb + 1) * 512], p[:, :], [rp], [RB[6]])
            for br in range(3):
                c0 = OFF_MERGE + br * D + db * 512

                def ev(ft, tb, p, rp, br=br, db=db):
                    dt_ = db * 4 + ft
                    bt = 44 + br * 16 + dt_
                    gt = SC[(ft + tb) % 2][:, 0:512]
                    rg = RS[(ft + tb) % 2]
                    act(gt, p[:, :], AF.Sigmoid, [rp, R_prm], [rg], bias=prm[:, P_BIN + bt:P_BIN + bt + 1])
                    mv, mr = mixv(dt_)
                    msl = mv[:, tb * 512:(tb + 1) * 512]
                    if br == 0:
                        tt(ya[:, ft, tb * 512:(tb + 1) * 512], ya[:, ft, tb * 512:(tb + 1) * 512], gt, ALU.mult, [rg, RB[5]], [RB[5]])
                    elif br == 1:
                        gidx = dt_ // 4
                        p2, rp2 = psum()
                        mm(p2[:, :], wpool_sb[:, gidx, (dt_ % 4) * 128:(dt_ % 4 + 1) * 128], pT[:, gidx, tb * 512:(tb + 1) * 512], True, True, [r_wp, RB[7]], [rp2])
                        tmp = SC[2][:, 0:512]
                        stt(tmp, p2[:, :], prm[:, P_PSC + dt_:P_PSC + dt_ + 1], gt, ALU.mult, ALU.mult, [rp2, rg, R_prm], [RS[2]])
                        tt(ya[:, ft, tb * 512:(tb + 1) * 512], ya[:, ft, tb * 512:(tb + 1) * 512], tmp, ALU.add, [RS[2], RB[5]], [RB[5]])
                    else:
                        tt(yc[:, ft, tb * 512:(tb + 1) * 512], yc[:, ft, tb * 512:(tb + 1) * 512], gt, ALU.mult, [rg, RB[6]], [RB[6]])
                        tt(msl, ya[:, ft, tb * 512:(tb + 1) * 512], yc[:, ft, tb * 512:(tb + 1) * 512], ALU.add, [RB[5], RB[6]], [mr])
                linear(wl[:, c0:c0 + 512], 16, 512, rhs_h(hv, hr), ev)

        for db in range(4):
            stage = BIG[5][:, 0:4 * T].rearrange("p (f t) -> p f t", t=T)
            tstage = BIG[6][:, 0:8 * 512].rearrange("p (i n) -> p i n", n=512)

            def ev(ft, tb, p, rp):
                if (ft + tb) % 2 == 0:
                    cpy(stage[:, ft, tb * 512:(tb + 1) * 512], p[:, :], [rp], [RB[5]])
                else:
                    act(stage[:, ft, tb * 512:(tb + 1) * 512], p[:, :], AF.Copy, [rp], [RB[5]])
            linear(w_out[l][:, db * 512:(db + 1) * 512], 16, 512, lambda kt, tb: (mixv(kt)[0][:, tb * 512:(tb + 1) * 512], [mixv(kt)[1]]), ev)
            out_block_to_fd(stage, RB[5], tstage, RB[6], db * 4)
        final_pass(l, g, 0, xsrc, False)

    def mlstm(l, g, qkvT, gT):
        qT, kT, vT = qkvT
        Rq, Rk, Rv = RB[2], RB[3], RB[4]
        nseq, nch = (1, 16) if g == 0 else (4, 4)
        LF, LI, BC_, UU, ABC, BL, MALL, MPREV, DEC, CSC, KFAC, ATOK, KSC, THR = [gm[:, i, :] for i in range(14)]
        p, rp = psum()
        for c in range(16):
            tr(p[0:64, c * 16:(c + 1) * 16], gT[0:16, c * 64:(c + 1) * 64], ident[0:16, 0:16], [RS[3], R_c], [rp])
        cpy(G[:, :], p[0:64, 0:256], [rp], [rG])
        G5 = G[:, :].rearrange("p (c d k h) -> p c d k h", d=2, k=2, h=4)
        for dr in range(2):
            lfv = LF[0:64, dr * 64:(dr + 1) * 64].rearrange("p (c h) -> p c h", h=4)
            liv = LI[0:64, dr * 64:(dr + 1) * 64].rearrange("p (c h) -> p c h", h=4)
            act(lfv, G5[:, :, dr, 1, :], AF.Exp, [rG], [rgm], scale=-1.0)
            cpy(liv, G5[:, :, dr, 0, :], [rG], [rgm])
        act(LF[0:64, :], LF[0:64, :], AF.Ln, [rgm], [rgm], bias=1.0)
        ts(LF[0:64, :], LF[0:64, :], -1.0, None, ALU.mult, None, [rgm], [rgm])
        p, rp = psum()
        mm(p[0:64, 0:64], triU[:, :], LF[0:64, 0:64], True, True, [R_c, rgm], [rp])
        mm(p[0:64, 64:128], triL[:, :], LF[0:64, 64:128], True, True, [R_c, rgm], [rp])
        cpy(BC_[0:64, :], p[0:64, 0:128], [rp], [rgm])
        tt(UU[0:64, :], LI[0:64, :], BC_[0:64, :], ALU.subtract, [rgm], [rgm])
        p, rp = psum()
        tr(p[:, 0:64], UU[0:64, :], ident[0:64, 0:64], [rgm, R_c], [rp])
        S.add("dve", (lambda o, a: (lambda e: e.tensor_reduce(o, a, AX.X, ALU.max)))(acol[:, :], p[:, 0:64]), [rp], [rgm])
        diag = SC[0][:, 0:128]
        ts(diag, ident[:, :], acol[:, 0:1], None, ALU.mult, None, [rgm, R_c], [RS[0]])
        p2, rp2 = psum()
        mm(p2[:, 0:128], ones[:, :], diag, True, True, [R_c, RS[0]], [rp2])
        cpy(ABC, p2[:, 0:128], [rp2], [rgm])
        p3, rp3 = psum()
        mm(p3[:, 0:128], ones[0:64, :], LF[0:64, :], True, True, [R_c, rgm], [rp3])
        cpy(BL, p3[:, 0:128], [rp3], [rgm])
        mcur = mcur_all[:, :, 0:nseq, :]
        if g == 0:
            dma("sp", mcur[:, :, 0, :], m0[l].rearrange("(d h) -> d h", h=4).partition_broadcast(128), (), [rgm])
        else:
            mset(mcur[:, :, :, :], 0.0, [rgm])

        def colv(arr, dr, cc):
            return arr[:, dr * 64:(dr + 1) * 64].rearrange("p (q c h) -> p q c h", q=nseq, h=4)[:, :, cc, :]
        for j in range(nch):
            for dr in range(2):
                cc = j if dr == 0 else nch - 1 - j
                cpy(colv(MPREV, dr, cc), mcur[:, dr, :, :], [rgm], [rgm])
                tt(colv(MALL, dr, cc), mcur[:, dr, :, :], colv(ABC, dr, cc), ALU.max, [rgm], [rgm])
                tt(mcur[:, dr, :, :], colv(MALL, dr, cc), colv(BL, dr, cc), ALU.add, [rgm], [rgm])
                if g == 1 and j == nch - 1:
                    pass
        if g == 1:
            for q in range(4):
                dma("sp", stm[q, l].rearrange("(o d h) -> o d h", o=1, h=4), mcur[0:1, :, q, :], [rgm], [R_st])
        tt(DEC, MPREV, MALL, ALU.subtract, [rgm], [rgm])
        act(DEC, DEC, AF.Exp, [rgm], [rgm])
        tt(CSC, MPREV, ABC, ALU.subtract, [rgm], [rgm])
        act(CSC, CSC, AF.Exp, [rgm], [rgm])
        tt(KFAC, ABC, MALL, ALU.subtract, [rgm], [rgm])
        act(KFAC, KFAC, AF.Exp, [rgm], [rgm])
        tt(ATOK[0:64, :], UU[0:64, :], ABC[0:64, :], ALU.subtract, [rgm], [rgm])
        act(ATOK[0:64, :], ATOK[0:64, :], AF.Exp, [rgm], [rgm])
        tt(KSC[0:64, :], ATOK[0:64, :], KFAC[0:64, :], ALU.mult, [rgm], [rgm])
        tt(THR[0:64, :], BC_[0:64, :], ABC[0:64, :], ALU.add, [rgm], [rgm])
        act(THR[0:64, :], THR[0:64, :], AF.Exp, [rgm], [rgm], scale=-1.0)

        Caug = BIG[7][:, 0:8 * 2 * 257].rearrange("p (s k e) -> p s k e", k=2, e=257)
        RC = RB[7]
        hsum = bf(BIG[5])[0:64, :]
        hsum2 = bf(BIG[6])[0:64, :]

        def hs(c, h):
            b = hsum if c < 8 else hsum2
            return b[:, (c % 8) * 1024 + h * 256:(c % 8) * 1024 + (h + 1) * 256]
        Rh = [RB[5], RB[6]]
        mset(bf(BIG[5])[0:64, 0:8192], 0.0, [RB[5]])
        mset(bf(BIG[6])[0:64, 0:8192], 0.0, [RB[6]])
        wk = bf(BIG[1])
        kp = wk[0:64, 0:1024].rearrange("p (h e) -> p h e", e=256)
        vaug = wk[0:64, 1024:1024 + 4 * 257].rearrange("p (h e) -> p h e", e=257)
        spT = wk[0:64, 2304:2304 + 256].rearrange("p (h t) -> p h t", t=64)
        cbf = wk[:, 2560:2560 + 4 * 514].rearrange("p (h k e) -> p h k e", k=2, e=257)
        mset(vaug[:, :, 256:257], 1.0, [R_va])

        for q in range(nseq):
            if g == 0:
                for dr in range(2):
                    for h in range(4):
                        dma("sp", Caug[:, dr * 4 + h, :, 0:256], C0[l, dr, h].rearrange("(k p) e -> p k e", p=128), (), [RC])
                        dma("sp", Caug[:, dr * 4 + h, :, 256:257], n0[l, dr, h].rearrange("(k p o) -> p k o", p=128, o=1), (), [RC], nonc=True)
            else:
                mset(Caug[:, :, :, :], 0.0, [RC])
            for j in range(nch):
                for dr in range(2):
                    cc = j if dr == 0 else nch - 1 - j
                    c = q * nch + cc
                    tsl = slice(c * 64, (c + 1) * 64)
                    col0 = dr * 64 + c * 4
                    pk, rpk = psum()
                    pv_, rpv = psum()
                    pkb = pk[:].bitcast(BF16)
                    pvb = pv_[:].bitcast(BF16)
                    for h in range(4):
                        for kt2 in range(2):
                            tr(pkb[0:64, h * 256 + kt2 * 128:h * 256 + (kt2 + 1) * 128], kT[:, h * 2 + kt2, tsl], identb[:, :], [Rk, R_c], [rpk])
                            tr(pvb[0:64, h * 256 + kt2 * 128:h * 256 + (kt2 + 1) * 128], vT[:, h * 2 + kt2, tsl], identb[:, :], [Rv, R_c], [rpv])
                    for h in range(4):
                        act(kp[:, h, :], pkb[0:64, h * 256:(h + 1) * 256], AF.Copy, [rpk, rgm], [R_kp], scale=KSC[0:64, col0 + h:col0 + h + 1])
                    cpy(vaug[:, :, 0:256], pvb[0:64, 0:1024].rearrange("p (h e) -> p h e", e=256), [rpv], [R_va])
                    pq, rpq = psum()
                    for h in range(4):
                        for kt2 in range(2):
                            mm(pq[0:64, h * 64:(h + 1) * 64], kT[:, h * 2 + kt2, tsl], qT[:, h * 2 + kt2, tsl], kt2 == 0, kt2 == 1, [Rk, Rq], [rpq])
                    msk = triU if dr == 0 else triL
                    for h in range(4):
                        stt(spT[:, h, :], pq[0:64, h * 64:(h + 1) * 64], ATOK[0:64, col0 + h:col0 + h + 1], msk[:, :], ALU.mult, ALU.mult, [rpq, rgm, R_c], [R_sp])
                    for h in range(4):
                        s_ = dr * 4 + h
                        act(cbf[:, h, :, :], Caug[:, s_, :, :], AF.Copy, [RC, rgm], [R_cb], scale=CSC[:, col0 + h:col0 + h + 1])
                    for h in range(4):
                        s_ = dr * 4 + h
                        pp, rpp = psum()
                        mm(pp[0:64, 0:257], spT[:, h, :], vaug[:, h, :], True, False, [R_sp, R_va], [rpp])
                        mm(pp[0:64, 0:257], qT[:, h * 2, tsl], cbf[:, h, 0, :], False, False, [Rq, R_cb], [rpp])
                        mm(pp[0:64, 0:257], qT[:, h * 2 + 1, tsl], cbf[:, h, 1, :], False, True, [Rq, R_cb], [rpp])
                        rr = rsm_t[:, h * 2:h * 2 + 1]
                        act(rr, pp[0:64, 256:257], AF.Abs, [rpp], [R_rs])
                        ts(rr, rr, THR[0:64, col0 + h:col0 + h + 1], None, ALU.max, None, [R_rs, rgm], [R_rs])
                        S.add("dve", (lambda o: (lambda e: e.reciprocal(o, o)))(rr), [R_rs], [R_rs])
                        hv_ = hs(c, h)
                        stt(hv_, pp[0:64, 0:256], rr, hv_, ALU.mult, ALU.add, [rpp, R_rs, Rh[c // 8]], [Rh[c // 8]])
                        for kt2 in range(2):
                            pc, rpc = psum()
                            mm(pc[:, 0:257], kp[:, h, kt2 * 128:(kt2 + 1) * 128], vaug[:, h, :], True, True, [R_kp, R_va], [rpc])
                            stt(Caug[:, s_, kt2, :], Caug[:, s_, kt2, :], DEC[:, col0 + h:col0 + h + 1], pc[:, 0:257], ALU.mult, ALU.add, [RC, rgm, rpc], [RC])
            if g == 1:
                for dr in range(2):
                    for h in range(4):
                        dma("sp", stC[q, l, dr, h].rearrange("(k p) e -> p k e", p=128), Caug[:, dr * 4 + h, :, 0:256], [RC], [R_st])
                        dma("sp", stn[q, l, dr, h].rearrange("(k p o) -> p k o", p=128, o=1), Caug[:, dr * 4 + h, :, 256:257], [RC], [R_st], nonc=True)

        hcT = bf(BIG[0])[:, 0:8 * T].rearrange("p (f t) -> p f t", t=T)
        mnw = prm[:, P_MNW:P_MNW + 8]
        for c in range(16):
            hb = hsum if c < 8 else hsum2
            hc_ = hb[:, (c % 8) * 1024:(c % 8 + 1) * 1024]
            sqc = SC[c % 2][0:64, :]
            rq = RS[c % 2]
            act(sqc, hc_, AF.Square, [Rh[c // 8]], [rq])
            S.add("dve", (lambda o, a: (lambda e: e.tensor_reduce(o, a, AX.X, ALU.add)))(hn_s[:, c, :], sqc.rearrange("p (h e) -> p h e", e=256)), [rq], [R_hn])
            ts(hn_s[:, c, :], hn_s[:, c, :], 1.0 / DH, EPS, ALU.mult, ALU.add, [R_hn], [R_hn])
            act(hn_s[:, c, :], hn_s[:, c, :], AF.Sqrt, [R_hn], [R_hn])
            S.add("dve", (lambda o: (lambda e: e.reciprocal(o, o)))(hn_s[:, c, :]), [R_hn], [R_hn])
            tt(sqc.rearrange("p (h e) -> p h e", e=256), hc_.rearrange("p (h e) -> p h e", e=256),
               hn_s[:, c, :].unsqueeze(2).to_broadcast([64, 4, 256]), ALU.mult, [Rh[c // 8], R_hn], [rq])
            p, rp = psum()
            for ft in range(8):
                tr(p[:, ft * 64:(ft + 1) * 64], sqc[:, ft * 128:(ft + 1) * 128], ident[0:64, 0:64], [rq, R_c], [rp])
            tt(hcT[:, :, c * 64:(c + 1) * 64], p[:, :].rearrange("p (f t) -> p f t", t=64),
               mnw.unsqueeze(2).to_broadcast([128, 8, 64]), ALU.mult, [rp, R_prm], [RB[0]])

    def ffn(l, g):
        xsrc = [(y[g * T + i * 128:g * T + (i + 1) * 128, :], [R_y[g][i]]) for i in range(NT)]
        hv, hr = [], []
        for kt in range(16):
            b = 0 if kt < 8 else 1
            hv.append(bf(BIG[b])[:, (kt % 8) * T:(kt % 8 + 1) * T])
            hr.append(RB[b])
        norm_pass(l, g, 1, xsrc, hv, hr, 7, RB[7], 6, RB[6])
        rhs = lambda kt, tb: (hv[kt][:, tb * 512:(tb + 1) * 512], [hr[kt]])

        def actv(j):
            b = 2 + j // 8
            return bf(BIG[b])[:, (j % 8) * T:(j % 8 + 1) * T], RB[b]
        fcw = prm[:, P_FCW:P_FCW + 132].rearrange("p (f k) -> p f k", k=3)
        u_sb, g_sb, gc, t1 = SC[0], SC[1], SC[2], SC[3]
        for qb in range(11):
            def ev_u(ft, tb, p, rp):
                if tb == 0:
                    cpy(u_sb[:, 0:512], p[:, :], [rp], [RS[0]])
                else:
                    act(u_sb[:, 512:1024], p[:, :], AF.Copy, [rp], [RS[0]])

            def ev_g(ft, tb, p, rp):
                if tb == 0:
                    act(g_sb[:, 0:512], p[:, :], AF.Copy, [rp], [RS[1]])
                else:
                    cpy(g_sb[:, 512:1024], p[:, :], [rp], [RS[1]])
            vu, ru = load_panel(w_ffn_up[l][:, qb * 512:(qb + 1) * 512], 16, 512)
            vg, rg_ = load_panel(w_ffn_up[l][:, DFF + qb * 512:DFF + (qb + 1) * 512], 16, 512)
            for ft in range(4):
                j = qb * 4 + ft
                for (vv, rv, evf) in ((vu, ru, ev_u), (vg, rg_, ev_g)):
                    for tb in range(2):
                        p, rp = psum()
                        for kt in range(16):
                            mm(p[:, :], vv[:, kt, ft * 128:(ft + 1) * 128], hv[kt][:, tb * 512:(tb + 1) * 512], kt == 0, kt == 15, [rv, hr[kt]], [rp])
                        evf(ft, tb, p, rp)
                ts(gc[:, :], g_sb[:, :], fcw[:, j, 1:2], prm[:, P_FCB + j:P_FCB + j + 1], ALU.mult, ALU.add, [RS[1], R_prm], [RS[2]])
                for k in (0, 2):
                    d = k - 1
                    dv, _ = shifted(gc[:, :], g, "col", d)
                    _, sv = shifted(g_sb[:, :], g, "col", d)
                    stt(dv, sv, fcw[:, j, k:k + 1], dv, ALU.mult, ALU.add, [RS[1], RS[2], R_prm], [RS[2]])
                act(t1[:, :], gc[:, :], AF.Square, [RS[2]], [RS[3]])
                ts(t1[:, :], t1[:, :], 0.044715, 1.0, ALU.mult, ALU.add, [RS[3]], [RS[3]])
                tt(t1[:, :], t1[:, :], gc[:, :], ALU.mult, [RS[3], RS[2]], [RS[3]])
                act(t1[:, :], t1[:, :], AF.Sigmoid, [RS[3]], [RS[3]], scale=1.5957691216057308)
                tt(gc[:, :], gc[:, :], u_sb[:, :], ALU.mult, [RS[2], RS[0]], [RS[2]])
                av, ar = actv(j)
                tt(av, gc[:, :], t1[:, :], ALU.mult, [RS[2], RS[3]], [ar])
        for db in range(4):
            stage = BIG[0][:, 0:4 * T].rearrange("p (f t) -> p f t", t=T)
            tstage = BIG[1][:, 0:8 * 512].rearrange("p (i n) -> p i n", n=512)
            for dj in range(4):
                dt_ = db * 4 + dj

                def ev(ft, tb, p, rp, dj=dj):
                    if tb == 0:
                        cpy(stage[:, dj, 0:512], p[:, :], [rp], [RB[0]])
                    else:
                        act(stage[:, dj, 512:1024], p[:, :], AF.Copy, [rp], [RB[0]])
                linear(w_ffn_down[l][:, dt_ * 128:(dt_ + 1) * 128], 44, 128, lambda kt, tb: (actv(kt)[0][:, tb * 512:(tb + 1) * 512], [actv(kt)[1]]), ev)
            out_block_to_fd(stage, RB[0], tstage, RB[1], db * 4)
        final_pass(l, g, 1, xsrc, l == depth - 1)

    for l in range(depth):
        load_params(l)
        mod_phase(l)
        for g in range(2):
            mixer(l, g)
            ffn(l, g)
    S.emit(nc, es)
    es.close()
    return nc


_CACHE = {}


def _consts():
    ident = np.eye(128, dtype=np.float32)
    s = np.arange(64)
    triU = (s[:, None] <= s[None, :]).astype(np.float32)
    triL = (s[:, None] >= s[None, :]).astype(np.float32)
    inv = np.zeros((2, 4, T), np.float32)
    for gi, L in enumerate((16, 256)):
        t = np.arange(L)
        for wi, w in enumerate(POOLW):
            lo = np.clip(t - w // 2, 0, L)
            hi = np.clip(t - w // 2 + w, 0, L)
            ic = (1.0 / (hi - lo)).astype(np.float32)
            if gi == 0:
                inv[gi, wi] = np.repeat(ic, 64)
            else:
                inv[gi, wi] = np.tile(ic, 4)
    return ident, triU, triL, inv


def kernel(x_prompt, x_sample, state_C, state_n, state_m, c, c_ctx, w_ada, b_ada, norm_mix_pre, norm_mix_post,
           norm_ffn_pre, norm_ffn_post, w_in, b_in, conv_a_w, conv_a_b, ln_a_w, ln_a_b, w_a_out, w_pool,
           pool_scale, mlstm_norm_w, w_c_out, w_out, w_ffn_up, ffn_conv_w, ffn_conv_b, w_ffn_down, _depth=None,
           _cores=8):
    f = lambda a: np.ascontiguousarray(np.asarray(a, dtype=np.float32))
    depth = _depth or w_ada.shape[0]
    if depth not in _CACHE:
        _CACHE[depth] = build(depth)
    nc = _CACHE[depth]
    ident, triU, triL, inv = _consts()
    shared = {"w_ada": f(w_ada)[:depth], "b_ada": f(b_ada)[:depth], "norm_mix_pre": f(norm_mix_pre)[:depth],
              "norm_mix_post": f(norm_mix_post)[:depth], "norm_ffn_pre": f(norm_ffn_pre)[:depth],
              "norm_ffn_post": f(norm_ffn_post)[:depth], "w_in": f(w_in)[:depth], "b_in": f(b_in)[:depth],
              "conv_a_w": f(conv_a_w)[:depth], "conv_a_b": f(conv_a_b)[:depth], "ln_a_w": f(ln_a_w)[:depth],
              "ln_a_b": f(ln_a_b)[:depth], "w_a_out": f(w_a_out)[:depth], "w_pool": f(w_pool)[:depth],
              "pool_scale": f(pool_scale)[:depth], "mlstm_norm_w": f(mlstm_norm_w)[:depth],
              "w_c_out": f(w_c_out)[:depth], "w_out": f(w_out)[:depth], "w_ffn_up": f(w_ffn_up)[:depth],
              "ffn_conv_w": f(ffn_conv_w)[:depth], "ffn_conv_b": f(ffn_conv_b)[:depth],
              "w_ffn_down": f(w_ffn_down)[:depth], "c_ident": ident, "c_triU": triU, "c_triL": triL,
              "c_invcnt": inv}
    xp = f(x_prompt)
    xsm = f(x_sample)
    sC, sn, sm = f(state_C), f(state_n), f(state_m)
    cc, cctx = f(c), f(c_ctx)
    in_maps = []
    for core in range(_cores):
        b = core % 4
        m = dict(shared)
        m["xs"] = np.ascontiguousarray(np.concatenate([xsm[b], xp[core * 4:(core + 1) * 4].reshape(T, D)], axis=0))
        m["cond"] = np.ascontiguousarray(np.stack([cc[b], cctx], axis=0))
        m["C0"] = np.ascontiguousarray(sC[b][:depth])
        m["n0"] = np.ascontiguousarray(sn[b][:depth])
        m["m0"] = np.ascontiguousarray(sm[b][:depth].reshape(depth, 8))
        in_maps.append(m)
    res = run_bass_kernel_spmd(nc, in_maps, core_ids=list(range(_cores)))
    R = res.results
    B = x_prompt.shape[0]
    yp = np.zeros((B, 256, D), np.float32)
    ys = np.zeros((4, T, D), np.float32)
    nC = np.zeros((B, depth, 2, NH, DH, DH), np.float32)
    nn = np.zeros((B, depth, 2, NH, DH), np.float32)
    nm = np.zeros((B, depth, 2, NH), np.float32)
    for core in range(_cores):
        r = R[core]
        yy = np.asarray(r["y"])
        if core < 4:
            ys[core] = yy[0:T]
        yp[core * 4:(core + 1) * 4] = yy[T:2 * T].reshape(4, 256, D)
        nC[core * 4:(core + 1) * 4] = np.asarray(r["stC"])
        nn[core * 4:(core + 1) * 4] = np.asarray(r["stn"])
        nm[core * 4:(core + 1) * 4] = np.asarray(r["stm"]).reshape(4, depth, 2, NH)
    return yp, ys, nC, nn, nm
```

```python
import numpy as np
from contextlib import ExitStack
import concourse.bass as bass
import concourse.mybir as mybir
from concourse.bass_utils import run_bass_kernel_spmd

F32 = mybir.dt.float32
BF16 = mybir.dt.bfloat16
AF = mybir.ActivationFunctionType
ALU = mybir.AluOpType
AX = mybir.AxisListType

D = 2048
T = 1024
NT = 8
WA = 512
WB = 512
WC = 1024
NH = 4
DH = 256
DFF = 5632
NIN = 11792
OFF_POOL = 1024
OFF_QKV = 1536
OFF_OG = 4608
OFF_GATES = 5632
OFF_MERGE = 5648
EPS = 1e-6
POOLW = (2, 4, 8, 16)
BIGN = 4112


def _flat(x):
    out = []
    for a in x:
        if a is None:
            continue
        if isinstance(a, (list, tuple)):
            out.extend(_flat(a))
        else:
            out.append(a)
    return out


SAME_ENG_WAIT = True


class Res:
    __slots__ = ("lw", "rd", "dr")

    def __init__(self):
        self.lw = None
        self.rd = {}
        self.dr = []


class Op:
    __slots__ = ("eng", "fn", "deps", "dma", "sig", "sigval", "semk")

    def __init__(self, eng, fn, deps, dma):
        self.eng = eng
        self.fn = fn
        self.deps = deps
        self.dma = dma
        self.sig = False
        self.sigval = 0
        self.semk = None


class Sched:
    ENGS = ("pe", "act", "dve", "pool", "sp")
    NDS = 12

    def __init__(self):
        self.ops = []

    def add(self, eng, fn, reads=(), writes=(), dma=False):
        oid = len(self.ops)
        deps = set()
        reads = _flat(reads)
        writes = _flat(writes)
        for r in reads:
            if r.lw is not None:
                deps.add(r.lw)
        for w in writes:
            if w.lw is not None:
                deps.add(w.lw)
            deps.update(w.rd.values())
            deps.update(w.dr)
        self.ops.append(Op(eng, fn, deps, dma))
        for r in reads:
            if dma:
                r.dr.append(oid)
            else:
                r.rd[eng] = oid
        for w in writes:
            w.lw = oid
            w.rd = {}
            w.dr = []
        return oid

    def emit(self, nc, es):
        ops = self.ops
        for o in ops:
            if o.dma:
                o.sig = True
            for d in o.deps:
                od = ops[d]
                if od.dma:
                    continue
                if od.eng == o.eng and not o.dma and (o.eng == "pe" or not SAME_ENG_WAIT):
                    continue
                od.sig = True
        cnt = {e: 0 for e in self.ENGS}
        dcnt = {}
        dn = {e: 0 for e in self.ENGS}
        for o in ops:
            if o.dma:
                k = (o.eng, dn[o.eng] % self.NDS)
                dn[o.eng] += 1
                dcnt[k] = dcnt.get(k, 0) + 1
                o.semk = k
                o.sigval = 16 * dcnt[k]
            elif o.sig:
                cnt[o.eng] += 1
                o.sigval = cnt[o.eng]
        sems = {}
        for e in self.ENGS:
            sems[e] = es.enter_context(nc.semaphore("s_" + e))
        for k in dcnt:
            sems[k] = es.enter_context(nc.semaphore("d_%s_%d" % k))
        block = es.enter_context(nc.Block())
        handles = {"pe": block.tensor, "act": block.scalar, "dve": block.vector, "pool": block.gpsimd,
                   "sp": block.sync}

        def run_engine(ename):
            def body(h):
                seen = {}
                for o in ops:
                    if o.eng != ename:
                        continue
                    need = {}
                    for d in o.deps:
                        od = ops[d]
                        if od.dma:
                            key = od.semk
                        else:
                            if od.eng == ename and not o.dma and (ename == "pe" or not SAME_ENG_WAIT):
                                continue
                            key = od.eng
                        if need.get(key, 0) < od.sigval:
                            need[key] = od.sigval
                    if o.dma and o.sigval > 16:
                        if need.get(o.semk, 0) < o.sigval - 16:
                            need[o.semk] = o.sigval - 16
                    for key, v in need.items():
                        if seen.get(key, 0) < v:
                            h.wait_ge(sems[key], v)
                            seen[key] = v
                    ins = o.fn(h)
                    if o.dma:
                        ins.then_inc(sems[o.semk], 16)
                    elif o.sig:
                        ins.then_inc(sems[ename], 1)
                for k, c in dcnt.items():
                    if k[0] == ename:
                        h.wait_ge(sems[k], 16 * c)
            return body

        for e in self.ENGS:
            handles[e](run_engine(e))


def build(depth=4):
    nc = bass.Bass("TRN2", target_bir_lowering=False)
    S = Sched()
    es = ExitStack()

    def din(name, shape):
        return nc.dram_tensor(name, list(shape), F32, kind="ExternalInput").ap()

    def dout(name, shape):
        return nc.dram_tensor(name, list(shape), F32, kind="ExternalOutput").ap()

    xs = din("xs", [2 * T, D])
    cond = din("cond", [2, D])
    C0 = din("C0", [depth, 2, NH, DH, DH])
    n0 = din("n0", [depth, 2, NH, DH])
    m0 = din("m0", [depth, 8])
    w_ada = din("w_ada", [depth, D, 6 * D])
    b_ada = din("b_ada", [depth, 6 * D])
    nrm = {k: din(k, [depth, D]) for k in ("norm_mix_pre", "norm_mix_post", "norm_ffn_pre", "norm_ffn_post")}
    w_in = din("w_in", [depth, D, NIN])
    b_in = din("b_in", [depth, NIN])
    conv_a_w = din("conv_a_w", [depth, 31, WA])
    conv_a_b = din("conv_a_b", [depth, WA])
    ln_a_w = din("ln_a_w", [depth, WA])
    ln_a_b = din("ln_a_b", [depth, WA])
    w_a_out = din("w_a_out", [depth, WA, D])
    w_pool = din("w_pool", [depth, 4, 128, 512])
    pool_scale = din("pool_scale", [depth, D])
    mlstm_norm_w = din("mlstm_norm_w", [depth, WC])
    w_c_out = din("w_c_out", [depth, WC, D])
    w_out = din("w_out", [depth, D, D])
    w_ffn_up = din("w_ffn_up", [depth, D, 2 * DFF])
    ffn_conv_w = din("ffn_conv_w", [depth, 3, DFF])
    ffn_conv_b = din("ffn_conv_b", [depth, DFF])
    w_ffn_down = din("w_ffn_down", [depth, DFF, D])
    c_ident = din("c_ident", [128, 128])
    c_triU = din("c_triU", [64, 64])
    c_triL = din("c_triL", [64, 64])
    c_invcnt = din("c_invcnt", [2, 4, T])

    y = dout("y", [2 * T, D])
    stC = dout("stC", [4, depth, 2, NH, DH, DH])
    stn = dout("stn", [4, depth, 2, NH, DH])
    stm = dout("stm", [4, depth, 8])
    mod_d = dout("mod_d", [2, 6 * D])
    f_d = dout("f_d", [T, D])
    R_y = [[Res() for _ in range(NT)] for _ in range(2)]
    R_f = [Res() for _ in range(NT)]
    R_modp = {(g_, pn_): Res() for g_ in range(2) for pn_ in range(24)}
    R_st = Res()

    def sb(name, shape, dt=F32):
        return es.enter_context(nc.sbuf_tensor(name, list(shape), dt))

    BIG = [sb("big%d" % i, [128, BIGN]) for i in range(8)]
    RBh = [[Res(), Res()] for _ in range(8)]
    RB = [tuple(h) for h in RBh]
    WBUF = [sb("wb%d" % i, [128, 8192], BF16) for i in range(2)]
    RW = [Res() for _ in range(2)]
    SC = [sb("sc%d" % i, [128, T]) for i in range(4)]
    RS = [Res() for _ in range(4)]
    junk = sb("junk", [128, 2048], BF16)
    R_junk = Res()
    PS = [es.enter_context(nc.psum_tensor("ps%d" % i, [128, 512], F32)) for i in range(8)]
    RP = [Res() for _ in range(8)]
    psn = [0]

    def psum():
        i = psn[0] % 8
        psn[0] += 1
        return PS[i], RP[i]

    wbn = [0]

    def bf(t, n=None):
        a = t[:].bitcast(BF16)
        return a

    def mm(out, lhsT, rhs, start, stop, R, W):
        S.add("pe", lambda e: e.matmul(out, lhsT, rhs, start=start, stop=stop), R, W)

    def tr(out, in_, ident, R, W):
        S.add("pe", lambda e: e.transpose(out, in_, ident), R, W)

    def act(out, in_, func, R, W, bias=None, scale=None, accum=None):
        kw = {}
        if bias is not None:
            kw["bias"] = bias
        if scale is not None:
            kw["scale"] = scale
        if accum is not None:
            kw["accum_out"] = accum
        S.add("act", lambda e: e.activation(out, in_, func, **kw), R, W)

    def tt(out, a, b, op, R, W, eng="dve"):
        S.add(eng, lambda e: e.tensor_tensor(out, a, b, op), R, W)

    def ts(out, a, s1, s2, op0, op1, R, W, eng="dve"):
        if op1 is None:
            S.add(eng, lambda e: e.tensor_scalar(out, a, s1, None, op0), R, W)
        else:
            S.add(eng, lambda e: e.tensor_scalar(out, a, s1, s2, op0, op1), R, W)

    def stt(out, a, s, b, op0, op1, R, W):
        S.add("dve", lambda e: e.scalar_tensor_tensor(out, a, s, b, op0, op1), R, W)

    def cpy(out, in_, R, W, eng="dve"):
        S.add(eng, lambda e: e.tensor_copy(out, in_), R, W)

    def mset(ap, v, W, eng="dve"):
        S.add(eng, lambda e: e.memset(ap, v), (), W)

    def dma(q, out, in_, R, W, nonc=False):
        if nonc:
            S.add(q, lambda e: e.dma_start(out=out, in_=in_, allow_slow_non_contiguous=True), R, W, dma=True)
        else:
            S.add(q, lambda e: e.dma_start(out=out, in_=in_), R, W, dma=True)

    gm = sb("gm", [128, 14, 128]); rgm = Res()
    G = sb("G", [64, 256]); rG = Res()
    acol = sb("acol", [128, 1])
    mcur_all = sb("mcur", [128, 2, 4, 4])
    rsm_t = sb("rsm_t", [64, 8])
    hn_s = sb("hn_s", [64, 16, 4]); R_hn = Res()
    wpool_sb = sb("wpool", [128, 4, 512], BF16); r_wp = Res()
    small = sb("small", [128, 8, 4]); rsm = [Res() for _ in range(8)]
    R_kp = [Res() for _ in range(4)]; R_va = Res(); R_sp = [Res() for _ in range(4)]
    R_cb = [Res() for _ in range(4)]; R_rs = [Res() for _ in range(4)]
    RCs = [Res() for _ in range(8)]; Rhc = [Res() for _ in range(16)]
    ident = sb("ident", [128, 128])
    identb = sb("identb", [128, 128], BF16)
    ones = sb("ones", [128, 128])
    triU = sb("triU", [64, 64])
    triL = sb("triL", [64, 64])
    R_c = Res()
    dma("sp", ident[:], c_ident, (), [R_c])
    dma("sp", triU[:], c_triU, (), [R_c])
    dma("sp", triL[:], c_triL, (), [R_c])
    cpy(identb[:], ident[:], [R_c], [R_c])
    mset(ones[:], 1.0, [R_c])

    scT = sb("scT", [128, 16, 64], BF16)
    R_scT = Res()
    cnd = SC[0]
    mset(BIG[0][0:64, 0:D], 0.0, [RB[0]])
    dma("sp", BIG[0][0:1, 0:D], cond[0:1, :], (), [RB[0]])
    dma("sp", BIG[0][32:33, 0:D], cond[1:2, :], (), [RB[0]])
    act(BIG[0][0:64, 0:D], BIG[0][0:64, 0:D], AF.Silu, [RB[0]], [RB[0]])
    for kt in range(16):
        p, rp = psum()
        tr(p[:, 0:64], BIG[0][0:64, kt * 128:(kt + 1) * 128], ident[0:64, 0:64], [RB[0], R_c], [rp])
        cpy(scT[:, kt, :], p[:, 0:64], [rp], [R_scT])

    prm = sb("prm", [128, 640])
    R_prm = Res()
    P_BIN, P_CAW, P_CAB, P_LNW, P_LNB, P_PSC, P_MNW, P_FCW, P_FCB, P_BQ = 0, 92, 216, 220, 224, 228, 244, 252, 384, 428
    gbias = sb("gbias", [16, 1])

    def load_rows_T(dst_ap_fn, src2d, nrows, ncolt, stage, rstage):
        dma("sp", stage[0:nrows, 0:ncolt * 128], src2d, (), [rstage])
        for c in range(ncolt):
            p, rp = psum()
            tr(p[:, 0:nrows], stage[0:nrows, c * 128:(c + 1) * 128], ident[0:nrows, 0:nrows], [rstage, R_c], [rp])
            cpy(dst_ap_fn(c), p[:, 0:nrows], [rp], [R_prm])

    def load_params(l):
        st, rs = BIG[7], RB[7]
        load_rows_T(lambda c: prm[:, P_BIN:P_BIN + 44], b_in[l, 0:5632].rearrange("(r c) -> r c", c=128), 44, 1, st, rs)
        load_rows_T(lambda c: prm[:, P_BIN + 44:P_BIN + 92], b_in[l, OFF_MERGE:NIN].rearrange("(r c) -> r c", c=128), 48, 1, st, rs)
        dma("sp", gbias[:], b_in[l, OFF_GATES:OFF_MERGE].rearrange("(p o) -> p o", o=1), (), [R_prm])
        pv = prm[:, P_CAW:P_CAW + 124].rearrange("p (f k) -> p f k", k=31)
        load_rows_T(lambda c: pv[:, c, :], conv_a_w[l], 31, 4, st, rs)
        load_rows_T(lambda c: prm[:, P_CAB:P_CAB + 4], conv_a_b[l].rearrange("(r c) -> r c", c=128), 4, 1, st, rs)
        load_rows_T(lambda c: prm[:, P_LNW:P_LNW + 4], ln_a_w[l].rearrange("(r c) -> r c", c=128), 4, 1, st, rs)
        load_rows_T(lambda c: prm[:, P_LNB:P_LNB + 4], ln_a_b[l].rearrange("(r c) -> r c", c=128), 4, 1, st, rs)
        load_rows_T(lambda c: prm[:, P_PSC:P_PSC + 16], pool_scale[l].rearrange("(r c) -> r c", c=128), 16, 1, st, rs)
        load_rows_T(lambda c: prm[:, P_MNW:P_MNW + 8], mlstm_norm_w[l].rearrange("(r c) -> r c", c=128), 8, 1, st, rs)
        fv = prm[:, P_FCW:P_FCW + 132].rearrange("p (f k) -> p f k", k=3)
        load_rows_T(lambda c: fv[:, c, :], ffn_conv_w[l][:, 0:2816], 3, 22, st, rs)
        load_rows_T(lambda c: fv[:, 22 + c, :], ffn_conv_w[l][:, 2816:5632], 3, 22, st, rs)
        load_rows_T(lambda c: prm[:, P_FCB:P_FCB + 44], ffn_conv_b[l].rearrange("(r c) -> r c", c=128), 44, 1, st, rs)
        ts(prm[:, P_BQ:P_BQ + 8], prm[:, P_BIN + 12:P_BIN + 20], 1.0 / 16.0, None, ALU.mult, None, [R_prm], [R_prm])

    def load_panel(src, KT, ncols):
        i = wbn[0] % 2
        wbn[0] += 1
        v = WBUF[i][:, 0:KT * ncols].rearrange("p (k n) -> p k n", n=ncols)
        dma("pool", v, src.rearrange("(k p) n -> p k n", p=128), (), [RW[i]])
        return v, RW[i]

    def linear(src, KT, ncols, rhs_fn, evac_fn, ft0=0):
        v, rw = load_panel(src, KT, ncols)
        for ft in range(ncols // 128):
            for tb in range(2):
                p, rp = psum()
                for kt in range(KT):
                    ra, rr = rhs_fn(kt, tb)
                    mm(p[:, :], v[:, kt, ft * 128:(ft + 1) * 128], ra, kt == 0, kt == KT - 1, [rw] + rr, [rp])
                evac_fn(ft0 + ft, tb, p, rp)

    def mod_phase(l):
        for pn in range(24):
            stg, rsg = SC[(pn % 2) * 2], RS[(pn % 2) * 2]
            bst, rbs = SC[(pn % 2) * 2 + 1], RS[(pn % 2) * 2 + 1]
            v, rw = load_panel(w_ada[l, :, pn * 512:(pn + 1) * 512], 16, 512)
            p, rp = psum()
            for kt in range(16):
                mm(p[0:64, :], scT[:, kt, :], v[:, kt, :], kt == 0, kt == 15, [rw, R_scT], [rp])
            dma("sp", bst[0:1, 0:512], b_ada[l, pn * 512:(pn + 1) * 512].rearrange("(o n) -> o n", o=1), (), [rbs])
            dma("sp", bst[32:33, 0:512], b_ada[l, pn * 512:(pn + 1) * 512].rearrange("(o n) -> o n", o=1), (), [rbs])
            tt(stg[0:1, 0:512], p[0:1, :], bst[0:1, 0:512], ALU.add, [rp, rbs], [rsg])
            tt(stg[32:33, 0:512], p[32:33, :], bst[32:33, 0:512], ALU.add, [rp, rbs], [rsg])
            dma("sp", mod_d[0:1, pn * 512:(pn + 1) * 512], stg[0:1, 0:512], [rsg], [R_modp[(0, pn)]])
            dma("sp", mod_d[1:2, pn * 512:(pn + 1) * 512], stg[32:33, 0:512], [rsg], [R_modp[(1, pn)]])

    def bcast_load(dst, rdst, vec, modsel=None):
        rr_ = [] if modsel is None else [R_modp[(modsel[0], modsel[1] * 4 + k_)] for k_ in range(4)]
        dma("sp", dst, vec.partition_broadcast(128), rr_, [rdst])

    def norm_pass(l, g, which, xsrc, hviews, hres, bcb, rbcb, xb, rxb):
        gam = BIG[bcb][:, 0:D]
        shf = BIG[bcb][:, D:2 * D]
        rb = RB[bcb]
        si, ci = (0, 1) if which == 0 else (3, 4)
        nw = nrm["norm_mix_pre" if which == 0 else "norm_ffn_pre"]
        bcast_load(gam, rb, mod_d[g, ci * D:(ci + 1) * D], (g, ci))
        bcast_load(shf, rb, nw[l])
        stt(gam, gam, 1.0, shf, ALU.add, ALU.mult, [rb], [rb])
        bcast_load(shf, rb, mod_d[g, si * D:(si + 1) * D], (g, si))
        xt = [BIG[xb][:, 0:D], BIG[xb][:, D:2 * D]]
        for i in range(NT):
            x_t = xt[i % 2]
            rx = RBh[xb][i % 2]
            ri = rsm[i]
            dma("sp", x_t, xsrc[i][0], xsrc[i][1], [rx])
            act(junk[:, :], x_t, AF.Square, [rx], [ri], accum=small[:, i, 0:1])
            ts(small[:, i, 1:2], small[:, i, 0:1], 1.0 / D, EPS, ALU.mult, ALU.add, [ri], [ri])
            act(small[:, i, 2:3], small[:, i, 1:2], AF.Sqrt, [ri], [ri])
            S.add("dve", (lambda o, a: (lambda e: e.reciprocal(o, a)))(small[:, i, 3:4], small[:, i, 2:3]), [ri], [ri])
            stt(x_t, x_t, small[:, i, 3:4], gam, ALU.mult, ALU.mult, [rx, ri, rb], [rx])
            tt(x_t, x_t, shf, ALU.add, [rx, rb], [rx])
            for q4 in range(4):
                p, rp = psum()
                for j in range(4):
                    kt = q4 * 4 + j
                    tr(p[:, j * 128:(j + 1) * 128], x_t[:, kt * 128:(kt + 1) * 128], ident[:, :], [rx, R_c], [rp])
                for j in range(4):
                    kt = q4 * 4 + j
                    if j % 2 == 0:
                        cpy(hviews[kt][:, i * 128:(i + 1) * 128], p[:, j * 128:(j + 1) * 128], [rp], [hres[kt]])
                    else:
                        act(hviews[kt][:, i * 128:(i + 1) * 128], p[:, j * 128:(j + 1) * 128], AF.Copy, [rp], [hres[kt]])

    def final_pass(l, g, which, xsrc, last):
        bcb = 7
        rb = RB[7]
        pg = BIG[7][:, 0:D]
        tmpb = BIG[7][:, D:2 * D]
        gi = 2 if which == 0 else 5
        nw = nrm["norm_mix_post" if which == 0 else "norm_ffn_post"]
        bcast_load(pg, rb, mod_d[g, gi * D:(gi + 1) * D], (g, gi))
        bcast_load(tmpb, rb, nw[l])
        tt(pg, pg, tmpb, ALU.mult, [rb], [rb])
        xt = [BIG[6][:, 0:D], BIG[6][:, D:2 * D]]
        ft_ = [BIG[5][:, 0:D], BIG[5][:, D:2 * D]]
        for i in range(NT):
            x_t = xt[i % 2]
            f_t = ft_[i % 2]
            rx = RBh[6][i % 2]
            rf = RBh[5][i % 2]
            ri = rsm[i]
            dma("sp", x_t, xsrc[i][0], xsrc[i][1], [rx])
            dma("sp", f_t, f_d[i * 128:(i + 1) * 128, :], [R_f[i]], [rf])
            act(junk[:, :], f_t, AF.Square, [rf], [ri], accum=small[:, i, 0:1])
            ts(small[:, i, 1:2], small[:, i, 0:1], 1.0 / D, EPS, ALU.mult, ALU.add, [ri], [ri])
            act(small[:, i, 2:3], small[:, i, 1:2], AF.Sqrt, [ri], [ri])
            S.add("dve", (lambda o, a: (lambda e: e.reciprocal(o, a)))(small[:, i, 3:4], small[:, i, 2:3]), [ri], [ri])
            stt(f_t, f_t, small[:, i, 3:4], pg, ALU.mult, ALU.mult, [rf, ri, rb], [rf])
            tt(f_t, f_t, x_t, ALU.add, [rf, rx], [rf])
            dma("sp", y[g * T + i * 128:g * T + (i + 1) * 128, :], f_t, [rf], [R_y[g][i]])

    def out_block_to_fd(stage, rstage, tstage, rtstage, d0):
        sv = stage
        tv = tstage
        for i in range(NT):
            p, rp = psum()
            for j in range(4):
                tr(p[:, j * 128:(j + 1) * 128], sv[:, j, i * 128:(i + 1) * 128], ident[:, :], [rstage, R_c], [rp])
            if i % 2 == 0:
                cpy(tv[:, i, :], p[:, :], [rp], [rtstage])
            else:
                act(tv[:, i, :], p[:, :], AF.Copy, [rp], [rtstage])
            dma("sp", f_d[i * 128:(i + 1) * 128, d0 * 128:d0 * 128 + 512], tv[:, i, :], [rtstage], [R_f[i]])

    def shifted(a, g, kind, d):
        if g == 0:
            a3 = a.rearrange("p (r c) -> p r c", c=64)
            if kind == "row":
                lo, hi = max(0, -d), 64 - max(0, d)
                return a3[:, :, lo:hi], a3[:, :, lo + d:hi + d]
            lo, hi = max(0, -d), 16 - max(0, d)
            return a3[:, lo:hi, :], a3[:, lo + d:hi + d, :]
        a3 = a.rearrange("p (r c) -> p r c", c=256)
        lo, hi = max(0, -d), 256 - max(0, d)
        return a3[:, :, lo:hi], a3[:, :, lo + d:hi + d]

    def mixer(l, g):
        xsrc = []
        for i in range(NT):
            if l == 0:
                xsrc.append((xs[g * T + i * 128:g * T + (i + 1) * 128, :], []))
            else:
                xsrc.append((y[g * T + i * 128:g * T + (i + 1) * 128, :], [R_y[g][i]]))
        wl = w_in[l]

        def hT_views(b0, b1):
            hv, hr = [], []
            for kt in range(16):
                b = b0 if kt < 8 else b1
                hv.append(bf(BIG[b])[:, (kt % 8) * T:(kt % 8 + 1) * T])
                hr.append(RB[b])
            return hv, hr

        hv, hr = hT_views(0, 1)
        norm_pass(l, g, 0, xsrc, hv, hr, 7, RB[7], 6, RB[6])

        def rhs_h(hv, hr):
            return lambda kt, tb: (hv[kt][:, tb * 512:(tb + 1) * 512], [hr[kt]])

        qkvT = [bf(BIG[2 + j])[:, 0:8 * T].rearrange("p (f t) -> p f t", t=T) for j in range(3)]
        for j in range(3):
            for pn in range(2):
                c0 = OFF_QKV + j * WC + pn * 512

                def ev(ft, tb, p, rp, j=j):
                    bt = OFF_QKV // 128 + j * 8 + ft
                    dst = qkvT[j][:, ft, tb * 512:(tb + 1) * 512]
                    if j == 0:
                        act(dst, p[:, :], AF.Identity, [rp, R_prm], [RB[2]], bias=prm[:, P_BQ + ft:P_BQ + ft + 1], scale=1.0 / 16.0)
                    elif (ft + tb) % 2 == 0:
                        act(dst, p[:, :], AF.Identity, [rp, R_prm], [RB[2 + j]], bias=prm[:, P_BIN + bt:P_BIN + bt + 1])
                    else:
                        ts(dst, p[:, :], prm[:, P_BIN + bt:P_BIN + bt + 1], None, ALU.add, None, [rp, R_prm], [RB[2 + j]])
                linear(wl[:, c0:c0 + 512], 16, 512, rhs_h(hv, hr), ev, ft0=pn * 4)
        gT = SC[3]
        i = wbn[0] % 2
        wbn[0] += 1
        gv = WBUF[i][:, 0:256].rearrange("p (k n) -> p k n", n=16)
        dma("pool", gv, wl[:, OFF_GATES:OFF_MERGE].rearrange("(k p) n -> p k n", p=128), (), [RW[i]], nonc=True)
        for tb in range(2):
            p, rp = psum()
            for kt in range(16):
                mm(p[0:16, :], gv[:, kt, :], hv[kt][:, tb * 512:(tb + 1) * 512], kt == 0, kt == 15, [RW[i], hr[kt]], [rp])
            act(gT[0:16, tb * 512:(tb + 1) * 512], p[0:16, :], AF.Identity, [rp, R_prm], [RS[3]], bias=gbias[:, 0:1])

        mlstm(l, g, qkvT, gT)

        hv, hr = hT_views(1, 2)
        norm_pass(l, g, 0, xsrc, hv, hr, 7, RB[7], 6, RB[6])
        hcT = bf(BIG[0])[:, 0:8 * T].rearrange("p (f t) -> p f t", t=T)

        for pn in range(2):
            c0 = OFF_OG + pn * 512

            def ev(ft, tb, p, rp):
                bt = OFF_OG // 128 + ft
                tmp = SC[(ft + tb) % 2][:, 0:512]
                rt = RS[(ft + tb) % 2]
                act(tmp, p[:, :], AF.Sigmoid, [rp, R_prm], [rt], bias=prm[:, P_BIN + bt:P_BIN + bt + 1])
                tt(hcT[:, ft, tb * 512:(tb + 1) * 512], hcT[:, ft, tb * 512:(tb + 1) * 512], tmp, ALU.mult, [rt, RB[0]], [RB[0]])
            linear(wl[:, c0:c0 + 512], 16, 512, rhs_h(hv, hr), ev, ft0=pn * 4)

        ga = BIG[3][:, 0:4 * T].rearrange("p (f t) -> p f t", t=T)
        gb = BIG[4][:, 0:4 * T].rearrange("p (f t) -> p f t", t=T)
        acc = BIG[5][:, 0:4 * T].rearrange("p (f t) -> p f t", t=T)
        sq = BIG[6][:, 0:4 * T].rearrange("p (f t) -> p f t", t=T)
        aT = bf(BIG[7])[:, 0:4 * T].rearrange("p (f t) -> p f t", t=T)
        pT = bf(BIG[7])[:, 4 * T:8 * T].rearrange("p (f t) -> p f t", t=T)
        for pn in range(2):
            def ev(ft, tb, p, rp):
                if ft < 4:
                    act(ga[:, ft, tb * 512:(tb + 1) * 512], p[:, :], AF.Identity, [rp, R_prm], [RB[3]], bias=prm[:, P_BIN + ft:P_BIN + ft + 1])
                else:
                    act(gb[:, ft - 4, tb * 512:(tb + 1) * 512], p[:, :], AF.Sigmoid, [rp, R_prm], [RB[4]], bias=prm[:, P_BIN + ft:P_BIN + ft + 1])
            linear(wl[:, pn * 512:(pn + 1) * 512], 16, 512, rhs_h(hv, hr), ev, ft0=pn * 4)
        caw = prm[:, P_CAW:P_CAW + 124].rearrange("p (f k) -> p f k", k=31)
        for ft in range(4):
            tt(ga[:, ft, :], ga[:, ft, :], gb[:, ft, :], ALU.mult, [RB[3], RB[4]], [RB[3]])
            ts(acc[:, ft, :], ga[:, ft, :], caw[:, ft, 15:16], prm[:, P_CAB + ft:P_CAB + ft + 1], ALU.mult, ALU.add, [RB[3], R_prm], [RB[5]])
            for k in range(31):
                d = k - 15
                if d == 0:
                    continue
                dv, _ = shifted(acc[:, ft, :], g, "row", d)
                _, sv = shifted(ga[:, ft, :], g, "row", d)
                stt(dv, sv, caw[:, ft, k:k + 1], dv, ALU.mult, ALU.add, [RB[3], RB[5], R_prm], [RB[5]])
            act(sq[:, ft, :], acc[:, ft, :], AF.Square, [RB[5]], [RB[6]])
        for tb in range(2):
            p1, r1 = psum()
            p2, r2 = psum()
            for ft in range(4):
                mm(p1[:, :], ones[:, :], acc[:, ft, tb * 512:(tb + 1) * 512], ft == 0, ft == 3, [R_c, RB[5]], [r1])
            for ft in range(4):
                mm(p2[:, :], ones[:, :], sq[:, ft, tb * 512:(tb + 1) * 512], ft == 0, ft == 3, [R_c, RB[6]], [r2])
            mean = SC[0][:, 0:512]
            var = SC[1][:, 0:512]
            act(mean, p1[:, :], AF.Copy, [r1], [RS[0]], scale=1.0 / WA)
            tt(var, mean, mean, ALU.mult, [RS[0]], [RS[1]])
            stt(var, p2[:, :], 1.0 / WA, var, ALU.mult, ALU.subtract, [r2, RS[1]], [RS[1]])
            ts(var, var, EPS, None, ALU.add, None, [RS[1]], [RS[1]])
            act(var, var, AF.Sqrt, [RS[1]], [RS[1]])
            S.add("dve", (lambda o: (lambda e: e.reciprocal(o, o)))(var), [RS[1]], [RS[1]])
            for ft in range(4):
                tmp = SC[2][:, 0:512]
                tt(tmp, acc[:, ft, tb * 512:(tb + 1) * 512], mean, ALU.subtract, [RB[5], RS[0]], [RS[2]])
                tt(tmp, tmp, var, ALU.mult, [RS[2], RS[1]], [RS[2]])
                act(aT[:, ft, tb * 512:(tb + 1) * 512], tmp, AF.Silu, [RS[2], R_prm], [RB[7]],
                    bias=prm[:, P_LNB + ft:P_LNB + ft + 1], scale=prm[:, P_LNW + ft:P_LNW + ft + 1])

        zp = BIG[3][:, 0:4 * T].rearrange("p (f t) -> p f t", t=T)
        pacc = BIG[4][:, 0:4 * T].rearrange("p (f t) -> p f t", t=T)

        def ev(ft, tb, p, rp):
            bt = OFF_POOL // 128 + ft
            act(zp[:, ft, tb * 512:(tb + 1) * 512], p[:, :], AF.Identity, [rp, R_prm], [RB[3]], bias=prm[:, P_BIN + bt:P_BIN + bt + 1])
        linear(wl[:, OFF_POOL:OFF_POOL + 512], 16, 512, rhs_h(hv, hr), ev)
        for ft in range(4):
            w = POOLW[ft]
            cpy(pacc[:, ft, :], zp[:, ft, :], [RB[3]], [RB[4]])
            for d in range(-(w // 2), w // 2):
                if d == 0:
                    continue
                dv, _ = shifted(pacc[:, ft, :], g, "col", d)
                _, sv = shifted(zp[:, ft, :], g, "col", d)
                tt(dv, dv, sv, ALU.add, [RB[3], RB[4]], [RB[4]])
            ic = SC[0]
            dma("sp", ic[:, :], c_invcnt[g, ft].partition_broadcast(128), (), [RS[0]])
            tt(pacc[:, ft, :], pacc[:, ft, :], ic[:, :], ALU.mult, [RB[4], RS[0]], [RB[4]])
            tt(pT[:, ft, :], pacc[:, ft, :], zp[:, ft, :], ALU.subtract, [RB[4], RB[3]], [RB[7]])

        def mixv(dt_):
            b = 3 if dt_ < 8 else 4
            return bf(BIG[b])[:, (dt_ % 8) * T:(dt_ % 8 + 1) * T], RB[b]
        dma("pool", wpool_sb[:, :, :], w_pool[l].rearrange("g c d -> c g d"), (), [r_wp])
        for db in range(4):
            wa_v, wa_r = load_panel(w_a_out[l][:, db * 512:(db + 1) * 512], 4, 512)
            wc_v, wc_r = load_panel(w_c_out[l][:, db * 512:(db + 1) * 512], 8, 512)
            ya = BIG[5][:, 0:4 * T].rearrange("p (f t) -> p f t", t=T)
            yc = BIG[6][:, 0:4 * T].rearrange("p (f t) -> p f t", t=T)
            for dj in range(4):
                for tb in range(2):
                    p, rp = psum()
                    for kt in range(4):
                        mm(p[:, :], wa_v[:, kt, dj * 128:(dj + 1) * 128], aT[:, kt, tb * 512:(tb + 1) * 512], kt == 0, kt == 3, [wa_r, RB[7]], [rp])
                    act(ya[:, dj, tb * 512:(tb + 1) * 512], p[:, :], AF.Copy, [rp], [RB[5]])
                    p, rp = psum()
                    for kt in range(8):
                        mm(p[:, :], wc_v[:, kt, dj * 128:(dj + 1) * 128], hcT[:, kt, tb * 512:(tb + 1) * 512], kt == 0, kt == 7, [wc_r, RB[0]], [rp])
                    cpy(yc[:, dj, tb * 512:(tb + 1) * 512], p[:, :], [rp], [RB[6]])
            for br in range(3):
                c0 = OFF_MERGE + br * D + db * 512

                def ev(ft, tb, p, rp, br=br, db=db):
                    dt_ = db * 4 + ft
                    bt = 44 + br * 16 + dt_
                    gt = SC[(ft + tb) % 2][:, 0:512]
                    rg = RS[(ft + tb) % 2]
                    act(gt, p[:, :], AF.Sigmoid, [rp, R_prm], [rg], bias=prm[:, P_BIN + bt:P_BIN + bt + 1])
                    mv, mr = mixv(dt_)
                    msl = mv[:, tb * 512:(tb + 1) * 512]
                    if br == 0:
                        tt(ya[:, ft, tb * 512:(tb + 1) * 512], ya[:, ft, tb * 512:(tb + 1) * 512], gt, ALU.mult, [rg, RB[5]], [RB[5]])
                    elif br == 1:
                        gidx = dt_ // 4
                        p2, rp2 = psum()
                        mm(p2[:, :], wpool_sb[:, gidx, (dt_ % 4) * 128:(dt_ % 4 + 1) * 128], pT[:, gidx, tb * 512:(tb + 1) * 512], True, True, [r_wp, RB[7]], [rp2])
                        tmp = SC[2][:, 0:512]
                        stt(tmp, p2[:, :], prm[:, P_PSC + dt_:P_PSC + dt_ + 1], gt, ALU.mult, ALU.mult, [rp2, rg, R_prm], [RS[2]])
                        tt(ya[:, ft, tb * 512:(tb + 1) * 512], ya[:, ft, tb * 512:(tb + 1) * 512], tmp, ALU.add, [RS[2], RB[5]], [RB[5]])
                    else:
                        tt(yc[:, ft, tb * 512:(tb + 1) * 512], yc[:, ft, tb * 512:(tb + 1) * 512], gt, ALU.mult, [rg, RB[6]], [RB[6]])
                        tt(msl, ya[:, ft, tb * 512:(tb + 1) * 512], yc[:, ft, tb * 512:(tb + 1) * 512], ALU.add, [RB[5], RB[6]], [mr])
                linear(wl[:, c0:c0 + 512], 16, 512, rhs_h(hv, hr), ev)

        for db in range(4):
            stage = BIG[5][:, 0:4 * T].rearrange("p (f t) -> p f t", t=T)
            tstage = BIG[6][:, 0:8 * 512].rearrange("p (i n) -> p i n", n=512)

            def ev(ft, tb, p, rp):
                if (ft + tb) % 2 == 0:
                    cpy(stage[:, ft, tb * 512:(tb + 1) * 512], p[:, :], [rp], [RB[5]])
                else:
                    act(stage[:, ft, tb * 512:(tb + 1) * 512], p[:, :], AF.Copy, [rp], [RB[5]])
            linear(w_out[l][:, db * 512:(db + 1) * 512], 16, 512, lambda kt, tb: (mixv(kt)[0][:, tb * 512:(tb + 1) * 512], [mixv(kt)[1]]), ev)
            out_block_to_fd(stage, RB[5], tstage, RB[6], db * 4)
        final_pass(l, g, 0, xsrc, False)

    def mlstm(l, g, qkvT, gT):
        qT, kT, vT = qkvT
        Rq, Rk, Rv = RB[2], RB[3], RB[4]
        nseq, nch = (1, 16) if g == 0 else (4, 4)
        LF, LI, BC_, UU, ABC, BL, MALL, MPREV, DEC, CSC, KFAC, ATOK, KSC, THR = [gm[:, i, :] for i in range(14)]
        p, rp = psum()
        for c in range(16):
            tr(p[0:64, c * 16:(c + 1) * 16], gT[0:16, c * 64:(c + 1) * 64], ident[0:16, 0:16], [RS[3], R_c], [rp])
        cpy(G[:, :], p[0:64, 0:256], [rp], [rG])
        G5 = G[:, :].rearrange("p (c d k h) -> p c d k h", d=2, k=2, h=4)
        for dr in range(2):
            lfv = LF[0:64, dr * 64:(dr + 1) * 64].rearrange("p (c h) -> p c h", h=4)
            liv = LI[0:64, dr * 64:(dr + 1) * 64].rearrange("p (c h) -> p c h", h=4)
            act(lfv, G5[:, :, dr, 1, :], AF.Exp, [rG], [rgm], scale=-1.0)
            cpy(liv, G5[:, :, dr, 0, :], [rG], [rgm])
        act(LF[0:64, :], LF[0:64, :], AF.Ln, [rgm], [rgm], bias=1.0)
        ts(LF[0:64, :], LF[0:64, :], -1.0, None, ALU.mult, None, [rgm], [rgm])
        p, rp = psum()
        mm(p[0:64, 0:64], triU[:, :], LF[0:64, 0:64], True, True, [R_c, rgm], [rp])
        mm(p[0:64, 64:128], triL[:, :], LF[0:64, 64:128], True, True, [R_c, rgm], [rp])
        cpy(BC_[0:64, :], p[0:64, 0:128], [rp], [rgm])
        tt(UU[0:64, :], LI[0:64, :], BC_[0:64, :], ALU.subtract, [rgm], [rgm])
        p, rp = psum()
        tr(p[:, 0:64], UU[0:64, :], ident[0:64, 0:64], [rgm, R_c], [rp])
        S.add("dve", (lambda o, a: (lambda e: e.tensor_reduce(o, a, AX.X, ALU.max)))(acol[:, :], p[:, 0:64]), [rp], [rgm])
        diag = SC[0][:, 0:128]
        ts(diag, ident[:, :], acol[:, 0:1], None, ALU.mult, None, [rgm, R_c], [RS[0]])
        p2, rp2 = psum()
        mm(p2[:, 0:128], ones[:, :], diag, True, True, [R_c, RS[0]], [rp2])
        cpy(ABC, p2[:, 0:128], [rp2], [rgm])
        p3, rp3 = psum()
        mm(p3[:, 0:128], ones[0:64, :], LF[0:64, :], True, True, [R_c, rgm], [rp3])
        cpy(BL, p3[:, 0:128], [rp3], [rgm])
        mcur = mcur_all[:, :, 0:nseq, :]
        if g == 0:
            dma("sp", mcur[:, :, 0, :], m0[l].rearrange("(d h) -> d h", h=4).partition_broadcast(128), (), [rgm])
        else:
            mset(mcur[:, :, :, :], 0.0, [rgm])

        def colv(arr, dr, cc):
            return arr[:, dr * 64:(dr + 1) * 64].rearrange("p (q c h) -> p q c h", q=nseq, h=4)[:, :, cc, :]
        for j in range(nch):
            for dr in range(2):
                cc = j if dr == 0 else nch - 1 - j
                cpy(colv(MPREV, dr, cc), mcur[:, dr, :, :], [rgm], [rgm])
                tt(colv(MALL, dr, cc), mcur[:, dr, :, :], colv(ABC, dr, cc), ALU.max, [rgm], [rgm])
                tt(mcur[:, dr, :, :], colv(MALL, dr, cc), colv(BL, dr, cc), ALU.add, [rgm], [rgm])
                if g == 1 and j == nch - 1:
                    pass
        if g == 1:
            for q in range(4):
                dma("sp", stm[q, l].rearrange("(o d h) -> o d h", o=1, h=4), mcur[0:1, :, q, :], [rgm], [Res()])
        tt(DEC, MPREV, MALL, ALU.subtract, [rgm], [rgm])
        act(DEC, DEC, AF.Exp, [rgm], [rgm])
        tt(CSC, MPREV, ABC, ALU.subtract, [rgm], [rgm])
        act(CSC, CSC, AF.Exp, [rgm], [rgm])
        tt(KFAC, ABC, MALL, ALU.subtract, [rgm], [rgm])
        act(KFAC, KFAC, AF.Exp, [rgm], [rgm])
        tt(ATOK[0:64, :], UU[0:64, :], ABC[0:64, :], ALU.subtract, [rgm], [rgm])
        act(ATOK[0:64, :], ATOK[0:64, :], AF.Exp, [rgm], [rgm])
        tt(KSC[0:64, :], ATOK[0:64, :], KFAC[0:64, :], ALU.mult, [rgm], [rgm])
        tt(THR[0:64, :], BC_[0:64, :], ABC[0:64, :], ALU.add, [rgm], [rgm])
        act(THR[0:64, :], THR[0:64, :], AF.Exp, [rgm], [rgm], scale=-1.0)

        Caug = BIG[7][:, 0:8 * 2 * 257].rearrange("p (s k e) -> p s k e", k=2, e=257)
        RC = RB[7]
        hsum = bf(BIG[5])[0:64, :]
        hsum2 = bf(BIG[6])[0:64, :]

        def hs(c, h):
            b = hsum if c < 8 else hsum2
            return b[:, (c % 8) * 1024 + h * 256:(c % 8) * 1024 + (h + 1) * 256]
        Rh = [RB[5], RB[6]]
        mset(bf(BIG[5])[0:64, 0:8192], 0.0, [RB[5], Rhc[0:8]])
        mset(bf(BIG[6])[0:64, 0:8192], 0.0, [RB[6], Rhc[8:16]])
        wk = bf(BIG[1])
        kp = wk[0:64, 0:1024].rearrange("p (h e) -> p h e", e=256)
        vaug = wk[0:64, 1024:1024 + 4 * 257].rearrange("p (h e) -> p h e", e=257)
        spT = wk[0:64, 2304:2304 + 256].rearrange("p (h t) -> p h t", t=64)
        cbf = wk[:, 2560:2560 + 4 * 514].rearrange("p (h k e) -> p h k e", k=2, e=257)
        mset(vaug[:, :, 256:257], 1.0, [R_va])

        for q in range(nseq):
            if g == 0:
                for dr in range(2):
                    for h in range(4):
                        dma("sp", Caug[:, dr * 4 + h, :, 0:256], C0[l, dr, h].rearrange("(k p) e -> p k e", p=128), [RB[7]], [RCs[dr * 4 + h]])
                        dma("sp", Caug[:, dr * 4 + h, :, 256:257], n0[l, dr, h].rearrange("(k p o) -> p k o", p=128, o=1), [RB[7]], [RCs[dr * 4 + h]], nonc=True)
            else:
                mset(Caug[:, :, :, :], 0.0, [RCs, RB[7]])
            for j in range(nch):
                for dr in range(2):
                    cc = j if dr == 0 else nch - 1 - j
                    c = q * nch + cc
                    tsl = slice(c * 64, (c + 1) * 64)
                    col0 = dr * 64 + c * 4
                    pk, rpk = psum()
                    pv_, rpv = psum()
                    pkb = pk[:].bitcast(BF16)
                    pvb = pv_[:].bitcast(BF16)
                    for h in range(4):
                        for kt2 in range(2):
                            tr(pkb[0:64, h * 256 + kt2 * 128:h * 256 + (kt2 + 1) * 128], kT[:, h * 2 + kt2, tsl], identb[:, :], [Rk, R_c], [rpk])
                            tr(pvb[0:64, h * 256 + kt2 * 128:h * 256 + (kt2 + 1) * 128], vT[:, h * 2 + kt2, tsl], identb[:, :], [Rv, R_c], [rpv])
                    for h in range(4):
                        act(kp[:, h, :], pkb[0:64, h * 256:(h + 1) * 256], AF.Copy, [rpk, rgm], [R_kp[h]], scale=KSC[0:64, col0 + h:col0 + h + 1])
                    cpy(vaug[:, :, 0:256], pvb[0:64, 0:1024].rearrange("p (h e) -> p h e", e=256), [rpv], [R_va])
                    pq, rpq = psum()
                    for h in range(4):
                        for kt2 in range(2):
                            mm(pq[0:64, h * 64:(h + 1) * 64], kT[:, h * 2 + kt2, tsl], qT[:, h * 2 + kt2, tsl], kt2 == 0, kt2 == 1, [Rk, Rq], [rpq])
                    msk = triU if dr == 0 else triL
                    for h in range(4):
                        stt(spT[:, h, :], pq[0:64, h * 64:(h + 1) * 64], ATOK[0:64, col0 + h:col0 + h + 1], msk[:, :], ALU.mult, ALU.mult, [rpq, rgm, R_c], [R_sp[h]])
                    for h in range(4):
                        s_ = dr * 4 + h
                        act(cbf[:, h, :, :], Caug[:, s_, :, :], AF.Copy, [RCs[s_], RB[7], rgm], [R_cb[h]], scale=CSC[:, col0 + h:col0 + h + 1])
                    for h in range(4):
                        s_ = dr * 4 + h
                        pp, rpp = psum()
                        mm(pp[0:64, 0:257], spT[:, h, :], vaug[:, h, :], True, False, [R_sp[h], R_va], [rpp])
                        mm(pp[0:64, 0:257], qT[:, h * 2, tsl], cbf[:, h, 0, :], False, False, [Rq, R_cb[h]], [rpp])
                        mm(pp[0:64, 0:257], qT[:, h * 2 + 1, tsl], cbf[:, h, 1, :], False, True, [Rq, R_cb[h]], [rpp])
                        rr = rsm_t[:, h * 2:h * 2 + 1]
                        act(rr, pp[0:64, 256:257], AF.Abs, [rpp], [R_rs[h]])
                        ts(rr, rr, THR[0:64, col0 + h:col0 + h + 1], None, ALU.max, None, [R_rs[h], rgm], [R_rs[h]])
                        S.add("dve", (lambda o: (lambda e: e.reciprocal(o, o)))(rr), [R_rs[h]], [R_rs[h]])
                        hv_ = hs(c, h)
                        stt(hv_, pp[0:64, 0:256], rr, hv_, ALU.mult, ALU.add, [rpp, R_rs[h], Rhc[c], Rh[c // 8]], [Rhc[c]])
                        for kt2 in range(2):
                            pc, rpc = psum()
                            mm(pc[:, 0:257], kp[:, h, kt2 * 128:(kt2 + 1) * 128], vaug[:, h, :], True, True, [R_kp[h], R_va], [rpc])
                            stt(Caug[:, s_, kt2, :], Caug[:, s_, kt2, :], DEC[:, col0 + h:col0 + h + 1], pc[:, 0:257], ALU.mult, ALU.add, [RCs[s_], RB[7], rgm, rpc], [RCs[s_]])
            if g == 1:
                for dr in range(2):
                    for h in range(4):
                        dma("sp", stC[q, l, dr, h].rearrange("(k p) e -> p k e", p=128), Caug[:, dr * 4 + h, :, 0:256], [RCs[dr * 4 + h], RB[7]], [Res()])
                        dma("sp", stn[q, l, dr, h].rearrange("(k p o) -> p k o", p=128, o=1), Caug[:, dr * 4 + h, :, 256:257], [RCs[dr * 4 + h], RB[7]], [Res()], nonc=True)

        hcT = bf(BIG[0])[:, 0:8 * T].rearrange("p (f t) -> p f t", t=T)
        mnw = prm[:, P_MNW:P_MNW + 8]
        for c in range(16):
            hb = hsum if c < 8 else hsum2
            hc_ = hb[:, (c % 8) * 1024:(c % 8 + 1) * 1024]
            sqc = SC[c % 2][0:64, :]
            rq = RS[c % 2]
            act(sqc, hc_, AF.Square, [Rhc[c], Rh[c // 8]], [rq])
            S.add("dve", (lambda o, a: (lambda e: e.tensor_reduce(o, a, AX.X, ALU.add)))(hn_s[:, c, :], sqc.rearrange("p (h e) -> p h e", e=256)), [rq], [R_hn])
            ts(hn_s[:, c, :], hn_s[:, c, :], 1.0 / DH, EPS, ALU.mult, ALU.add, [R_hn], [R_hn])
            act(hn_s[:, c, :], hn_s[:, c, :], AF.Sqrt, [R_hn], [R_hn])
            S.add("dve", (lambda o: (lambda e: e.reciprocal(o, o)))(hn_s[:, c, :]), [R_hn], [R_hn])
            tt(sqc.rearrange("p (h e) -> p h e", e=256), hc_.rearrange("p (h e) -> p h e", e=256),
               hn_s[:, c, :].unsqueeze(2).to_broadcast([64, 4, 256]), ALU.mult, [Rhc[c], Rh[c // 8], R_hn], [rq])
            p, rp = psum()
            for ft in range(8):
                tr(p[:, ft * 64:(ft + 1) * 64], sqc[:, ft * 128:(ft + 1) * 128], ident[0:64, 0:64], [rq, R_c], [rp])
            tt(hcT[:, :, c * 64:(c + 1) * 64], p[:, :].rearrange("p (f t) -> p f t", t=64),
               mnw.unsqueeze(2).to_broadcast([128, 8, 64]), ALU.mult, [rp, R_prm], [RB[0]])

    def ffn(l, g):
        xsrc = [(y[g * T + i * 128:g * T + (i + 1) * 128, :], [R_y[g][i]]) for i in range(NT)]
        hv, hr = [], []
        for kt in range(16):
            b = 0 if kt < 8 else 1
            hv.append(bf(BIG[b])[:, (kt % 8) * T:(kt % 8 + 1) * T])
            hr.append(RB[b])
        norm_pass(l, g, 1, xsrc, hv, hr, 7, RB[7], 6, RB[6])
        rhs = lambda kt, tb: (hv[kt][:, tb * 512:(tb + 1) * 512], [hr[kt]])

        def actv(j):
            b = 2 + j // 8
            return bf(BIG[b])[:, (j % 8) * T:(j % 8 + 1) * T], RB[b]
        fcw = prm[:, P_FCW:P_FCW + 132].rearrange("p (f k) -> p f k", k=3)
        u_sb, g_sb, gc, t1 = SC[0], SC[1], SC[2], SC[3]
        for qb in range(11):
            def ev_u(ft, tb, p, rp):
                if tb == 0:
                    cpy(u_sb[:, 0:512], p[:, :], [rp], [RS[0]])
                else:
                    act(u_sb[:, 512:1024], p[:, :], AF.Copy, [rp], [RS[0]])

            def ev_g(ft, tb, p, rp):
                if tb == 0:
                    act(g_sb[:, 0:512], p[:, :], AF.Copy, [rp], [RS[1]])
                else:
                    cpy(g_sb[:, 512:1024], p[:, :], [rp], [RS[1]])
            vu, ru = load_panel(w_ffn_up[l][:, qb * 512:(qb + 1) * 512], 16, 512)
            vg, rg_ = load_panel(w_ffn_up[l][:, DFF + qb * 512:DFF + (qb + 1) * 512], 16, 512)
            for ft in range(4):
                j = qb * 4 + ft
                for (vv, rv, evf) in ((vu, ru, ev_u), (vg, rg_, ev_g)):
                    for tb in range(2):
                        p, rp = psum()
                        for kt in range(16):
                            mm(p[:, :], vv[:, kt, ft * 128:(ft + 1) * 128], hv[kt][:, tb * 512:(tb + 1) * 512], kt == 0, kt == 15, [rv, hr[kt]], [rp])
                        evf(ft, tb, p, rp)
                ts(gc[:, :], g_sb[:, :], fcw[:, j, 1:2], prm[:, P_FCB + j:P_FCB + j + 1], ALU.mult, ALU.add, [RS[1], R_prm], [RS[2]])
                for k in (0, 2):
                    d = k - 1
                    dv, _ = shifted(gc[:, :], g, "col", d)
                    _, sv = shifted(g_sb[:, :], g, "col", d)
                    stt(dv, sv, fcw[:, j, k:k + 1], dv, ALU.mult, ALU.add, [RS[1], RS[2], R_prm], [RS[2]])
                act(t1[:, :], gc[:, :], AF.Square, [RS[2]], [RS[3]])
                ts(t1[:, :], t1[:, :], 0.044715, 1.0, ALU.mult, ALU.add, [RS[3]], [RS[3]])
                tt(t1[:, :], t1[:, :], gc[:, :], ALU.mult, [RS[3], RS[2]], [RS[3]])
                act(t1[:, :], t1[:, :], AF.Sigmoid, [RS[3]], [RS[3]], scale=1.5957691216057308)
                tt(gc[:, :], gc[:, :], u_sb[:, :], ALU.mult, [RS[2], RS[0]], [RS[2]])
                av, ar = actv(j)
                tt(av, gc[:, :], t1[:, :], ALU.mult, [RS[2], RS[3]], [ar])
        for db in range(4):
            stage = BIG[0][:, 0:4 * T].rearrange("p (f t) -> p f t", t=T)
            tstage = BIG[1][:, 0:8 * 512].rearrange("p (i n) -> p i n", n=512)
            for dj in range(4):
                dt_ = db * 4 + dj

                def ev(ft, tb, p, rp, dj=dj):
                    if tb == 0:
                        cpy(stage[:, dj, 0:512], p[:, :], [rp], [RB[0]])
                    else:
                        act(stage[:, dj, 512:1024], p[:, :], AF.Copy, [rp], [RB[0]])
                linear(w_ffn_down[l][:, dt_ * 128:(dt_ + 1) * 128], 44, 128, lambda kt, tb: (actv(kt)[0][:, tb * 512:(tb + 1) * 512], [actv(kt)[1]]), ev)
            out_block_to_fd(stage, RB[0], tstage, RB[1], db * 4)
        final_pass(l, g, 1, xsrc, l == depth - 1)

    for l in range(depth):
        load_params(l)
        mod_phase(l)
        for g in range(2):
            mixer(l, g)
            ffn(l, g)
    S.emit(nc, es)
    es.close()
    return nc


_CACHE = {}


def _consts():
    ident = np.eye(128, dtype=np.float32)
    s = np.arange(64)
    triU = (s[:, None] <= s[None, :]).astype(np.float32)
    triL = (s[:, None] >= s[None, :]).astype(np.float32)
    inv = np.zeros((2, 4, T), np.float32)
    for gi, L in enumerate((16, 256)):
        t = np.arange(L)
        for wi, w in enumerate(POOLW):
            lo = np.clip(t - w // 2, 0, L)
            hi = np.clip(t - w // 2 + w, 0, L)
            ic = (1.0 / (hi - lo)).astype(np.float32)
            if gi == 0:
                inv[gi, wi] = np.repeat(ic, 64)
            else:
                inv[gi, wi] = np.tile(ic, 4)
    return ident, triU, triL, inv


def kernel(x_prompt, x_sample, state_C, state_n, state_m, c, c_ctx, w_ada, b_ada, norm_mix_pre, norm_mix_post,
           norm_ffn_pre, norm_ffn_post, w_in, b_in, conv_a_w, conv_a_b, ln_a_w, ln_a_b, w_a_out, w_pool,
           pool_scale, mlstm_norm_w, w_c_out, w_out, w_ffn_up, ffn_conv_w, ffn_conv_b, w_ffn_down, _depth=None,
           _cores=8):
    f = lambda a: np.ascontiguousarray(np.asarray(a, dtype=np.float32))
    depth = _depth or w_ada.shape[0]
    if depth not in _CACHE:
        _CACHE[depth] = build(depth)
    nc = _CACHE[depth]
    ident, triU, triL, inv = _consts()
    shared = {"w_ada": f(w_ada)[:depth], "b_ada": f(b_ada)[:depth], "norm_mix_pre": f(norm_mix_pre)[:depth],
              "norm_mix_post": f(norm_mix_post)[:depth], "norm_ffn_pre": f(norm_ffn_pre)[:depth],
              "norm_ffn_post": f(norm_ffn_post)[:depth], "w_in": f(w_in)[:depth], "b_in": f(b_in)[:depth],
              "conv_a_w": f(conv_a_w)[:depth], "conv_a_b": f(conv_a_b)[:depth], "ln_a_w": f(ln_a_w)[:depth],
              "ln_a_b": f(ln_a_b)[:depth], "w_a_out": f(w_a_out)[:depth], "w_pool": f(w_pool)[:depth],
              "pool_scale": f(pool_scale)[:depth], "mlstm_norm_w": f(mlstm_norm_w)[:depth],
              "w_c_out": f(w_c_out)[:depth], "w_out": f(w_out)[:depth], "w_ffn_up": f(w_ffn_up)[:depth],
              "ffn_conv_w": f(ffn_conv_w)[:depth], "ffn_conv_b": f(ffn_conv_b)[:depth],
              "w_ffn_down": f(w_ffn_down)[:depth], "c_ident": ident, "c_triU": triU, "c_triL": triL,
              "c_invcnt": inv}
    xp = f(x_prompt)
    xsm = f(x_sample)
    sC, sn, sm = f(state_C), f(state_n), f(state_m)
    cc, cctx = f(c), f(c_ctx)
    in_maps = []
    for core in range(_cores):
        b = core % 4
        m = dict(shared)
        m["xs"] = np.ascontiguousarray(np.concatenate([xsm[b], xp[core * 4:(core + 1) * 4].reshape(T, D)], axis=0))
        m["cond"] = np.ascontiguousarray(np.stack([cc[b], cctx], axis=0))
        m["C0"] = np.ascontiguousarray(sC[b][:depth])
        m["n0"] = np.ascontiguousarray(sn[b][:depth])
        m["m0"] = np.ascontiguousarray(sm[b][:depth].reshape(depth, 8))
        in_maps.append(m)
    res = run_bass_kernel_spmd(nc, in_maps, core_ids=list(range(_cores)))
    R = res.results
    B = x_prompt.shape[0]
    yp = np.zeros((B, 256, D), np.float32)
    ys = np.zeros((4, T, D), np.float32)
    nC = np.zeros((B, depth, 2, NH, DH, DH), np.float32)
    nn = np.zeros((B, depth, 2, NH, DH), np.float32)
    nm = np.zeros((B, depth, 2, NH), np.float32)
    for core in range(_cores):
        r = R[core]
        yy = np.asarray(r["y"])
        if core < 4:
            ys[core] = yy[0:T]
        yp[core * 4:(core + 1) * 4] = yy[T:2 * T].reshape(4, 256, D)
        nC[core * 4:(core + 1) * 4] = np.asarray(r["stC"])
        nn[core * 4:(core + 1) * 4] = np.asarray(r["stn"])
        nm[core * 4:(core + 1) * 4] = np.asarray(r["stm"]).reshape(4, depth, 2, NH)
    return yp, ys, nC, nn, nm
```

```python
import numpy as np
from contextlib import ExitStack
import concourse.bass as bass
import concourse.mybir as mybir
from concourse.bass_utils import run_bass_kernel_spmd

F32 = mybir.dt.float32
BF16 = mybir.dt.bfloat16
AF = mybir.ActivationFunctionType
ALU = mybir.AluOpType
AX = mybir.AxisListType

D = 2048
T = 1024
NT = 8
WA = 512
WB = 512
WC = 1024
NH = 4
DH = 256
DFF = 5632
NIN = 11792
OFF_POOL = 1024
OFF_QKV = 1536
OFF_OG = 4608
OFF_GATES = 5632
OFF_MERGE = 5648
EPS = 1e-6
POOLW = (2, 4, 8, 16)
BIGN = 4112


def _flat(x):
    out = []
    for a in x:
        if a is None:
            continue
        if isinstance(a, (list, tuple)):
            out.extend(_flat(a))
        else:
            out.append(a)
    return out


SAME_ENG_WAIT = True
GORDER = (0, 1)


class Res:
    __slots__ = ("lw", "rd", "dr")

    def __init__(self):
        self.lw = None
        self.rd = {}
        self.dr = []


class Op:
    __slots__ = ("eng", "fn", "deps", "dma", "sig", "sigval", "semk")

    def __init__(self, eng, fn, deps, dma):
        self.eng = eng
        self.fn = fn
        self.deps = deps
        self.dma = dma
        self.sig = False
        self.sigval = 0
        self.semk = None


class Sched:
    ENGS = ("pe", "act", "dve", "pool", "sp")
    NDS = 12

    def __init__(self):
        self.ops = []

    def add(self, eng, fn, reads=(), writes=(), dma=False):
        oid = len(self.ops)
        deps = set()
        reads = _flat(reads)
        writes = _flat(writes)
        for r in reads:
            if r.lw is not None:
                deps.add(r.lw)
        for w in writes:
            if w.lw is not None:
                deps.add(w.lw)
            deps.update(w.rd.values())
            deps.update(w.dr)
        self.ops.append(Op(eng, fn, deps, dma))
        for r in reads:
            if dma:
                r.dr.append(oid)
            else:
                r.rd[eng] = oid
        for w in writes:
            w.lw = oid
            w.rd = {}
            w.dr = []
        return oid

    def emit(self, nc, es):
        ops = self.ops
        for o in ops:
            if o.dma:
                o.sig = True
            for d in o.deps:
                od = ops[d]
                if od.dma:
                    continue
                if od.eng == o.eng and not o.dma and (o.eng == "pe" or not SAME_ENG_WAIT):
                    continue
                od.sig = True
        cnt = {e: 0 for e in self.ENGS}
        dcnt = {}
        dn = {e: 0 for e in self.ENGS}
        for o in ops:
            if o.dma:
                k = (o.eng, dn[o.eng] % self.NDS)
                dn[o.eng] += 1
                dcnt[k] = dcnt.get(k, 0) + 1
                o.semk = k
                o.sigval = 16 * dcnt[k]
            elif o.sig:
                cnt[o.eng] += 1
                o.sigval = cnt[o.eng]
        sems = {}
        for e in self.ENGS:
            sems[e] = es.enter_context(nc.semaphore("s_" + e))
        for k in dcnt:
            sems[k] = es.enter_context(nc.semaphore("d_%s_%d" % k))
        block = es.enter_context(nc.Block())
        handles = {"pe": block.tensor, "act": block.scalar, "dve": block.vector, "pool": block.gpsimd,
                   "sp": block.sync}

        def run_engine(ename):
            def body(h):
                seen = {}
                for o in ops:
                    if o.eng != ename:
                        continue
                    need = {}
                    for d in o.deps:
                        od = ops[d]
                        if od.dma:
                            key = od.semk
                        else:
                            if od.eng == ename and not o.dma and (ename == "pe" or not SAME_ENG_WAIT):
                                continue
                            key = od.eng
                        if need.get(key, 0) < od.sigval:
                            need[key] = od.sigval
                    if o.dma and o.sigval > 16:
                        if need.get(o.semk, 0) < o.sigval - 16:
                            need[o.semk] = o.sigval - 16
                    for key, v in need.items():
                        if seen.get(key, 0) < v:
                            h.wait_ge(sems[key], v)
                            seen[key] = v
                    ins = o.fn(h)
                    if o.dma:
                        ins.then_inc(sems[o.semk], 16)
                    elif o.sig:
                        ins.then_inc(sems[ename], 1)
                for k, c in dcnt.items():
                    if k[0] == ename:
                        h.wait_ge(sems[k], 16 * c)
            return body

        for e in self.ENGS:
            handles[e](run_engine(e))


def build(depth=4):
    nc = bass.Bass("TRN2", target_bir_lowering=False)
    S = Sched()
    es = ExitStack()

    def din(name, shape):
        return nc.dram_tensor(name, list(shape), F32, kind="ExternalInput").ap()

    def dout(name, shape):
        return nc.dram_tensor(name, list(shape), F32, kind="ExternalOutput").ap()

    xs = din("xs", [2 * T, D])
    cond = din("cond", [2, D])
    C0 = din("C0", [depth, 2, NH, DH, DH])
    n0 = din("n0", [depth, 2, NH, DH])
    m0 = din("m0", [depth, 8])
    w_ada = din("w_ada", [depth, D, 6 * D])
    b_ada = din("b_ada", [depth, 6 * D])
    nrm = {k: din(k, [depth, D]) for k in ("norm_mix_pre", "norm_mix_post", "norm_ffn_pre", "norm_ffn_post")}
    w_in = din("w_in", [depth, D, NIN])
    b_in = din("b_in", [depth, NIN])
    conv_a_w = din("conv_a_w", [depth, 31, WA])
    conv_a_b = din("conv_a_b", [depth, WA])
    ln_a_w = din("ln_a_w", [depth, WA])
    ln_a_b = din("ln_a_b", [depth, WA])
    w_a_out = din("w_a_out", [depth, WA, D])
    w_pool = din("w_pool", [depth, 4, 128, 512])
    pool_scale = din("pool_scale", [depth, D])
    mlstm_norm_w = din("mlstm_norm_w", [depth, WC])
    w_c_out = din("w_c_out", [depth, WC, D])
    w_out = din("w_out", [depth, D, D])
    w_ffn_up = din("w_ffn_up", [depth, D, 2 * DFF])
    ffn_conv_w = din("ffn_conv_w", [depth, 3, DFF])
    ffn_conv_b = din("ffn_conv_b", [depth, DFF])
    w_ffn_down = din("w_ffn_down", [depth, DFF, D])
    c_ident = din("c_ident", [128, 128])
    c_triU = din("c_triU", [64, 64])
    c_triL = din("c_triL", [64, 64])
    c_invcnt = din("c_invcnt", [2, 4, T])

    y = dout("y", [2 * T, D])
    stC = dout("stC", [4, depth, 2, NH, DH, DH])
    stn = dout("stn", [4, depth, 2, NH, DH])
    stm = dout("stm", [4, depth, 8])
    mod_d = dout("mod_d", [depth, 2, 6 * D])
    hT_d = dout("hT_d", [2, 128, 4096])
    R_hTd = Res()
    f_d = dout("f_d", [T, D])
    R_y = [[Res() for _ in range(NT)] for _ in range(2)]
    R_f = [Res() for _ in range(NT)]
    R_modp = {(l_, g_, pn_): Res() for l_ in range(depth) for g_ in range(2) for pn_ in range(24)}
    R_st = Res()

    def sb(name, shape, dt=F32):
        return es.enter_context(nc.sbuf_tensor(name, list(shape), dt))

    BIG = [sb("big%d" % i, [128, BIGN]) for i in range(8)]
    RBh = [[Res(), Res()] for _ in range(8)]
    RB = [tuple(h) for h in RBh]
    WBUF = [sb("wb%d" % i, [128, 8192], BF16) for i in range(2)]
    RW = [Res() for _ in range(2)]
    SC = [sb("sc%d" % i, [128, T]) for i in range(4)]
    RS = [Res() for _ in range(4)]
    junk = sb("junk", [128, 2048], BF16)
    R_junk = Res()
    PS = [es.enter_context(nc.psum_tensor("ps%d" % i, [128, 512], F32)) for i in range(8)]
    RP = [Res() for _ in range(8)]
    psn = [0]

    def psum():
        i = psn[0] % 8
        psn[0] += 1
        return PS[i], RP[i]

    wbn = [0]

    def bf(t, n=None):
        a = t[:].bitcast(BF16)
        return a

    def mm(out, lhsT, rhs, start, stop, R, W):
        S.add("pe", lambda e: e.matmul(out, lhsT, rhs, start=start, stop=stop), R, W)

    def tr(out, in_, ident, R, W):
        S.add("pe", lambda e: e.transpose(out, in_, ident), R, W)

    def act(out, in_, func, R, W, bias=None, scale=None, accum=None):
        kw = {}
        if bias is not None:
            kw["bias"] = bias
        if scale is not None:
            kw["scale"] = scale
        if accum is not None:
            kw["accum_out"] = accum
        S.add("act", lambda e: e.activation(out, in_, func, **kw), R, W)

    def tt(out, a, b, op, R, W, eng="dve"):
        S.add(eng, lambda e: e.tensor_tensor(out, a, b, op), R, W)

    def ts(out, a, s1, s2, op0, op1, R, W, eng="dve"):
        if op1 is None:
            S.add(eng, lambda e: e.tensor_scalar(out, a, s1, None, op0), R, W)
        else:
            S.add(eng, lambda e: e.tensor_scalar(out, a, s1, s2, op0, op1), R, W)

    def stt(out, a, s, b, op0, op1, R, W):
        S.add("dve", lambda e: e.scalar_tensor_tensor(out, a, s, b, op0, op1), R, W)

    def cpy(out, in_, R, W, eng="dve"):
        S.add(eng, lambda e: e.tensor_copy(out, in_), R, W)

    def mset(ap, v, W, eng="dve"):
        S.add(eng, lambda e: e.memset(ap, v), (), W)

    def dma(q, out, in_, R, W, nonc=False):
        if nonc:
            S.add(q, lambda e: e.dma_start(out=out, in_=in_, allow_slow_non_contiguous=True), R, W, dma=True)
        else:
            S.add(q, lambda e: e.dma_start(out=out, in_=in_), R, W, dma=True)

    gm = sb("gm", [128, 14, 128]); rgm = Res()
    G = sb("G", [64, 256]); rG = Res()
    acol = sb("acol", [128, 1])
    mcur_all = sb("mcur", [128, 2, 4, 4])
    rsm_t = sb("rsm_t", [64, 8])
    hn_s = sb("hn_s", [64, 16, 4]); R_hn = Res()
    wpool_sb = sb("wpool", [128, 4, 512], BF16); r_wp = Res()
    small = sb("small", [128, 8, 8]); rsm = [Res() for _ in range(8)]
    R_kp = [Res() for _ in range(4)]; R_va = Res(); R_sp = [Res() for _ in range(4)]
    R_cb = [Res() for _ in range(4)]; R_rs = [Res() for _ in range(4)]
    RCs = [Res() for _ in range(8)]; Rhc = [Res() for _ in range(16)]
    ident = sb("ident", [128, 128])
    identb = sb("identb", [128, 128], BF16)
    ones = sb("ones", [128, 128])
    triU = sb("triU", [64, 64])
    triL = sb("triL", [64, 64])
    R_c = Res()
    dma("sp", ident[:], c_ident, (), [R_c])
    dma("sp", triU[:], c_triU, (), [R_c])
    dma("sp", triL[:], c_triL, (), [R_c])
    cpy(identb[:], ident[:], [R_c], [R_c])
    mset(ones[:], 1.0, [R_c])

    scT = sb("scT", [128, 16, 64], BF16)
    R_scT = Res()
    cnd = SC[0]
    mset(BIG[0][0:64, 0:D], 0.0, [RB[0]])
    dma("sp", BIG[0][0:1, 0:D], cond[0:1, :], (), [RB[0]])
    dma("sp", BIG[0][32:33, 0:D], cond[1:2, :], (), [RB[0]])
    act(BIG[0][0:64, 0:D], BIG[0][0:64, 0:D], AF.Silu, [RB[0]], [RB[0]])
    for kt in range(16):
        p, rp = psum()
        tr(p[:, 0:64], BIG[0][0:64, kt * 128:(kt + 1) * 128], ident[0:64, 0:64], [RB[0], R_c], [rp])
        cpy(scT[:, kt, :], p[:, 0:64], [rp], [R_scT])

    prm = sb("prm", [128, 640])
    R_prm = Res()
    P_BIN, P_CAW, P_CAB, P_LNW, P_LNB, P_PSC, P_MNW, P_FCW, P_FCB, P_BQ = 0, 92, 216, 220, 224, 228, 244, 252, 384, 428
    gbias = sb("gbias", [16, 1])

    def load_rows_T(dst_ap_fn, src2d, nrows, ncolt, stage, rstage):
        for c0 in range(0, ncolt, 8):
            nc_ = min(8, ncolt - c0)
            stage, rstage = SC[2 + (c0 // 8) % 2], RS[2 + (c0 // 8) % 2]
            dma("sp", stage[0:nrows, 0:nc_ * 128], src2d[:, c0 * 128:(c0 + nc_) * 128], (), [rstage])
            for c in range(nc_):
                p, rp = psum()
                tr(p[:, 0:nrows], stage[0:nrows, c * 128:(c + 1) * 128], ident[0:nrows, 0:nrows], [rstage, R_c], [rp])
                cpy(dst_ap_fn(c0 + c), p[:, 0:nrows], [rp], [R_prm])

    def load_params(l):
        st, rs = None, None
        load_rows_T(lambda c: prm[:, P_BIN:P_BIN + 44], b_in[l, 0:5632].rearrange("(r c) -> r c", c=128), 44, 1, st, rs)
        load_rows_T(lambda c: prm[:, P_BIN + 44:P_BIN + 92], b_in[l, OFF_MERGE:NIN].rearrange("(r c) -> r c", c=128), 48, 1, st, rs)
        dma("sp", gbias[:], b_in[l, OFF_GATES:OFF_MERGE].rearrange("(p o) -> p o", o=1), (), [R_prm])
        pv = prm[:, P_CAW:P_CAW + 124].rearrange("p (f k) -> p f k", k=31)
        load_rows_T(lambda c: pv[:, c, :], conv_a_w[l], 31, 4, st, rs)
        load_rows_T(lambda c: prm[:, P_CAB:P_CAB + 4], conv_a_b[l].rearrange("(r c) -> r c", c=128), 4, 1, st, rs)
        load_rows_T(lambda c: prm[:, P_LNW:P_LNW + 4], ln_a_w[l].rearrange("(r c) -> r c", c=128), 4, 1, st, rs)
        load_rows_T(lambda c: prm[:, P_LNB:P_LNB + 4], ln_a_b[l].rearrange("(r c) -> r c", c=128), 4, 1, st, rs)
        load_rows_T(lambda c: prm[:, P_PSC:P_PSC + 16], pool_scale[l].rearrange("(r c) -> r c", c=128), 16, 1, st, rs)
        load_rows_T(lambda c: prm[:, P_MNW:P_MNW + 8], mlstm_norm_w[l].rearrange("(r c) -> r c", c=128), 8, 1, st, rs)
        fv = prm[:, P_FCW:P_FCW + 132].rearrange("p (f k) -> p f k", k=3)
        load_rows_T(lambda c: fv[:, c, :], ffn_conv_w[l][:, 0:2816], 3, 22, st, rs)
        load_rows_T(lambda c: fv[:, 22 + c, :], ffn_conv_w[l][:, 2816:5632], 3, 22, st, rs)
        load_rows_T(lambda c: prm[:, P_FCB:P_FCB + 44], ffn_conv_b[l].rearrange("(r c) -> r c", c=128), 44, 1, st, rs)
        ts(prm[:, P_BQ:P_BQ + 8], prm[:, P_BIN + 12:P_BIN + 20], 1.0 / 16.0, None, ALU.mult, None, [R_prm], [R_prm])

    def load_panel(src, KT, ncols):
        i = wbn[0] % 2
        wbn[0] += 1
        v = WBUF[i][:, 0:KT * ncols].rearrange("p (k n) -> p k n", n=ncols)
        dma("pool", v, src.rearrange("(k p) n -> p k n", p=128), (), [RW[i]])
        return v, RW[i]

    def linear(src, KT, ncols, rhs_fn, evac_fn, ft0=0):
        v, rw = load_panel(src, KT, ncols)
        for ft in range(ncols // 128):
            for tb in range(2):
                p, rp = psum()
                for kt in range(KT):
                    ra, rr = rhs_fn(kt, tb)
                    mm(p[:, :], v[:, kt, ft * 128:(ft + 1) * 128], ra, kt == 0, kt == KT - 1, [rw] + rr, [rp])
                evac_fn(ft0 + ft, tb, p, rp)

    def mod_phase(l):
        for pn in range(24):
            stg, rsg = SC[(pn % 2) * 2], RS[(pn % 2) * 2]
            bst, rbs = SC[(pn % 2) * 2 + 1], RS[(pn % 2) * 2 + 1]
            v, rw = load_panel(w_ada[l, :, pn * 512:(pn + 1) * 512], 16, 512)
            p, rp = psum()
            for kt in range(16):
                mm(p[0:64, :], scT[:, kt, :], v[:, kt, :], kt == 0, kt == 15, [rw, R_scT], [rp])
            dma("sp", bst[0:1, 0:512], b_ada[l, pn * 512:(pn + 1) * 512].rearrange("(o n) -> o n", o=1), (), [rbs])
            dma("sp", bst[32:33, 0:512], b_ada[l, pn * 512:(pn + 1) * 512].rearrange("(o n) -> o n", o=1), (), [rbs])
            tt(stg[0:1, 0:512], p[0:1, :], bst[0:1, 0:512], ALU.add, [rp, rbs], [rsg])
            tt(stg[32:33, 0:512], p[32:33, :], bst[32:33, 0:512], ALU.add, [rp, rbs], [rsg])
            dma("sp", mod_d[l, 0:1, pn * 512:(pn + 1) * 512], stg[0:1, 0:512], [rsg], [R_modp[(l, 0, pn)]])
            dma("sp", mod_d[l, 1:2, pn * 512:(pn + 1) * 512], stg[32:33, 0:512], [rsg], [R_modp[(l, 1, pn)]])

    def bcast_load(dst, rdst, vec, modsel=None):
        rr_ = [] if modsel is None else [R_modp[(modsel[0], modsel[1], modsel[2] * 4 + k_)] for k_ in range(4)]
        dma("sp", dst, vec.partition_broadcast(128), rr_, [rdst])

    def hT_views(b0, b1):
        hv, hr = [], []
        for kt in range(16):
            b = b0 if kt < 8 else b1
            hv.append(bf(BIG[b])[:, (kt % 8) * T:(kt % 8 + 1) * T])
            hr.append(RB[b])
        return hv, hr

    def fused_pass(g, fin, nrmsel):
        gam = BIG[7][:, 0:D]
        shf = BIG[7][:, D:2 * D]
        pg = BIG[2][:, 0:D]
        tmpb = BIG[2][:, D:2 * D]
        rb = RB[7]
        rpg = RB[2]
        if fin is not None:
            lf, wf = fin
            gi = 2 if wf == 0 else 5
            nwf = nrm["norm_mix_post" if wf == 0 else "norm_ffn_post"]
            bcast_load(pg, rpg, mod_d[lf, g, gi * D:(gi + 1) * D], (lf, g, gi))
            bcast_load(tmpb, rpg, nwf[lf])
            tt(pg, pg, tmpb, ALU.mult, [rpg], [rpg])
            if lf == 0 and wf == 0:
                xsrc = [(xs[g * T + i * 128:g * T + (i + 1) * 128, :], []) for i in range(NT)]
            else:
                xsrc = [(y[g * T + i * 128:g * T + (i + 1) * 128, :], [R_y[g][i]]) for i in range(NT)]
        else:
            xsrc = [(xs[g * T + i * 128:g * T + (i + 1) * 128, :], []) for i in range(NT)]
        if nrmsel is not None:
            ln_, wn = nrmsel
            si, ci = (0, 1) if wn == 0 else (3, 4)
            nw = nrm["norm_mix_pre" if wn == 0 else "norm_ffn_pre"]
            bcast_load(gam, rb, mod_d[ln_, g, ci * D:(ci + 1) * D], (ln_, g, ci))
            bcast_load(shf, rb, nw[ln_])
            stt(gam, gam, 1.0, shf, ALU.add, ALU.mult, [rb], [rb])
            bcast_load(shf, rb, mod_d[ln_, g, si * D:(si + 1) * D], (ln_, g, si))
            hviews, hres = hT_views(0, 1)
        xbuf = [(BIG[6][:, 0:D], RBh[6][0]), (BIG[6][:, D:2 * D], RBh[6][1]), (BIG[5][:, 0:D], RBh[5][0]), (BIG[5][:, D:2 * D], RBh[5][1])]
        fbuf = [(BIG[4][:, 0:D], RBh[4][0]), (BIG[4][:, D:2 * D], RBh[4][1]), (BIG[3][:, 0:D], RBh[3][0]), (BIG[3][:, D:2 * D], RBh[3][1])]
        for i in range(NT):
            x_t, rx = xbuf[i % 4]
            f_t, rf = fbuf[i % 4]
            ri = rsm[i]
            dma("sp", x_t, xsrc[i][0], xsrc[i][1], [rx])
            if fin is not None:
                dma("sp", f_t, f_d[i * 128:(i + 1) * 128, :], [R_f[i]], [rf])
                act(junk[:, :], f_t, AF.Square, [rf], [ri], accum=small[:, i, 0:1])
                act(small[:, i, 1:2], small[:, i, 0:1], AF.Sqrt, [ri], [ri], scale=1.0 / D, bias=EPS)
                S.add("dve", (lambda o, a: (lambda e: e.reciprocal(o, a)))(small[:, i, 2:3], small[:, i, 1:2]), [ri], [ri])
                stt(f_t, f_t, small[:, i, 2:3], pg, ALU.mult, ALU.mult, [rf, ri, rpg], [rf])
                tt(f_t, f_t, x_t, ALU.add, [rf, rx], [rf])
                dma("sp", y[g * T + i * 128:g * T + (i + 1) * 128, :], f_t, [rf], [R_y[g][i]])
                cur, rcur = f_t, rf
            else:
                cur, rcur = x_t, rx
            if nrmsel is not None:
                act(junk[:, :], cur, AF.Square, [rcur], [ri], accum=small[:, i, 4:5])
                act(small[:, i, 5:6], small[:, i, 4:5], AF.Sqrt, [ri], [ri], scale=1.0 / D, bias=EPS)
                S.add("dve", (lambda o, a: (lambda e: e.reciprocal(o, a)))(small[:, i, 6:7], small[:, i, 5:6]), [ri], [ri])
                stt(x_t, cur, small[:, i, 6:7], gam, ALU.mult, ALU.mult, [rcur, ri, rb], [rx])
                tt(x_t, x_t, shf, ALU.add, [rx, rb], [rx])
                for q4 in range(4):
                    p, rp = psum()
                    for j in range(4):
                        kt = q4 * 4 + j
                        tr(p[:, j * 128:(j + 1) * 128], x_t[:, kt * 128:(kt + 1) * 128], ident[:, :], [rx, R_c], [rp])
                    for j in range(4):
                        kt = q4 * 4 + j
                        if j % 2 == 0:
                            cpy(hviews[kt][:, i * 128:(i + 1) * 128], p[:, j * 128:(j + 1) * 128], [rp], [hres[kt]])
                        else:
                            act(hviews[kt][:, i * 128:(i + 1) * 128], p[:, j * 128:(j + 1) * 128], AF.Copy, [rp], [hres[kt]])
        if nrmsel is not None and nrmsel[1] == 0:
            dma("sp", hT_d[0], BIG[0][:, 0:4096], [RB[0]], [R_hTd])
            dma("sp", hT_d[1], BIG[1][:, 0:4096], [RB[1]], [R_hTd])

    def out_block_to_fd(stage, rstage, tstage, rtstage, d0):
        sv = stage
        tv = tstage
        for i in range(NT):
            p, rp = psum()
            for j in range(4):
                tr(p[:, j * 128:(j + 1) * 128], sv[:, j, i * 128:(i + 1) * 128], ident[:, :], [rstage, R_c], [rp])
            if i % 2 == 0:
                cpy(tv[:, i, :], p[:, :], [rp], [rtstage])
            else:
                act(tv[:, i, :], p[:, :], AF.Copy, [rp], [rtstage])
            dma("sp", f_d[i * 128:(i + 1) * 128, d0 * 128:d0 * 128 + 512], tv[:, i, :], [rtstage], [R_f[i]])

    def shifted(a, g, kind, d):
        if g == 0:
            a3 = a.rearrange("p (r c) -> p r c", c=64)
            if kind == "row":
                lo, hi = max(0, -d), 64 - max(0, d)
                return a3[:, :, lo:hi], a3[:, :, lo + d:hi + d]
            lo, hi = max(0, -d), 16 - max(0, d)
            return a3[:, lo:hi, :], a3[:, lo + d:hi + d, :]
        a3 = a.rearrange("p (r c) -> p r c", c=256)
        lo, hi = max(0, -d), 256 - max(0, d)
        return a3[:, :, lo:hi], a3[:, :, lo + d:hi + d]

    def mixer(l, g):
        wl = w_in[l]
        hv, hr = hT_views(0, 1)

        def rhs_h(hv, hr):
            return lambda kt, tb: (hv[kt][:, tb * 512:(tb + 1) * 512], [hr[kt]])

        qkvT = [bf(BIG[2 + j])[:, 0:8 * T].rearrange("p (f t) -> p f t", t=T) for j in range(3)]
        for j in range(3):
            for pn in range(2):
                c0 = OFF_QKV + j * WC + pn * 512

                def ev(ft, tb, p, rp, j=j):
                    bt = OFF_QKV // 128 + j * 8 + ft
                    dst = qkvT[j][:, ft, tb * 512:(tb + 1) * 512]
                    if j == 0:
                        act(dst, p[:, :], AF.Identity, [rp, R_prm], [RB[2]], bias=prm[:, P_BQ + ft:P_BQ + ft + 1], scale=1.0 / 16.0)
                    elif (ft + tb) % 2 == 0:
                        act(dst, p[:, :], AF.Identity, [rp, R_prm], [RB[2 + j]], bias=prm[:, P_BIN + bt:P_BIN + bt + 1])
                    else:
                        ts(dst, p[:, :], prm[:, P_BIN + bt:P_BIN + bt + 1], None, ALU.add, None, [rp, R_prm], [RB[2 + j]])
                linear(wl[:, c0:c0 + 512], 16, 512, rhs_h(hv, hr), ev, ft0=pn * 4)
        gT = SC[3]
        i = wbn[0] % 2
        wbn[0] += 1
        gv = WBUF[i][:, 0:256].rearrange("p (k n) -> p k n", n=16)
        dma("pool", gv, wl[:, OFF_GATES:OFF_MERGE].rearrange("(k p) n -> p k n", p=128), (), [RW[i]], nonc=True)
        for tb in range(2):
            p, rp = psum()
            for kt in range(16):
                mm(p[0:16, :], gv[:, kt, :], hv[kt][:, tb * 512:(tb + 1) * 512], kt == 0, kt == 15, [RW[i], hr[kt]], [rp])
            act(gT[0:16, tb * 512:(tb + 1) * 512], p[0:16, :], AF.Identity, [rp, R_prm], [RS[3]], bias=gbias[:, 0:1])

        mlstm(l, g, qkvT, gT)

        hv, hr = hT_views(1, 2)
        dma("sp", BIG[1][:, 0:4096], hT_d[0], [R_hTd], [RB[1], R_kp, R_va, R_sp, R_cb])
        dma("sp", BIG[2][:, 0:4096], hT_d[1], [R_hTd], [RB[2]])
        hcT = bf(BIG[0])[:, 0:8 * T].rearrange("p (f t) -> p f t", t=T)

        for pn in range(2):
            c0 = OFF_OG + pn * 512

            def ev(ft, tb, p, rp):
                bt = OFF_OG // 128 + ft
                tmp = SC[(ft + tb) % 2][:, 0:512]
                rt = RS[(ft + tb) % 2]
                act(tmp, p[:, :], AF.Sigmoid, [rp, R_prm], [rt], bias=prm[:, P_BIN + bt:P_BIN + bt + 1])
                tt(hcT[:, ft, tb * 512:(tb + 1) * 512], hcT[:, ft, tb * 512:(tb + 1) * 512], tmp, ALU.mult, [rt, RB[0]], [RB[0]])
            linear(wl[:, c0:c0 + 512], 16, 512, rhs_h(hv, hr), ev, ft0=pn * 4)

        ga = BIG[3][:, 0:4 * T].rearrange("p (f t) -> p f t", t=T)
        gb = BIG[4][:, 0:4 * T].rearrange("p (f t) -> p f t", t=T)
        acc = BIG[5][:, 0:4 * T].rearrange("p (f t) -> p f t", t=T)
        sq = BIG[6][:, 0:4 * T].rearrange("p (f t) -> p f t", t=T)
        aT = bf(BIG[7])[:, 0:4 * T].rearrange("p (f t) -> p f t", t=T)
        pT = bf(BIG[7])[:, 4 * T:8 * T].rearrange("p (f t) -> p f t", t=T)
        for pn in range(2):
            def ev(ft, tb, p, rp):
                if ft < 4:
                    act(ga[:, ft, tb * 512:(tb + 1) * 512], p[:, :], AF.Identity, [rp, R_prm], [RB[3]], bias=prm[:, P_BIN + ft:P_BIN + ft + 1])
                else:
                    act(gb[:, ft - 4, tb * 512:(tb + 1) * 512], p[:, :], AF.Sigmoid, [rp, R_prm], [RB[4]], bias=prm[:, P_BIN + ft:P_BIN + ft + 1])
            linear(wl[:, pn * 512:(pn + 1) * 512], 16, 512, rhs_h(hv, hr), ev, ft0=pn * 4)
        caw = prm[:, P_CAW:P_CAW + 124].rearrange("p (f k) -> p f k", k=31)
        for ft in range(4):
            tt(ga[:, ft, :], ga[:, ft, :], gb[:, ft, :], ALU.mult, [RB[3], RB[4]], [RB[3]])
            ts(acc[:, ft, :], ga[:, ft, :], caw[:, ft, 15:16], prm[:, P_CAB + ft:P_CAB + ft + 1], ALU.mult, ALU.add, [RB[3], R_prm], [RB[5]])
            for k in range(31):
                d = k - 15
                if d == 0:
                    continue
                dv, _ = shifted(acc[:, ft, :], g, "row", d)
                _, sv = shifted(ga[:, ft, :], g, "row", d)
                stt(dv, sv, caw[:, ft, k:k + 1], dv, ALU.mult, ALU.add, [RB[3], RB[5], R_prm], [RB[5]])
            act(sq[:, ft, :], acc[:, ft, :], AF.Square, [RB[5]], [RB[6]])
        for tb in range(2):
            p1, r1 = psum()
            p2, r2 = psum()
            for ft in range(4):
                mm(p1[:, :], ones[:, :], acc[:, ft, tb * 512:(tb + 1) * 512], ft == 0, ft == 3, [R_c, RB[5]], [r1])
            for ft in range(4):
                mm(p2[:, :], ones[:, :], sq[:, ft, tb * 512:(tb + 1) * 512], ft == 0, ft == 3, [R_c, RB[6]], [r2])
            mean = SC[0][:, 0:512]
            var = SC[1][:, 0:512]
            act(mean, p1[:, :], AF.Copy, [r1], [RS[0]], scale=1.0 / WA)
            tt(var, mean, mean, ALU.mult, [RS[0]], [RS[1]])
            stt(var, p2[:, :], 1.0 / WA, var, ALU.mult, ALU.subtract, [r2, RS[1]], [RS[1]])
            ts(var, var, EPS, None, ALU.add, None, [RS[1]], [RS[1]])
            act(var, var, AF.Sqrt, [RS[1]], [RS[1]])
            S.add("dve", (lambda o: (lambda e: e.reciprocal(o, o)))(var), [RS[1]], [RS[1]])
            for ft in range(4):
                tmp = SC[2][:, 0:512]
                tt(tmp, acc[:, ft, tb * 512:(tb + 1) * 512], mean, ALU.subtract, [RB[5], RS[0]], [RS[2]])
                tt(tmp, tmp, var, ALU.mult, [RS[2], RS[1]], [RS[2]])
                act(aT[:, ft, tb * 512:(tb + 1) * 512], tmp, AF.Silu, [RS[2], R_prm], [RB[7]],
                    bias=prm[:, P_LNB + ft:P_LNB + ft + 1], scale=prm[:, P_LNW + ft:P_LNW + ft + 1])

        zp = BIG[3][:, 0:4 * T].rearrange("p (f t) -> p f t", t=T)
        pacc = BIG[4][:, 0:4 * T].rearrange("p (f t) -> p f t", t=T)

        def ev(ft, tb, p, rp):
            bt = OFF_POOL // 128 + ft
            act(zp[:, ft, tb * 512:(tb + 1) * 512], p[:, :], AF.Identity, [rp, R_prm], [RB[3]], bias=prm[:, P_BIN + bt:P_BIN + bt + 1])
        linear(wl[:, OFF_POOL:OFF_POOL + 512], 16, 512, rhs_h(hv, hr), ev)
        for ft in range(4):
            w = POOLW[ft]
            cpy(pacc[:, ft, :], zp[:, ft, :], [RB[3]], [RB[4]])
            for d in range(-(w // 2), w // 2):
                if d == 0:
                    continue
                dv, _ = shifted(pacc[:, ft, :], g, "col", d)
                _, sv = shifted(zp[:, ft, :], g, "col", d)
                tt(dv, dv, sv, ALU.add, [RB[3], RB[4]], [RB[4]])
            ic = SC[0]
            dma("sp", ic[:, :], c_invcnt[g, ft].partition_broadcast(128), (), [RS[0]])
            tt(pacc[:, ft, :], pacc[:, ft, :], ic[:, :], ALU.mult, [RB[4], RS[0]], [RB[4]])
            tt(pT[:, ft, :], pacc[:, ft, :], zp[:, ft, :], ALU.subtract, [RB[4], RB[3]], [RB[7]])

        def mixv(dt_):
            b = 3 if dt_ < 8 else 4
            return bf(BIG[b])[:, (dt_ % 8) * T:(dt_ % 8 + 1) * T], RB[b]
        dma("pool", wpool_sb[:, :, :], w_pool[l].rearrange("g c d -> c g d"), (), [r_wp])
        for db in range(4):
            wa_v, wa_r = load_panel(w_a_out[l][:, db * 512:(db + 1) * 512], 4, 512)
            wc_v, wc_r = load_panel(w_c_out[l][:, db * 512:(db + 1) * 512], 8, 512)
            ya = BIG[5][:, 0:4 * T].rearrange("p (f t) -> p f t", t=T)
            yc = BIG[6][:, 0:4 * T].rearrange("p (f t) -> p f t", t=T)
            for dj in range(4):
                for tb in range(2):
                    p, rp = psum()
                    for kt in range(4):
                        mm(p[:, :], wa_v[:, kt, dj * 128:(dj + 1) * 128], aT[:, kt, tb * 512:(tb + 1) * 512], kt == 0, kt == 3, [wa_r, RB[7]], [rp])
                    act(ya[:, dj, tb * 512:(tb + 1) * 512], p[:, :], AF.Copy, [rp], [RB[5]])
                    p, rp = psum()
                    for kt in range(8):
                        mm(p[:, :], wc_v[:, kt, dj * 128:(dj + 1) * 128], hcT[:, kt, tb * 512:(tb + 1) * 512], kt == 0, kt == 7, [wc_r, RB[0]], [rp])
                    cpy(yc[:, dj, tb * 512:(tb + 1) * 512], p[:, :], [rp], [RB[6]])
            for br in range(3):
                c0 = OFF_MERGE + br * D + db * 512

                def ev(ft, tb, p, rp, br=br, db=db):
                    dt_ = db * 4 + ft
                    bt = 44 + br * 16 + dt_
                    gt = SC[(ft + tb) % 2][:, 0:512]
                    rg = RS[(ft + tb) % 2]
                    act(gt, p[:, :], AF.Sigmoid, [rp, R_prm], [rg], bias=prm[:, P_BIN + bt:P_BIN + bt + 1])
                    mv, mr = mixv(dt_)
                    msl = mv[:, tb * 512:(tb + 1) * 512]
                    if br == 0:
                        tt(ya[:, ft, tb * 512:(tb + 1) * 512], ya[:, ft, tb * 512:(tb + 1) * 512], gt, ALU.mult, [rg, RB[5]], [RB[5]])
                    elif br == 1:
                        gidx = dt_ // 4
                        p2, rp2 = psum()
                        mm(p2[:, :], wpool_sb[:, gidx, (dt_ % 4) * 128:(dt_ % 4 + 1) * 128], pT[:, gidx, tb * 512:(tb + 1) * 512], True, True, [r_wp, RB[7]], [rp2])
                        tmp = SC[2][:, 0:512]
                        stt(tmp, p2[:, :], prm[:, P_PSC + dt_:P_PSC + dt_ + 1], gt, ALU.mult, ALU.mult, [rp2, rg, R_prm], [RS[2]])
                        tt(ya[:, ft, tb * 512:(tb + 1) * 512], ya[:, ft, tb * 512:(tb + 1) * 512], tmp, ALU.add, [RS[2], RB[5]], [RB[5]])
                    else:
                        tt(yc[:, ft, tb * 512:(tb + 1) * 512], yc[:, ft, tb * 512:(tb + 1) * 512], gt, ALU.mult, [rg, RB[6]], [RB[6]])
                        tt(msl, ya[:, ft, tb * 512:(tb + 1) * 512], yc[:, ft, tb * 512:(tb + 1) * 512], ALU.add, [RB[5], RB[6]], [mr])
                linear(wl[:, c0:c0 + 512], 16, 512, rhs_h(hv, hr), ev)

        for db in range(4):
            stage = BIG[5][:, 0:4 * T].rearrange("p (f t) -> p f t", t=T)
            tstage = BIG[6][:, 0:8 * 512].rearrange("p (i n) -> p i n", n=512)

            def ev(ft, tb, p, rp):
                if (ft + tb) % 2 == 0:
                    cpy(stage[:, ft, tb * 512:(tb + 1) * 512], p[:, :], [rp], [RB[5]])
                else:
                    act(stage[:, ft, tb * 512:(tb + 1) * 512], p[:, :], AF.Copy, [rp], [RB[5]])
            linear(w_out[l][:, db * 512:(db + 1) * 512], 16, 512, lambda kt, tb: (mixv(kt)[0][:, tb * 512:(tb + 1) * 512], [mixv(kt)[1]]), ev)
            out_block_to_fd(stage, RB[5], tstage, RB[6], db * 4)

    def mlstm(l, g, qkvT, gT):
        qT, kT, vT = qkvT
        Rq, Rk, Rv = RB[2], RB[3], RB[4]
        nseq, nch = (1, 16) if g == 0 else (4, 4)
        LF, LI, BC_, UU, ABC, BL, MALL, MPREV, DEC, CSC, KFAC, ATOK, KSC, THR = [gm[:, i, :] for i in range(14)]
        p, rp = psum()
        for c in range(16):
            tr(p[0:64, c * 16:(c + 1) * 16], gT[0:16, c * 64:(c + 1) * 64], ident[0:16, 0:16], [RS[3], R_c], [rp])
        cpy(G[:, :], p[0:64, 0:256], [rp], [rG])
        G5 = G[:, :].rearrange("p (c d k h) -> p c d k h", d=2, k=2, h=4)
        for dr in range(2):
            lfv = LF[0:64, dr * 64:(dr + 1) * 64].rearrange("p (c h) -> p c h", h=4)
            liv = LI[0:64, dr * 64:(dr + 1) * 64].rearrange("p (c h) -> p c h", h=4)
            act(lfv, G5[:, :, dr, 1, :], AF.Exp, [rG], [rgm], scale=-1.0)
            cpy(liv, G5[:, :, dr, 0, :], [rG], [rgm])
        act(LF[0:64, :], LF[0:64, :], AF.Ln, [rgm], [rgm], bias=1.0)
        ts(LF[0:64, :], LF[0:64, :], -1.0, None, ALU.mult, None, [rgm], [rgm])
        p, rp = psum()
        mm(p[0:64, 0:64], triU[:, :], LF[0:64, 0:64], True, True, [R_c, rgm], [rp])
        mm(p[0:64, 64:128], triL[:, :], LF[0:64, 64:128], True, True, [R_c, rgm], [rp])
        cpy(BC_[0:64, :], p[0:64, 0:128], [rp], [rgm])
        tt(UU[0:64, :], LI[0:64, :], BC_[0:64, :], ALU.subtract, [rgm], [rgm])
        p, rp = psum()
        tr(p[:, 0:64], UU[0:64, :], ident[0:64, 0:64], [rgm, R_c], [rp])
        S.add("dve", (lambda o, a: (lambda e: e.tensor_reduce(o, a, AX.X, ALU.max)))(acol[:, :], p[:, 0:64]), [rp], [rgm])
        diag = SC[0][:, 0:128]
        ts(diag, ident[:, :], acol[:, 0:1], None, ALU.mult, None, [rgm, R_c], [RS[0]])
        p2, rp2 = psum()
        mm(p2[:, 0:128], ones[:, :], diag, True, True, [R_c, RS[0]], [rp2])
        cpy(ABC, p2[:, 0:128], [rp2], [rgm])
        p3, rp3 = psum()
        mm(p3[:, 0:128], ones[0:64, :], LF[0:64, :], True, True, [R_c, rgm], [rp3])
        cpy(BL, p3[:, 0:128], [rp3], [rgm])
        mcur = mcur_all[:, :, 0:nseq, :]
        if g == 0:
            dma("sp", mcur[:, :, 0, :], m0[l].rearrange("(d h) -> d h", h=4).partition_broadcast(128), (), [rgm])
        else:
            mset(mcur[:, :, :, :], 0.0, [rgm])

        def colv(arr, dr, cc):
            return arr[:, dr * 64:(dr + 1) * 64].rearrange("p (q c h) -> p q c h", q=nseq, h=4)[:, :, cc, :]
        for j in range(nch):
            for dr in range(2):
                cc = j if dr == 0 else nch - 1 - j
                cpy(colv(MPREV, dr, cc), mcur[:, dr, :, :], [rgm], [rgm])
                tt(colv(MALL, dr, cc), mcur[:, dr, :, :], colv(ABC, dr, cc), ALU.max, [rgm], [rgm])
                tt(mcur[:, dr, :, :], colv(MALL, dr, cc), colv(BL, dr, cc), ALU.add, [rgm], [rgm])
                if g == 1 and j == nch - 1:
                    pass
        if g == 1:
            for q in range(4):
                dma("sp", stm[q, l].rearrange("(o d h) -> o d h", o=1, h=4), mcur[0:1, :, q, :], [rgm], [Res()])
        tt(DEC, MPREV, MALL, ALU.subtract, [rgm], [rgm])
        act(DEC, DEC, AF.Exp, [rgm], [rgm])
        tt(CSC, MPREV, ABC, ALU.subtract, [rgm], [rgm])
        act(CSC, CSC, AF.Exp, [rgm], [rgm])
        tt(KFAC, ABC, MALL, ALU.subtract, [rgm], [rgm])
        act(KFAC, KFAC, AF.Exp, [rgm], [rgm])
        tt(ATOK[0:64, :], UU[0:64, :], ABC[0:64, :], ALU.subtract, [rgm], [rgm])
        act(ATOK[0:64, :], ATOK[0:64, :], AF.Exp, [rgm], [rgm])
        tt(KSC[0:64, :], ATOK[0:64, :], KFAC[0:64, :], ALU.mult, [rgm], [rgm])
        tt(THR[0:64, :], BC_[0:64, :], ABC[0:64, :], ALU.add, [rgm], [rgm])
        act(THR[0:64, :], THR[0:64, :], AF.Exp, [rgm], [rgm], scale=-1.0)

        Caug = BIG[7][:, 0:8 * 2 * 257].rearrange("p (s k e) -> p s k e", k=2, e=257)
        RC = RB[7]
        hsum = bf(BIG[5])[0:64, :]
        hsum2 = bf(BIG[6])[0:64, :]

        def hs(c, h):
            b = hsum if c < 8 else hsum2
            return b[:, (c % 8) * 1024 + h * 256:(c % 8) * 1024 + (h + 1) * 256]
        Rh = [RB[5], RB[6]]
        mset(bf(BIG[5])[0:64, 0:8192], 0.0, [RB[5], Rhc[0:8]])
        mset(bf(BIG[6])[0:64, 0:8192], 0.0, [RB[6], Rhc[8:16]])
        wk = bf(BIG[1])
        kp = wk[0:64, 0:1024].rearrange("p (h e) -> p h e", e=256)
        vaug = wk[0:64, 1024:1024 + 4 * 257].rearrange("p (h e) -> p h e", e=257)
        spT = wk[0:64, 2304:2304 + 256].rearrange("p (h t) -> p h t", t=64)
        cbf = wk[:, 2560:2560 + 4 * 514].rearrange("p (h k e) -> p h k e", k=2, e=257)
        mset(wk[:, 0:4624], 0.0, [RB[1], R_kp, R_va, R_sp, R_cb])
        mset(vaug[:, :, 256:257], 1.0, [R_va])

        for q in range(nseq):
            if g == 0:
                for dr in range(2):
                    for h in range(4):
                        dma("sp", Caug[:, dr * 4 + h, :, 0:256], C0[l, dr, h].rearrange("(k p) e -> p k e", p=128), (), [RCs[dr * 4 + h], RB[7]])
                        dma("sp", Caug[:, dr * 4 + h, :, 256:257], n0[l, dr, h].rearrange("(k p o) -> p k o", p=128, o=1), (), [RCs[dr * 4 + h], RB[7]], nonc=True)
            else:
                mset(Caug[:, :, :, :], 0.0, [RCs, RB[7]])
            for j in range(nch):
                for dr in range(2):
                    cc = j if dr == 0 else nch - 1 - j
                    c = q * nch + cc
                    tsl = slice(c * 64, (c + 1) * 64)
                    col0 = dr * 64 + c * 4
                    pk, rpk = psum()
                    pv_, rpv = psum()
                    pkb = pk[:].bitcast(BF16)
                    pvb = pv_[:].bitcast(BF16)
                    for h in range(4):
                        for kt2 in range(2):
                            tr(pkb[0:64, h * 256 + kt2 * 128:h * 256 + (kt2 + 1) * 128], kT[:, h * 2 + kt2, tsl], identb[:, :], [Rk, R_c], [rpk])
                            tr(pvb[0:64, h * 256 + kt2 * 128:h * 256 + (kt2 + 1) * 128], vT[:, h * 2 + kt2, tsl], identb[:, :], [Rv, R_c], [rpv])
                    for h in range(4):
                        act(kp[:, h, :], pkb[0:64, h * 256:(h + 1) * 256], AF.Copy, [rpk, rgm], [R_kp[h]], scale=KSC[0:64, col0 + h:col0 + h + 1])
                    cpy(vaug[:, :, 0:256], pvb[0:64, 0:1024].rearrange("p (h e) -> p h e", e=256), [rpv], [R_va])
                    pq, rpq = psum()
                    for h in range(4):
                        for kt2 in range(2):
                            mm(pq[0:64, h * 64:(h + 1) * 64], kT[:, h * 2 + kt2, tsl], qT[:, h * 2 + kt2, tsl], kt2 == 0, kt2 == 1, [Rk, Rq], [rpq])
                    msk = triU if dr == 0 else triL
                    for h in range(4):
                        stt(spT[:, h, :], pq[0:64, h * 64:(h + 1) * 64], ATOK[0:64, col0 + h:col0 + h + 1], msk[:, :], ALU.mult, ALU.mult, [rpq, rgm, R_c], [R_sp[h]])
                    for h in range(4):
                        s_ = dr * 4 + h
                        act(cbf[:, h, :, :], Caug[:, s_, :, :], AF.Copy, [RCs[s_], RB[7], rgm], [R_cb[h]], scale=CSC[:, col0 + h:col0 + h + 1])
                    for h in range(4):
                        s_ = dr * 4 + h
                        pp, rpp = psum()
                        mm(pp[0:64, 0:257], spT[:, h, :], vaug[:, h, :], True, False, [R_sp[h], R_va], [rpp])
                        mm(pp[0:64, 0:257], qT[:, h * 2, tsl], cbf[:, h, 0, :], False, False, [Rq, R_cb[h]], [rpp])
                        mm(pp[0:64, 0:257], qT[:, h * 2 + 1, tsl], cbf[:, h, 1, :], False, True, [Rq, R_cb[h]], [rpp])
                        rr = rsm_t[:, h * 2:h * 2 + 1]
                        act(rr, pp[0:64, 256:257], AF.Abs, [rpp], [R_rs[h]])
                        ts(rr, rr, THR[0:64, col0 + h:col0 + h + 1], None, ALU.max, None, [R_rs[h], rgm], [R_rs[h]])
                        S.add("dve", (lambda o: (lambda e: e.reciprocal(o, o)))(rr), [R_rs[h]], [R_rs[h]])
                        hv_ = hs(c, h)
                        stt(hv_, pp[0:64, 0:256], rr, hv_, ALU.mult, ALU.add, [rpp, R_rs[h], Rhc[c], Rh[c // 8]], [Rhc[c]])
                        for kt2 in range(2):
                            pc, rpc = psum()
                            mm(pc[:, 0:257], kp[:, h, kt2 * 128:(kt2 + 1) * 128], vaug[:, h, :], True, True, [R_kp[h], R_va], [rpc])
                            stt(Caug[:, s_, kt2, :], Caug[:, s_, kt2, :], DEC[:, col0 + h:col0 + h + 1], pc[:, 0:257], ALU.mult, ALU.add, [RCs[s_], RB[7], rgm, rpc], [RCs[s_]])
            if g == 1:
                for dr in range(2):
                    for h in range(4):
                        dma("sp", stC[q, l, dr, h].rearrange("(k p) e -> p k e", p=128), Caug[:, dr * 4 + h, :, 0:256], [RCs[dr * 4 + h], RB[7]], [Res()])
                        dma("sp", stn[q, l, dr, h].rearrange("(k p o) -> p k o", p=128, o=1), Caug[:, dr * 4 + h, :, 256:257], [RCs[dr * 4 + h], RB[7]], [Res()], nonc=True)

        hcT = bf(BIG[0])[:, 0:8 * T].rearrange("p (f t) -> p f t", t=T)
        mnw = prm[:, P_MNW:P_MNW + 8]
        for c in range(16):
            hb = hsum if c < 8 else hsum2
            hc_ = hb[:, (c % 8) * 1024:(c % 8 + 1) * 1024]
            sqc = SC[c % 2][0:64, :]
            rq = RS[c % 2]
            act(sqc, hc_, AF.Square, [Rhc[c], Rh[c // 8]], [rq])
            S.add("dve", (lambda o, a: (lambda e: e.tensor_reduce(o, a, AX.X, ALU.add)))(hn_s[:, c, :], sqc.rearrange("p (h e) -> p h e", e=256)), [rq], [R_hn])
            ts(hn_s[:, c, :], hn_s[:, c, :], 1.0 / DH, EPS, ALU.mult, ALU.add, [R_hn], [R_hn])
            act(hn_s[:, c, :], hn_s[:, c, :], AF.Sqrt, [R_hn], [R_hn])
            S.add("dve", (lambda o: (lambda e: e.reciprocal(o, o)))(hn_s[:, c, :]), [R_hn], [R_hn])
            tt(sqc.rearrange("p (h e) -> p h e", e=256), hc_.rearrange("p (h e) -> p h e", e=256),
               hn_s[:, c, :].unsqueeze(2).to_broadcast([64, 4, 256]), ALU.mult, [Rhc[c], Rh[c // 8], R_hn], [rq])
            p, rp = psum()
            for ft in range(8):
                tr(p[:, ft * 64:(ft + 1) * 64], sqc[:, ft * 128:(ft + 1) * 128], ident[0:64, 0:64], [rq, R_c], [rp])
            tt(hcT[:, :, c * 64:(c + 1) * 64], p[:, :].rearrange("p (f t) -> p f t", t=64),
               mnw.unsqueeze(2).to_broadcast([128, 8, 64]), ALU.mult, [rp, R_prm], [RB[0]])

    def ffn(l, g):
        hv, hr = hT_views(0, 1)
        rhs = lambda kt, tb: (hv[kt][:, tb * 512:(tb + 1) * 512], [hr[kt]])

        def actv(j):
            b = 2 + j // 8
            return bf(BIG[b])[:, (j % 8) * T:(j % 8 + 1) * T], RB[b]
        fcw = prm[:, P_FCW:P_FCW + 132].rearrange("p (f k) -> p f k", k=3)
        u_sb, g_sb, gc, t1 = SC[0], SC[1], SC[2], SC[3]
        for qb in range(11):
            def ev_u(ft, tb, p, rp):
                if tb == 0:
                    cpy(u_sb[:, 0:512], p[:, :], [rp], [RS[0]])
                else:
                    act(u_sb[:, 512:1024], p[:, :], AF.Copy, [rp], [RS[0]])

            def ev_g(ft, tb, p, rp):
                if tb == 0:
                    act(g_sb[:, 0:512], p[:, :], AF.Copy, [rp], [RS[1]])
                else:
                    cpy(g_sb[:, 512:1024], p[:, :], [rp], [RS[1]])
            vu, ru = load_panel(w_ffn_up[l][:, qb * 512:(qb + 1) * 512], 16, 512)
            vg, rg_ = load_panel(w_ffn_up[l][:, DFF + qb * 512:DFF + (qb + 1) * 512], 16, 512)
            for ft in range(4):
                j = qb * 4 + ft
                for (vv, rv, evf) in ((vu, ru, ev_u), (vg, rg_, ev_g)):
                    for tb in range(2):
                        p, rp = psum()
                        for kt in range(16):
                            mm(p[:, :], vv[:, kt, ft * 128:(ft + 1) * 128], hv[kt][:, tb * 512:(tb + 1) * 512], kt == 0, kt == 15, [rv, hr[kt]], [rp])
                        evf(ft, tb, p, rp)
                ts(gc[:, :], g_sb[:, :], fcw[:, j, 1:2], prm[:, P_FCB + j:P_FCB + j + 1], ALU.mult, ALU.add, [RS[1], R_prm], [RS[2]])
                for k in (0, 2):
                    d = k - 1
                    dv, _ = shifted(gc[:, :], g, "col", d)
                    _, sv = shifted(g_sb[:, :], g, "col", d)
                    stt(dv, sv, fcw[:, j, k:k + 1], dv, ALU.mult, ALU.add, [RS[1], RS[2], R_prm], [RS[2]])
                act(t1[:, :], gc[:, :], AF.Square, [RS[2]], [RS[3]])
                ts(t1[:, :], t1[:, :], 0.044715, 1.0, ALU.mult, ALU.add, [RS[3]], [RS[3]])
                tt(t1[:, :], t1[:, :], gc[:, :], ALU.mult, [RS[3], RS[2]], [RS[3]])
                act(t1[:, :], t1[:, :], AF.Sigmoid, [RS[3]], [RS[3]], scale=1.5957691216057308)
                tt(gc[:, :], gc[:, :], u_sb[:, :], ALU.mult, [RS[2], RS[0]], [RS[2]])
                av, ar = actv(j)
                tt(av, gc[:, :], t1[:, :], ALU.mult, [RS[2], RS[3]], [ar])
        for db in range(4):
            stage = BIG[0][:, 0:4 * T].rearrange("p (f t) -> p f t", t=T)
            tstage = BIG[1][:, 0:8 * 512].rearrange("p (i n) -> p i n", n=512)
            for dj in range(4):
                dt_ = db * 4 + dj

                def ev(ft, tb, p, rp, dj=dj):
                    if tb == 0:
                        cpy(stage[:, dj, 0:512], p[:, :], [rp], [RB[0]])
                    else:
                        act(stage[:, dj, 512:1024], p[:, :], AF.Copy, [rp], [RB[0]])
                linear(w_ffn_down[l][:, dt_ * 128:(dt_ + 1) * 128], 44, 128, lambda kt, tb: (actv(kt)[0][:, tb * 512:(tb + 1) * 512], [actv(kt)[1]]), ev)
            out_block_to_fd(stage, RB[0], tstage, RB[1], db * 4)

    for l in range(depth):
        mod_phase(l)
    for g in GORDER:
        load_params(0)
        fused_pass(g, None, (0, 0))
        for l in range(depth):
            mixer(l, g)
            fused_pass(g, (l, 0), (l, 1))
            ffn(l, g)
            if l + 1 < depth:
                load_params(l + 1)
                fused_pass(g, (l, 1), (l + 1, 0))
            else:
                fused_pass(g, (l, 1), None)
    S.emit(nc, es)
    es.close()
    return nc


_CACHE = {}


def _consts():
    ident = np.eye(128, dtype=np.float32)
    s = np.arange(64)
    triU = (s[:, None] <= s[None, :]).astype(np.float32)
    triL = (s[:, None] >= s[None, :]).astype(np.float32)
    inv = np.zeros((2, 4, T), np.float32)
    for gi, L in enumerate((16, 256)):
        t = np.arange(L)
        for wi, w in enumerate(POOLW):
            lo = np.clip(t - w // 2, 0, L)
            hi = np.clip(t - w // 2 + w, 0, L)
            ic = (1.0 / (hi - lo)).astype(np.float32)
            if gi == 0:
                inv[gi, wi] = np.repeat(ic, 64)
            else:
                inv[gi, wi] = np.tile(ic, 4)
    return ident, triU, triL, inv


def kernel(x_prompt, x_sample, state_C, state_n, state_m, c, c_ctx, w_ada, b_ada, norm_mix_pre, norm_mix_post,
           norm_ffn_pre, norm_ffn_post, w_in, b_in, conv_a_w, conv_a_b, ln_a_w, ln_a_b, w_a_out, w_pool,
           pool_scale, mlstm_norm_w, w_c_out, w_out, w_ffn_up, ffn_conv_w, ffn_conv_b, w_ffn_down, _depth=None,
           _cores=8):
    f = lambda a: np.ascontiguousarray(np.asarray(a, dtype=np.float32))
    depth = _depth or w_ada.shape[0]
    if depth not in _CACHE:
        _CACHE[depth] = build(depth)
    nc = _CACHE[depth]
    ident, triU, triL, inv = _consts()
    shared = {"w_ada": f(w_ada)[:depth], "b_ada": f(b_ada)[:depth], "norm_mix_pre": f(norm_mix_pre)[:depth],
              "norm_mix_post": f(norm_mix_post)[:depth], "norm_ffn_pre": f(norm_ffn_pre)[:depth],
              "norm_ffn_post": f(norm_ffn_post)[:depth], "w_in": f(w_in)[:depth], "b_in": f(b_in)[:depth],
              "conv_a_w": f(conv_a_w)[:depth], "conv_a_b": f(conv_a_b)[:depth], "ln_a_w": f(ln_a_w)[:depth],
              "ln_a_b": f(ln_a_b)[:depth], "w_a_out": f(w_a_out)[:depth], "w_pool": f(w_pool)[:depth],
              "pool_scale": f(pool_scale)[:depth], "mlstm_norm_w": f(mlstm_norm_w)[:depth],
              "w_c_out": f(w_c_out)[:depth], "w_out": f(w_out)[:depth], "w_ffn_up": f(w_ffn_up)[:depth],
              "ffn_conv_w": f(ffn_conv_w)[:depth], "ffn_conv_b": f(ffn_conv_b)[:depth],
              "w_ffn_down": f(w_ffn_down)[:depth], "c_ident": ident, "c_triU": triU, "c_triL": triL,
              "c_invcnt": inv}
    xp = f(x_prompt)
    xsm = f(x_sample)
    sC, sn, sm = f(state_C), f(state_n), f(state_m)
    cc, cctx = f(c), f(c_ctx)
    in_maps = []
    for core in range(_cores):
        b = core % 4
        m = dict(shared)
        m["xs"] = np.ascontiguousarray(np.concatenate([xsm[b], xp[core * 4:(core + 1) * 4].reshape(T, D)], axis=0))
        m["cond"] = np.ascontiguousarray(np.stack([cc[b], cctx], axis=0))
        m["C0"] = np.ascontiguousarray(sC[b][:depth])
        m["n0"] = np.ascontiguousarray(sn[b][:depth])
        m["m0"] = np.ascontiguousarray(sm[b][:depth].reshape(depth, 8))
        in_maps.append(m)
    res = run_bass_kernel_spmd(nc, in_maps, core_ids=list(range(_cores)))
    R = res.results
    B = x_prompt.shape[0]
    yp = np.zeros((B, 256, D), np.float32)
    ys = np.zeros((4, T, D), np.float32)
    nC = np.zeros((B, depth, 2, NH, DH, DH), np.float32)
    nn = np.zeros((B, depth, 2, NH, DH), np.float32)
    nm = np.zeros((B, depth, 2, NH), np.float32)
    for core in range(_cores):
        r = R[core]
        yy = np.asarray(r["y"])
        if core < 4:
            ys[core] = yy[0:T]
        yp[core * 4:(core + 1) * 4] = yy[T:2 * T].reshape(4, 256, D)
        nC[core * 4:(core + 1) * 4] = np.asarray(r["stC"])
        nn[core * 4:(core + 1) * 4] = np.asarray(r["stn"])
        nm[core * 4:(core + 1) * 4] = np.asarray(r["stm"]).reshape(4, depth, 2, NH)
    return yp, ys, nC, nn, nm
```

```python
import numpy as np
from contextlib import ExitStack
import concourse.bass as bass
import concourse.mybir as mybir
from concourse.bass_utils import run_bass_kernel_spmd

F32 = mybir.dt.float32
BF16 = mybir.dt.bfloat16
AF = mybir.ActivationFunctionType
ALU = mybir.AluOpType
AX = mybir.AxisListType

D = 2048
T = 1024
NT = 8
WA = 512
WB = 512
WC = 1024
NH = 4
DH = 256
DFF = 5632
NIN = 11792
OFF_POOL = 1024
OFF_QKV = 1536
OFF_OG = 4608
OFF_GATES = 5632
OFF_MERGE = 5648
EPS = 1e-6
POOLW = (2, 4, 8, 16)
BIGN = 4112


def _flat(x):
    out = []
    for a in x:
        if a is None:
            continue
        if isinstance(a, (list, tuple)):
            out.extend(_flat(a))
        else:
            out.append(a)
    return out


SAME_ENG_WAIT = True
GORDER = (0, 1)


class Res:
    __slots__ = ("lw", "rd", "dr")

    def __init__(self):
        self.lw = None
        self.rd = {}
        self.dr = []


class Op:
    __slots__ = ("eng", "fn", "deps", "dma", "sig", "sigval", "semk")

    def __init__(self, eng, fn, deps, dma):
        self.eng = eng
        self.fn = fn
        self.deps = deps
        self.dma = dma
        self.sig = False
        self.sigval = 0
        self.semk = None


class Sched:
    ENGS = ("pe", "act", "dve", "pool", "sp")
    NDS = 12

    def __init__(self):
        self.ops = []

    def add(self, eng, fn, reads=(), writes=(), dma=False):
        oid = len(self.ops)
        deps = set()
        reads = _flat(reads)
        writes = _flat(writes)
        for r in reads:
            if r.lw is not None:
                deps.add(r.lw)
        for w in writes:
            if w.lw is not None:
                deps.add(w.lw)
            deps.update(w.rd.values())
            deps.update(w.dr)
        self.ops.append(Op(eng, fn, deps, dma))
        for r in reads:
            if dma:
                r.dr.append(oid)
            else:
                r.rd[eng] = oid
        for w in writes:
            w.lw = oid
            w.rd = {}
            w.dr = []
        return oid

    def emit(self, nc, es):
        ops = self.ops
        for o in ops:
            if o.dma:
                o.sig = True
            for d in o.deps:
                od = ops[d]
                if od.dma:
                    continue
                if od.eng == o.eng and not o.dma and (o.eng == "pe" or not SAME_ENG_WAIT):
                    continue
                od.sig = True
        cnt = {e: 0 for e in self.ENGS}
        dcnt = {}
        dn = {e: 0 for e in self.ENGS}
        for o in ops:
            if o.dma:
                k = (o.eng, dn[o.eng] % self.NDS)
                dn[o.eng] += 1
                dcnt[k] = dcnt.get(k, 0) + 1
                o.semk = k
                o.sigval = 16 * dcnt[k]
            elif o.sig:
                cnt[o.eng] += 1
                o.sigval = cnt[o.eng]
        sems = {}
        for e in self.ENGS:
            sems[e] = es.enter_context(nc.semaphore("s_" + e))
        for k in dcnt:
            sems[k] = es.enter_context(nc.semaphore("d_%s_%d" % k))
        block = es.enter_context(nc.Block())
        handles = {"pe": block.tensor, "act": block.scalar, "dve": block.vector, "pool": block.gpsimd,
                   "sp": block.sync}

        def run_engine(ename):
            def body(h):
                seen = {}
                for o in ops:
                    if o.eng != ename:
                        continue
                    need = {}
                    for d in o.deps:
                        od = ops[d]
                        if od.dma:
                            key = od.semk
                        else:
                            if od.eng == ename and not o.dma and (ename == "pe" or not SAME_ENG_WAIT):
                                continue
                            key = od.eng
                        if need.get(key, 0) < od.sigval:
                            need[key] = od.sigval
                    if o.dma and o.sigval > 16:
                        if need.get(o.semk, 0) < o.sigval - 16:
                            need[o.semk] = o.sigval - 16
                    for key, v in need.items():
                        if seen.get(key, 0) < v:
                            h.wait_ge(sems[key], v)
                            seen[key] = v
                    ins = o.fn(h)
                    if o.dma:
                        ins.then_inc(sems[o.semk], 16)
                    elif o.sig:
                        ins.then_inc(sems[ename], 1)
                for k, c in dcnt.items():
                    if k[0] == ename:
                        h.wait_ge(sems[k], 16 * c)
            return body

        for e in self.ENGS:
            handles[e](run_engine(e))


def build(depth=4):
    nc = bass.Bass("TRN2", target_bir_lowering=False)
    S = Sched()
    es = ExitStack()

    def din(name, shape):
        return nc.dram_tensor(name, list(shape), F32, kind="ExternalInput").ap()

    def dout(name, shape):
        return nc.dram_tensor(name, list(shape), F32, kind="ExternalOutput").ap()

    xs = din("xs", [2 * T, D])
    cond = din("cond", [2, D])
    C0 = din("C0", [depth, 2, NH, DH, DH])
    n0 = din("n0", [depth, 2, NH, DH])
    m0 = din("m0", [depth, 8])
    w_ada = din("w_ada", [depth, D, 6 * D])
    b_ada = din("b_ada", [depth, 6 * D])
    nrm = {k: din(k, [depth, D]) for k in ("norm_mix_pre", "norm_mix_post", "norm_ffn_pre", "norm_ffn_post")}
    w_in = din("w_in", [depth, D, NIN])
    b_in = din("b_in", [depth, NIN])
    conv_a_w = din("conv_a_w", [depth, 31, WA])
    conv_a_b = din("conv_a_b", [depth, WA])
    ln_a_w = din("ln_a_w", [depth, WA])
    ln_a_b = din("ln_a_b", [depth, WA])
    w_a_out = din("w_a_out", [depth, WA, D])
    w_pool = din("w_pool", [depth, 4, 128, 512])
    pool_scale = din("pool_scale", [depth, D])
    mlstm_norm_w = din("mlstm_norm_w", [depth, WC])
    w_c_out = din("w_c_out", [depth, WC, D])
    w_out = din("w_out", [depth, D, D])
    w_ffn_up = din("w_ffn_up", [depth, D, 2 * DFF])
    ffn_conv_w = din("ffn_conv_w", [depth, 3, DFF])
    ffn_conv_b = din("ffn_conv_b", [depth, DFF])
    w_ffn_down = din("w_ffn_down", [depth, DFF, D])
    c_ident = din("c_ident", [128, 128])
    c_triU = din("c_triU", [64, 64])
    c_triL = din("c_triL", [64, 64])
    c_invcnt = din("c_invcnt", [2, 4, T])

    y = dout("y", [2 * T, D])
    stC = dout("stC", [4, depth, 2, NH, DH, DH])
    stn = dout("stn", [4, depth, 2, NH, DH])
    stm = dout("stm", [4, depth, 8])
    mod_d = dout("mod_d", [depth, 2, 6 * D])
    hT_d = dout("hT_d", [2, 128, 4096])
    R_hTd = Res()
    f_d = dout("f_d", [T, D])
    R_y = [[Res() for _ in range(NT)] for _ in range(2)]
    R_f = [Res() for _ in range(NT)]
    R_modp = {(l_, g_, pn_): Res() for l_ in range(depth) for g_ in range(2) for pn_ in range(24)}
    R_st = Res()

    def sb(name, shape, dt=F32):
        return es.enter_context(nc.sbuf_tensor(name, list(shape), dt))

    BIG = [sb("big%d" % i, [128, BIGN]) for i in range(8)]
    RBh = [[Res(), Res()] for _ in range(8)]
    RB = [tuple(h) for h in RBh]
    WBUF = [sb("wb%d" % i, [128, 8192], BF16) for i in range(2)]
    RW = [Res() for _ in range(2)]
    SC = [sb("sc%d" % i, [128, T]) for i in range(4)]
    RS = [Res() for _ in range(4)]
    junk = sb("junk", [128, 2048], BF16)
    R_junk = Res()
    PS = [es.enter_context(nc.psum_tensor("ps%d" % i, [128, 512], F32)) for i in range(8)]
    RP = [Res() for _ in range(8)]
    psn = [0]

    def psum():
        i = psn[0] % 8
        psn[0] += 1
        return PS[i], RP[i]

    wbn = [0]

    def bf(t, n=None):
        a = t[:].bitcast(BF16)
        return a

    def mm(out, lhsT, rhs, start, stop, R, W):
        S.add("pe", lambda e: e.matmul(out, lhsT, rhs, start=start, stop=stop), R, W)

    def tr(out, in_, ident, R, W):
        S.add("pe", lambda e: e.transpose(out, in_, ident), R, W)

    def act(out, in_, func, R, W, bias=None, scale=None, accum=None):
        kw = {}
        if bias is not None:
            kw["bias"] = bias
        if scale is not None:
            kw["scale"] = scale
        if accum is not None:
            kw["accum_out"] = accum
        S.add("act", lambda e: e.activation(out, in_, func, **kw), R, W)

    def tt(out, a, b, op, R, W, eng="dve"):
        S.add(eng, lambda e: e.tensor_tensor(out, a, b, op), R, W)

    def ts(out, a, s1, s2, op0, op1, R, W, eng="dve"):
        if op1 is None:
            S.add(eng, lambda e: e.tensor_scalar(out, a, s1, None, op0), R, W)
        else:
            S.add(eng, lambda e: e.tensor_scalar(out, a, s1, s2, op0, op1), R, W)

    def stt(out, a, s, b, op0, op1, R, W):
        S.add("dve", lambda e: e.scalar_tensor_tensor(out, a, s, b, op0, op1), R, W)

    def cpy(out, in_, R, W, eng="dve"):
        S.add(eng, lambda e: e.tensor_copy(out, in_), R, W)

    def mset(ap, v, W, eng="dve"):
        S.add(eng, lambda e: e.memset(ap, v), (), W)

    def dma(q, out, in_, R, W, nonc=False):
        if nonc:
            S.add(q, lambda e: e.dma_start(out=out, in_=in_, allow_slow_non_contiguous=True), R, W, dma=True)
        else:
            S.add(q, lambda e: e.dma_start(out=out, in_=in_), R, W, dma=True)

    gm = sb("gm", [128, 14, 128]); rgm = Res()
    G = sb("G", [64, 256]); rG = Res()
    acol = sb("acol", [128, 1])
    mcur_all = sb("mcur", [128, 2, 4, 4])
    rsm_t = sb("rsm_t", [64, 8])
    hn_s = sb("hn_s", [64, 16, 4]); R_hn = Res()
    wpool_sb = sb("wpool", [128, 4, 512], BF16); r_wp = Res()
    small = sb("small", [128, 8, 8]); rsm = [Res() for _ in range(8)]
    R_kp = [Res() for _ in range(4)]; R_va = Res(); R_sp = [Res() for _ in range(4)]
    R_cb = [Res() for _ in range(4)]; R_rs = [Res() for _ in range(4)]
    RCs = [Res() for _ in range(8)]; Rhc = [Res() for _ in range(16)]
    ident = sb("ident", [128, 128])
    identb = sb("identb", [128, 128], BF16)
    ones = sb("ones", [128, 128])
    triU = sb("triU", [64, 64])
    triL = sb("triL", [64, 64])
    R_c = Res()
    dma("sp", ident[:], c_ident, (), [R_c])
    dma("sp", triU[:], c_triU, (), [R_c])
    dma("sp", triL[:], c_triL, (), [R_c])
    cpy(identb[:], ident[:], [R_c], [R_c])
    mset(ones[:], 1.0, [R_c])

    scT = sb("scT", [128, 16, 64], BF16)
    R_scT = Res()
    cnd = SC[0]
    mset(BIG[0][0:64, 0:D], 0.0, [RB[0]])
    dma("sp", BIG[0][0:1, 0:D], cond[0:1, :], (), [RB[0]])
    dma("sp", BIG[0][32:33, 0:D], cond[1:2, :], (), [RB[0]])
    act(BIG[0][0:64, 0:D], BIG[0][0:64, 0:D], AF.Silu, [RB[0]], [RB[0]])
    for kt in range(16):
        p, rp = psum()
        tr(p[:, 0:64], BIG[0][0:64, kt * 128:(kt + 1) * 128], ident[0:64, 0:64], [RB[0], R_c], [rp])
        cpy(scT[:, kt, :], p[:, 0:64], [rp], [R_scT])

    prm = sb("prm", [128, 640])
    R_prm = Res()
    P_BIN, P_CAW, P_CAB, P_LNW, P_LNB, P_PSC, P_MNW, P_FCW, P_FCB, P_BQ = 0, 92, 216, 220, 224, 228, 244, 252, 384, 428
    gbias = sb("gbias", [16, 1])

    def load_rows_T(dst_ap_fn, src2d, nrows, ncolt, stage, rstage):
        for c0 in range(0, ncolt, 8):
            nc_ = min(8, ncolt - c0)
            stage, rstage = SC[2 + (c0 // 8) % 2], RS[2 + (c0 // 8) % 2]
            dma("sp", stage[0:nrows, 0:nc_ * 128], src2d[:, c0 * 128:(c0 + nc_) * 128], (), [rstage])
            for c in range(nc_):
                p, rp = psum()
                tr(p[:, 0:nrows], stage[0:nrows, c * 128:(c + 1) * 128], ident[0:nrows, 0:nrows], [rstage, R_c], [rp])
                cpy(dst_ap_fn(c0 + c), p[:, 0:nrows], [rp], [R_prm])

    def load_params(l):
        st, rs = None, None
        load_rows_T(lambda c: prm[:, P_BIN:P_BIN + 44], b_in[l, 0:5632].rearrange("(r c) -> r c", c=128), 44, 1, st, rs)
        load_rows_T(lambda c: prm[:, P_BIN + 44:P_BIN + 92], b_in[l, OFF_MERGE:NIN].rearrange("(r c) -> r c", c=128), 48, 1, st, rs)
        dma("sp", gbias[:], b_in[l, OFF_GATES:OFF_MERGE].rearrange("(p o) -> p o", o=1), (), [R_prm])
        pv = prm[:, P_CAW:P_CAW + 124].rearrange("p (f k) -> p f k", k=31)
        load_rows_T(lambda c: pv[:, c, :], conv_a_w[l], 31, 4, st, rs)
        load_rows_T(lambda c: prm[:, P_CAB:P_CAB + 4], conv_a_b[l].rearrange("(r c) -> r c", c=128), 4, 1, st, rs)
        load_rows_T(lambda c: prm[:, P_LNW:P_LNW + 4], ln_a_w[l].rearrange("(r c) -> r c", c=128), 4, 1, st, rs)
        load_rows_T(lambda c: prm[:, P_LNB:P_LNB + 4], ln_a_b[l].rearrange("(r c) -> r c", c=128), 4, 1, st, rs)
        load_rows_T(lambda c: prm[:, P_PSC:P_PSC + 16], pool_scale[l].rearrange("(r c) -> r c", c=128), 16, 1, st, rs)
        load_rows_T(lambda c: prm[:, P_MNW:P_MNW + 8], mlstm_norm_w[l].rearrange("(r c) -> r c", c=128), 8, 1, st, rs)
        fv = prm[:, P_FCW:P_FCW + 132].rearrange("p (f k) -> p f k", k=3)
        load_rows_T(lambda c: fv[:, c, :], ffn_conv_w[l][:, 0:2816], 3, 22, st, rs)
        load_rows_T(lambda c: fv[:, 22 + c, :], ffn_conv_w[l][:, 2816:5632], 3, 22, st, rs)
        load_rows_T(lambda c: prm[:, P_FCB:P_FCB + 44], ffn_conv_b[l].rearrange("(r c) -> r c", c=128), 44, 1, st, rs)
        ts(prm[:, P_BQ:P_BQ + 8], prm[:, P_BIN + 12:P_BIN + 20], 1.0 / 16.0, None, ALU.mult, None, [R_prm], [R_prm])

    def load_panel(src, KT, ncols):
        i = wbn[0] % 2
        wbn[0] += 1
        v = WBUF[i][:, 0:KT * ncols].rearrange("p (k n) -> p k n", n=ncols)
        dma("pool", v, src.rearrange("(k p) n -> p k n", p=128), (), [RW[i]])
        return v, RW[i]

    def linear(src, KT, ncols, rhs_fn, evac_fn, ft0=0):
        v, rw = load_panel(src, KT, ncols)
        for ft in range(ncols // 128):
            for tb in range(2):
                p, rp = psum()
                for kt in range(KT):
                    ra, rr = rhs_fn(kt, tb)
                    mm(p[:, :], v[:, kt, ft * 128:(ft + 1) * 128], ra, kt == 0, kt == KT - 1, [rw] + rr, [rp])
                evac_fn(ft0 + ft, tb, p, rp)

    def mod_phase(l):
        for pn in range(24):
            stg, rsg = SC[(pn % 2) * 2], RS[(pn % 2) * 2]
            bst, rbs = SC[(pn % 2) * 2 + 1], RS[(pn % 2) * 2 + 1]
            v, rw = load_panel(w_ada[l, :, pn * 512:(pn + 1) * 512], 16, 512)
            p, rp = psum()
            for kt in range(16):
                mm(p[0:64, :], scT[:, kt, :], v[:, kt, :], kt == 0, kt == 15, [rw, R_scT], [rp])
            dma("sp", bst[0:1, 0:512], b_ada[l, pn * 512:(pn + 1) * 512].rearrange("(o n) -> o n", o=1), (), [rbs])
            dma("sp", bst[32:33, 0:512], b_ada[l, pn * 512:(pn + 1) * 512].rearrange("(o n) -> o n", o=1), (), [rbs])
            tt(stg[0:1, 0:512], p[0:1, :], bst[0:1, 0:512], ALU.add, [rp, rbs], [rsg])
            tt(stg[32:33, 0:512], p[32:33, :], bst[32:33, 0:512], ALU.add, [rp, rbs], [rsg])
            dma("sp", mod_d[l, 0:1, pn * 512:(pn + 1) * 512], stg[0:1, 0:512], [rsg], [R_modp[(l, 0, pn)]])
            dma("sp", mod_d[l, 1:2, pn * 512:(pn + 1) * 512], stg[32:33, 0:512], [rsg], [R_modp[(l, 1, pn)]])

    def bcast_load(dst, rdst, vec, modsel=None):
        rr_ = [] if modsel is None else [R_modp[(modsel[0], modsel[1], modsel[2] * 4 + k_)] for k_ in range(4)]
        dma("sp", dst, vec.partition_broadcast(128), rr_, [rdst])

    RH = {0: [Res() for _ in range(8)], 1: [Res() for _ in range(8)]}
    clm = sb("clm", [128, 2])

    def claim(b):
        mset(clm[:, 0:1], 0.0, [RB[b], RH[b]])

    def hT_views(b0, b1, fine=False):
        hv, hr = [], []
        for kt in range(16):
            b = b0 if kt < 8 else b1
            hv.append(bf(BIG[b])[:, (kt % 8) * T:(kt % 8 + 1) * T])
            hr.append(RH[b][kt % 8] if fine else RB[b])
        return hv, hr

    def fused_pass(g, fin, nrmsel):
        gam = BIG[7][:, 0:D]
        shf = BIG[7][:, D:2 * D]
        pg = BIG[2][:, 0:D]
        tmpb = BIG[2][:, D:2 * D]
        rb = RB[7]
        rpg = RB[2]
        if fin is not None:
            lf, wf = fin
            gi = 2 if wf == 0 else 5
            nwf = nrm["norm_mix_post" if wf == 0 else "norm_ffn_post"]
            bcast_load(pg, rpg, mod_d[lf, g, gi * D:(gi + 1) * D], (lf, g, gi))
            bcast_load(tmpb, rpg, nwf[lf])
            tt(pg, pg, tmpb, ALU.mult, [rpg], [rpg])
            if lf == 0 and wf == 0:
                xsrc = [(xs[g * T + i * 128:g * T + (i + 1) * 128, :], []) for i in range(NT)]
            else:
                xsrc = [(y[g * T + i * 128:g * T + (i + 1) * 128, :], [R_y[g][i]]) for i in range(NT)]
        else:
            xsrc = [(xs[g * T + i * 128:g * T + (i + 1) * 128, :], []) for i in range(NT)]
        if nrmsel is not None:
            ln_, wn = nrmsel
            si, ci = (0, 1) if wn == 0 else (3, 4)
            nw = nrm["norm_mix_pre" if wn == 0 else "norm_ffn_pre"]
            bcast_load(gam, rb, mod_d[ln_, g, ci * D:(ci + 1) * D], (ln_, g, ci))
            bcast_load(shf, rb, nw[ln_])
            stt(gam, gam, 1.0, shf, ALU.add, ALU.mult, [rb], [rb])
            bcast_load(shf, rb, mod_d[ln_, g, si * D:(si + 1) * D], (ln_, g, si))
            claim(0)
            claim(1)
            hviews, hres = hT_views(0, 1, fine=True)
        xbuf = [(BIG[6][:, 0:D], RBh[6][0]), (BIG[6][:, D:2 * D], RBh[6][1]), (BIG[5][:, 0:D], RBh[5][0]), (BIG[5][:, D:2 * D], RBh[5][1])]
        fbuf = [(BIG[4][:, 0:D], RBh[4][0]), (BIG[4][:, D:2 * D], RBh[4][1]), (BIG[3][:, 0:D], RBh[3][0]), (BIG[3][:, D:2 * D], RBh[3][1])]
        for i in range(NT):
            x_t, rx = xbuf[i % 4]
            f_t, rf = fbuf[i % 4]
            ri = rsm[i]
            dma("sp", x_t, xsrc[i][0], xsrc[i][1], [rx])
            if fin is not None:
                dma("sp", f_t, f_d[i * 128:(i + 1) * 128, :], [R_f[i]], [rf])
                act(junk[:, :], f_t, AF.Square, [rf], [ri], accum=small[:, i, 0:1])
                act(small[:, i, 1:2], small[:, i, 0:1], AF.Sqrt, [ri], [ri], scale=1.0 / D, bias=EPS)
                S.add("dve", (lambda o, a: (lambda e: e.reciprocal(o, a)))(small[:, i, 2:3], small[:, i, 1:2]), [ri], [ri])
                stt(f_t, f_t, small[:, i, 2:3], pg, ALU.mult, ALU.mult, [rf, ri, rpg], [rf])
                tt(f_t, f_t, x_t, ALU.add, [rf, rx], [rf])
                dma("sp", y[g * T + i * 128:g * T + (i + 1) * 128, :], f_t, [rf], [R_y[g][i]])
                cur, rcur = f_t, rf
            else:
                cur, rcur = x_t, rx
            if nrmsel is not None:
                act(junk[:, :], cur, AF.Square, [rcur], [ri], accum=small[:, i, 4:5])
                act(small[:, i, 5:6], small[:, i, 4:5], AF.Sqrt, [ri], [ri], scale=1.0 / D, bias=EPS)
                S.add("dve", (lambda o, a: (lambda e: e.reciprocal(o, a)))(small[:, i, 6:7], small[:, i, 5:6]), [ri], [ri])
                stt(x_t, cur, small[:, i, 6:7], gam, ALU.mult, ALU.mult, [rcur, ri, rb], [rx])
                tt(x_t, x_t, shf, ALU.add, [rx, rb], [rx])
                for q4 in range(4):
                    p, rp = psum()
                    for j in range(4):
                        kt = q4 * 4 + j
                        tr(p[:, j * 128:(j + 1) * 128], x_t[:, kt * 128:(kt + 1) * 128], ident[:, :], [rx, R_c], [rp])
                    b_ = q4 // 2
                    k0 = (q4 % 2) * 4
                    dst = bf(BIG[b_])[:, 0:8 * T].rearrange("p (k t) -> p k t", t=T)[:, k0:k0 + 4, i * 128:(i + 1) * 128]
                    src = p[:, :].rearrange("p (k t) -> p k t", t=128)
                    if q4 % 2 == 0:
                        cpy(dst, src, [rp], [RH[b_][k0:k0 + 4]])
                    else:
                        act(dst, src, AF.Copy, [rp], [RH[b_][k0:k0 + 4]])
        if nrmsel is not None and nrmsel[1] == 0:
            dma("sp", hT_d[0], BIG[0][:, 0:4096], [RH[0]], [R_hTd])
            dma("sp", hT_d[1], BIG[1][:, 0:4096], [RH[1]], [R_hTd])

    def out_block_to_fd(stage, rstage, tstage, rtstage, d0):
        sv = stage
        tv = tstage
        for i in range(NT):
            p, rp = psum()
            for j in range(4):
                tr(p[:, j * 128:(j + 1) * 128], sv[:, j, i * 128:(i + 1) * 128], ident[:, :], [rstage, R_c], [rp])
            if i % 2 == 0:
                cpy(tv[:, i, :], p[:, :], [rp], [rtstage])
            else:
                act(tv[:, i, :], p[:, :], AF.Copy, [rp], [rtstage])
            dma("sp", f_d[i * 128:(i + 1) * 128, d0 * 128:d0 * 128 + 512], tv[:, i, :], [rtstage], [R_f[i]])

    def shifted(a, g, kind, d):
        if g == 0:
            a3 = a.rearrange("p (r c) -> p r c", c=64)
            if kind == "row":
                lo, hi = max(0, -d), 64 - max(0, d)
                return a3[:, :, lo:hi], a3[:, :, lo + d:hi + d]
            lo, hi = max(0, -d), 16 - max(0, d)
            return a3[:, lo:hi, :], a3[:, lo + d:hi + d, :]
        a3 = a.rearrange("p (r c) -> p r c", c=256)
        lo, hi = max(0, -d), 256 - max(0, d)
        return a3[:, :, lo:hi], a3[:, :, lo + d:hi + d]

    def mixer(l, g):
        wl = w_in[l]
        hv, hr = hT_views(0, 1, fine=True)

        def rhs_h(hv, hr):
            return lambda kt, tb: (hv[kt][:, tb * 512:(tb + 1) * 512], [hr[kt]])

        qkvT = [bf(BIG[2 + j])[:, 0:8 * T].rearrange("p (f t) -> p f t", t=T) for j in range(3)]
        for j in range(3):
            for pn in range(2):
                c0 = OFF_QKV + j * WC + pn * 512

                def ev(ft, tb, p, rp, j=j):
                    bt = OFF_QKV // 128 + j * 8 + ft
                    dst = qkvT[j][:, ft, tb * 512:(tb + 1) * 512]
                    if j == 0:
                        act(dst, p[:, :], AF.Identity, [rp, R_prm], [RB[2]], bias=prm[:, P_BQ + ft:P_BQ + ft + 1], scale=1.0 / 16.0)
                    elif (ft + tb) % 2 == 0:
                        act(dst, p[:, :], AF.Identity, [rp, R_prm], [RB[2 + j]], bias=prm[:, P_BIN + bt:P_BIN + bt + 1])
                    else:
                        ts(dst, p[:, :], prm[:, P_BIN + bt:P_BIN + bt + 1], None, ALU.add, None, [rp, R_prm], [RB[2 + j]])
                linear(wl[:, c0:c0 + 512], 16, 512, rhs_h(hv, hr), ev, ft0=pn * 4)
        gT = SC[3]
        i = wbn[0] % 2
        wbn[0] += 1
        gv = WBUF[i][:, 0:256].rearrange("p (k n) -> p k n", n=16)
        dma("pool", gv, wl[:, OFF_GATES:OFF_MERGE].rearrange("(k p) n -> p k n", p=128), (), [RW[i]], nonc=True)
        for tb in range(2):
            p, rp = psum()
            for kt in range(16):
                mm(p[0:16, :], gv[:, kt, :], hv[kt][:, tb * 512:(tb + 1) * 512], kt == 0, kt == 15, [RW[i], hr[kt]], [rp])
            act(gT[0:16, tb * 512:(tb + 1) * 512], p[0:16, :], AF.Identity, [rp, R_prm], [RS[3]], bias=gbias[:, 0:1])

        claim(0)
        claim(1)
        mlstm(l, g, qkvT, gT)

        hv, hr = hT_views(1, 2)
        dma("sp", BIG[1][:, 0:4096], hT_d[0], [R_hTd], [RB[1], R_kp, R_va, R_sp, R_cb])
        dma("sp", BIG[2][:, 0:4096], hT_d[1], [R_hTd], [RB[2]])
        hcT = bf(BIG[0])[:, 0:8 * T].rearrange("p (f t) -> p f t", t=T)

        for pn in range(2):
            c0 = OFF_OG + pn * 512

            def ev(ft, tb, p, rp):
                bt = OFF_OG // 128 + ft
                tmp = SC[(ft + tb) % 2][:, 0:512]
                rt = RS[(ft + tb) % 2]
                act(tmp, p[:, :], AF.Sigmoid, [rp, R_prm], [rt], bias=prm[:, P_BIN + bt:P_BIN + bt + 1])
                tt(hcT[:, ft, tb * 512:(tb + 1) * 512], hcT[:, ft, tb * 512:(tb + 1) * 512], tmp, ALU.mult, [rt, RB[0]], [RB[0]])
            linear(wl[:, c0:c0 + 512], 16, 512, rhs_h(hv, hr), ev, ft0=pn * 4)

        ga = BIG[3][:, 0:4 * T].rearrange("p (f t) -> p f t", t=T)
        gb = BIG[4][:, 0:4 * T].rearrange("p (f t) -> p f t", t=T)
        acc = BIG[5][:, 0:4 * T].rearrange("p (f t) -> p f t", t=T)
        sq = BIG[6][:, 0:4 * T].rearrange("p (f t) -> p f t", t=T)
        aT = bf(BIG[7])[:, 0:4 * T].rearrange("p (f t) -> p f t", t=T)
        pT = bf(BIG[7])[:, 4 * T:8 * T].rearrange("p (f t) -> p f t", t=T)
        for pn in range(2):
            def ev(ft, tb, p, rp):
                if ft < 4:
                    act(ga[:, ft, tb * 512:(tb + 1) * 512], p[:, :], AF.Identity, [rp, R_prm], [RB[3]], bias=prm[:, P_BIN + ft:P_BIN + ft + 1])
                else:
                    act(gb[:, ft - 4, tb * 512:(tb + 1) * 512], p[:, :], AF.Sigmoid, [rp, R_prm], [RB[4]], bias=prm[:, P_BIN + ft:P_BIN + ft + 1])
            linear(wl[:, pn * 512:(pn + 1) * 512], 16, 512, rhs_h(hv, hr), ev, ft0=pn * 4)
        caw = prm[:, P_CAW:P_CAW + 124].rearrange("p (f k) -> p f k", k=31)
        for ft in range(4):
            tt(ga[:, ft, :], ga[:, ft, :], gb[:, ft, :], ALU.mult, [RB[3], RB[4]], [RB[3]])
            ts(acc[:, ft, :], ga[:, ft, :], caw[:, ft, 15:16], prm[:, P_CAB + ft:P_CAB + ft + 1], ALU.mult, ALU.add, [RB[3], R_prm], [RB[5]])
            for k in range(31):
                d = k - 15
                if d == 0:
                    continue
                dv, _ = shifted(acc[:, ft, :], g, "row", d)
                _, sv = shifted(ga[:, ft, :], g, "row", d)
                stt(dv, sv, caw[:, ft, k:k + 1], dv, ALU.mult, ALU.add, [RB[3], RB[5], R_prm], [RB[5]])
            act(sq[:, ft, :], acc[:, ft, :], AF.Square, [RB[5]], [RB[6]])
        for tb in range(2):
            p1, r1 = psum()
            p2, r2 = psum()
            for ft in range(4):
                mm(p1[:, :], ones[:, :], acc[:, ft, tb * 512:(tb + 1) * 512], ft == 0, ft == 3, [R_c, RB[5]], [r1])
            for ft in range(4):
                mm(p2[:, :], ones[:, :], sq[:, ft, tb * 512:(tb + 1) * 512], ft == 0, ft == 3, [R_c, RB[6]], [r2])
            mean = SC[0][:, 0:512]
            var = SC[1][:, 0:512]
            act(mean, p1[:, :], AF.Copy, [r1], [RS[0]], scale=1.0 / WA)
            tt(var, mean, mean, ALU.mult, [RS[0]], [RS[1]])
            stt(var, p2[:, :], 1.0 / WA, var, ALU.mult, ALU.subtract, [r2, RS[1]], [RS[1]])
            ts(var, var, EPS, None, ALU.add, None, [RS[1]], [RS[1]])
            act(var, var, AF.Sqrt, [RS[1]], [RS[1]])
            S.add("dve", (lambda o: (lambda e: e.reciprocal(o, o)))(var), [RS[1]], [RS[1]])
            for ft in range(4):
                tmp = SC[2][:, 0:512]
                tt(tmp, acc[:, ft, tb * 512:(tb + 1) * 512], mean, ALU.subtract, [RB[5], RS[0]], [RS[2]])
                tt(tmp, tmp, var, ALU.mult, [RS[2], RS[1]], [RS[2]])
                act(aT[:, ft, tb * 512:(tb + 1) * 512], tmp, AF.Silu, [RS[2], R_prm], [RB[7]],
                    bias=prm[:, P_LNB + ft:P_LNB + ft + 1], scale=prm[:, P_LNW + ft:P_LNW + ft + 1])

        zp = BIG[3][:, 0:4 * T].rearrange("p (f t) -> p f t", t=T)
        pacc = BIG[4][:, 0:4 * T].rearrange("p (f t) -> p f t", t=T)

        def ev(ft, tb, p, rp):
            bt = OFF_POOL // 128 + ft
            act(zp[:, ft, tb * 512:(tb + 1) * 512], p[:, :], AF.Identity, [rp, R_prm], [RB[3]], bias=prm[:, P_BIN + bt:P_BIN + bt + 1])
        linear(wl[:, OFF_POOL:OFF_POOL + 512], 16, 512, rhs_h(hv, hr), ev)
        for ft in range(4):
            w = POOLW[ft]
            cpy(pacc[:, ft, :], zp[:, ft, :], [RB[3]], [RB[4]])
            for d in range(-(w // 2), w // 2):
                if d == 0:
                    continue
                dv, _ = shifted(pacc[:, ft, :], g, "col", d)
                _, sv = shifted(zp[:, ft, :], g, "col", d)
                tt(dv, dv, sv, ALU.add, [RB[3], RB[4]], [RB[4]])
            ic = SC[0]
            dma("sp", ic[:, :], c_invcnt[g, ft].partition_broadcast(128), (), [RS[0]])
            tt(pacc[:, ft, :], pacc[:, ft, :], ic[:, :], ALU.mult, [RB[4], RS[0]], [RB[4]])
            tt(pT[:, ft, :], pacc[:, ft, :], zp[:, ft, :], ALU.subtract, [RB[4], RB[3]], [RB[7]])

        def mixv(dt_):
            b = 3 if dt_ < 8 else 4
            return bf(BIG[b])[:, (dt_ % 8) * T:(dt_ % 8 + 1) * T], RB[b]
        dma("pool", wpool_sb[:, :, :], w_pool[l].rearrange("g c d -> c g d"), (), [r_wp])
        for db in range(4):
            wa_v, wa_r = load_panel(w_a_out[l][:, db * 512:(db + 1) * 512], 4, 512)
            wc_v, wc_r = load_panel(w_c_out[l][:, db * 512:(db + 1) * 512], 8, 512)
            ya = BIG[5][:, 0:4 * T].rearrange("p (f t) -> p f t", t=T)
            yc = BIG[6][:, 0:4 * T].rearrange("p (f t) -> p f t", t=T)
            for dj in range(4):
                for tb in range(2):
                    p, rp = psum()
                    for kt in range(4):
                        mm(p[:, :], wa_v[:, kt, dj * 128:(dj + 1) * 128], aT[:, kt, tb * 512:(tb + 1) * 512], kt == 0, kt == 3, [wa_r, RB[7]], [rp])
                    act(ya[:, dj, tb * 512:(tb + 1) * 512], p[:, :], AF.Copy, [rp], [RB[5]])
                    p, rp = psum()
                    for kt in range(8):
                        mm(p[:, :], wc_v[:, kt, dj * 128:(dj + 1) * 128], hcT[:, kt, tb * 512:(tb + 1) * 512], kt == 0, kt == 7, [wc_r, RB[0]], [rp])
                    cpy(yc[:, dj, tb * 512:(tb + 1) * 512], p[:, :], [rp], [RB[6]])
            for br in range(3):
                c0 = OFF_MERGE + br * D + db * 512

                def ev(ft, tb, p, rp, br=br, db=db):
                    dt_ = db * 4 + ft
                    bt = 44 + br * 16 + dt_
                    gt = SC[(ft + tb) % 2][:, 0:512]
                    rg = RS[(ft + tb) % 2]
                    act(gt, p[:, :], AF.Sigmoid, [rp, R_prm], [rg], bias=prm[:, P_BIN + bt:P_BIN + bt + 1])
                    mv, mr = mixv(dt_)
                    msl = mv[:, tb * 512:(tb + 1) * 512]
                    if br == 0:
                        tt(ya[:, ft, tb * 512:(tb + 1) * 512], ya[:, ft, tb * 512:(tb + 1) * 512], gt, ALU.mult, [rg, RB[5]], [RB[5]])
                    elif br == 1:
                        gidx = dt_ // 4
                        p2, rp2 = psum()
                        mm(p2[:, :], wpool_sb[:, gidx, (dt_ % 4) * 128:(dt_ % 4 + 1) * 128], pT[:, gidx, tb * 512:(tb + 1) * 512], True, True, [r_wp, RB[7]], [rp2])
                        tmp = SC[2][:, 0:512]
                        stt(tmp, p2[:, :], prm[:, P_PSC + dt_:P_PSC + dt_ + 1], gt, ALU.mult, ALU.mult, [rp2, rg, R_prm], [RS[2]])
                        tt(ya[:, ft, tb * 512:(tb + 1) * 512], ya[:, ft, tb * 512:(tb + 1) * 512], tmp, ALU.add, [RS[2], RB[5]], [RB[5]])
                    else:
                        tt(yc[:, ft, tb * 512:(tb + 1) * 512], yc[:, ft, tb * 512:(tb + 1) * 512], gt, ALU.mult, [rg, RB[6]], [RB[6]])
                        tt(msl, ya[:, ft, tb * 512:(tb + 1) * 512], yc[:, ft, tb * 512:(tb + 1) * 512], ALU.add, [RB[5], RB[6]], [mr])
                linear(wl[:, c0:c0 + 512], 16, 512, rhs_h(hv, hr), ev)

        for db in range(4):
            stage = BIG[5][:, 0:4 * T].rearrange("p (f t) -> p f t", t=T)
            tstage = BIG[6][:, 0:8 * 512].rearrange("p (i n) -> p i n", n=512)

            def ev(ft, tb, p, rp):
                if (ft + tb) % 2 == 0:
                    cpy(stage[:, ft, tb * 512:(tb + 1) * 512], p[:, :], [rp], [RB[5]])
                else:
                    act(stage[:, ft, tb * 512:(tb + 1) * 512], p[:, :], AF.Copy, [rp], [RB[5]])
            linear(w_out[l][:, db * 512:(db + 1) * 512], 16, 512, lambda kt, tb: (mixv(kt)[0][:, tb * 512:(tb + 1) * 512], [mixv(kt)[1]]), ev)
            out_block_to_fd(stage, RB[5], tstage, RB[6], db * 4)

    def mlstm(l, g, qkvT, gT):
        qT, kT, vT = qkvT
        Rq, Rk, Rv = RB[2], RB[3], RB[4]
        nseq, nch = (1, 16) if g == 0 else (4, 4)
        LF, LI, BC_, UU, ABC, BL, MALL, MPREV, DEC, CSC, KFAC, ATOK, KSC, THR = [gm[:, i, :] for i in range(14)]
        p, rp = psum()
        for c in range(16):
            tr(p[0:64, c * 16:(c + 1) * 16], gT[0:16, c * 64:(c + 1) * 64], ident[0:16, 0:16], [RS[3], R_c], [rp])
        cpy(G[:, :], p[0:64, 0:256], [rp], [rG])
        G5 = G[:, :].rearrange("p (c d k h) -> p c d k h", d=2, k=2, h=4)
        for dr in range(2):
            lfv = LF[0:64, dr * 64:(dr + 1) * 64].rearrange("p (c h) -> p c h", h=4)
            liv = LI[0:64, dr * 64:(dr + 1) * 64].rearrange("p (c h) -> p c h", h=4)
            act(lfv, G5[:, :, dr, 1, :], AF.Exp, [rG], [rgm], scale=-1.0)
            cpy(liv, G5[:, :, dr, 0, :], [rG], [rgm])
        act(LF[0:64, :], LF[0:64, :], AF.Ln, [rgm], [rgm], bias=1.0)
        ts(LF[0:64, :], LF[0:64, :], -1.0, None, ALU.mult, None, [rgm], [rgm])
        p, rp = psum()
        mm(p[0:64, 0:64], triU[:, :], LF[0:64, 0:64], True, True, [R_c, rgm], [rp])
        mm(p[0:64, 64:128], triL[:, :], LF[0:64, 64:128], True, True, [R_c, rgm], [rp])
        cpy(BC_[0:64, :], p[0:64, 0:128], [rp], [rgm])
        tt(UU[0:64, :], LI[0:64, :], BC_[0:64, :], ALU.subtract, [rgm], [rgm])
        p, rp = psum()
        tr(p[:, 0:64], UU[0:64, :], ident[0:64, 0:64], [rgm, R_c], [rp])
        S.add("dve", (lambda o, a: (lambda e: e.tensor_reduce(o, a, AX.X, ALU.max)))(acol[:, :], p[:, 0:64]), [rp], [rgm])
        diag = SC[0][:, 0:128]
        ts(diag, ident[:, :], acol[:, 0:1], None, ALU.mult, None, [rgm, R_c], [RS[0]])
        p2, rp2 = psum()
        mm(p2[:, 0:128], ones[:, :], diag, True, True, [R_c, RS[0]], [rp2])
        cpy(ABC, p2[:, 0:128], [rp2], [rgm])
        p3, rp3 = psum()
        mm(p3[:, 0:128], ones[0:64, :], LF[0:64, :], True, True, [R_c, rgm], [rp3])
        cpy(BL, p3[:, 0:128], [rp3], [rgm])
        mcur = mcur_all[:, :, 0:nseq, :]
        if g == 0:
            dma("sp", mcur[:, :, 0, :], m0[l].rearrange("(d h) -> d h", h=4).partition_broadcast(128), (), [rgm])
        else:
            mset(mcur[:, :, :, :], 0.0, [rgm])

        def colv(arr, dr, cc):
            return arr[:, dr * 64:(dr + 1) * 64].rearrange("p (q c h) -> p q c h", q=nseq, h=4)[:, :, cc, :]
        for j in range(nch):
            for dr in range(2):
                cc = j if dr == 0 else nch - 1 - j
                cpy(colv(MPREV, dr, cc), mcur[:, dr, :, :], [rgm], [rgm])
                tt(colv(MALL, dr, cc), mcur[:, dr, :, :], colv(ABC, dr, cc), ALU.max, [rgm], [rgm])
                tt(mcur[:, dr, :, :], colv(MALL, dr, cc), colv(BL, dr, cc), ALU.add, [rgm], [rgm])
                if g == 1 and j == nch - 1:
                    pass
        if g == 1:
            for q in range(4):
                dma("sp", stm[q, l].rearrange("(o d h) -> o d h", o=1, h=4), mcur[0:1, :, q, :], [rgm], [Res()])
        tt(DEC, MPREV, MALL, ALU.subtract, [rgm], [rgm])
        act(DEC, DEC, AF.Exp, [rgm], [rgm])
        tt(CSC, MPREV, ABC, ALU.subtract, [rgm], [rgm])
        act(CSC, CSC, AF.Exp, [rgm], [rgm])
        tt(KFAC, ABC, MALL, ALU.subtract, [rgm], [rgm])
        act(KFAC, KFAC, AF.Exp, [rgm], [rgm])
        tt(ATOK[0:64, :], UU[0:64, :], ABC[0:64, :], ALU.subtract, [rgm], [rgm])
        act(ATOK[0:64, :], ATOK[0:64, :], AF.Exp, [rgm], [rgm])
        tt(KSC[0:64, :], ATOK[0:64, :], KFAC[0:64, :], ALU.mult, [rgm], [rgm])
        tt(THR[0:64, :], BC_[0:64, :], ABC[0:64, :], ALU.add, [rgm], [rgm])
        act(THR[0:64, :], THR[0:64, :], AF.Exp, [rgm], [rgm], scale=-1.0)

        Caug = BIG[7][:, 0:8 * 2 * 257].rearrange("p (s k e) -> p s k e", k=2, e=257)
        RC = RB[7]
        hsum = bf(BIG[5])[0:64, :]
        hsum2 = bf(BIG[6])[0:64, :]

        def hs(c, h):
            b = hsum if c < 8 else hsum2
            return b[:, (c % 8) * 1024 + h * 256:(c % 8) * 1024 + (h + 1) * 256]
        Rh = [RB[5], RB[6]]
        mset(bf(BIG[5])[0:64, 0:8192], 0.0, [RB[5], Rhc[0:8]])
        mset(bf(BIG[6])[0:64, 0:8192], 0.0, [RB[6], Rhc[8:16]])
        wk = bf(BIG[1])
        kp = wk[0:64, 0:1024].rearrange("p (h e) -> p h e", e=256)
        vaug = wk[0:64, 1024:1024 + 4 * 257].rearrange("p (h e) -> p h e", e=257)
        spT = wk[0:64, 2304:2304 + 256].rearrange("p (h t) -> p h t", t=64)
        cbf = wk[:, 2560:2560 + 4 * 514].rearrange("p (h k e) -> p h k e", k=2, e=257)
        mset(wk[:, 0:4624], 0.0, [RB[1], R_kp, R_va, R_sp, R_cb])
        mset(vaug[:, :, 256:257], 1.0, [R_va])

        for q in range(nseq):
            if g == 0:
                for dr in range(2):
                    for h in range(4):
                        dma("sp", Caug[:, dr * 4 + h, :, 0:256], C0[l, dr, h].rearrange("(k p) e -> p k e", p=128), (), [RCs[dr * 4 + h], RB[7]])
                        dma("sp", Caug[:, dr * 4 + h, :, 256:257], n0[l, dr, h].rearrange("(k p o) -> p k o", p=128, o=1), (), [RCs[dr * 4 + h], RB[7]], nonc=True)
            else:
                mset(Caug[:, :, :, :], 0.0, [RCs, RB[7]])
            for j in range(nch):
                for dr in range(2):
                    cc = j if dr == 0 else nch - 1 - j
                    c = q * nch + cc
                    tsl = slice(c * 64, (c + 1) * 64)
                    col0 = dr * 64 + c * 4
                    pk, rpk = psum()
                    pv_, rpv = psum()
                    pkb = pk[:].bitcast(BF16)
                    pvb = pv_[:].bitcast(BF16)
                    for h in range(4):
                        for kt2 in range(2):
                            tr(pkb[0:64, h * 256 + kt2 * 128:h * 256 + (kt2 + 1) * 128], kT[:, h * 2 + kt2, tsl], identb[:, :], [Rk, R_c], [rpk])
                            tr(pvb[0:64, h * 256 + kt2 * 128:h * 256 + (kt2 + 1) * 128], vT[:, h * 2 + kt2, tsl], identb[:, :], [Rv, R_c], [rpv])
                    for h in range(4):
                        act(kp[:, h, :], pkb[0:64, h * 256:(h + 1) * 256], AF.Copy, [rpk, rgm], [R_kp[h]], scale=KSC[0:64, col0 + h:col0 + h + 1])
                    cpy(vaug[:, :, 0:256], pvb[0:64, 0:1024].rearrange("p (h e) -> p h e", e=256), [rpv], [R_va])
                    pq, rpq = psum()
                    for h in range(4):
                        for kt2 in range(2):
                            mm(pq[0:64, h * 64:(h + 1) * 64], kT[:, h * 2 + kt2, tsl], qT[:, h * 2 + kt2, tsl], kt2 == 0, kt2 == 1, [Rk, Rq], [rpq])
                    msk = triU if dr == 0 else triL
                    for h in range(4):
                        stt(spT[:, h, :], pq[0:64, h * 64:(h + 1) * 64], ATOK[0:64, col0 + h:col0 + h + 1], msk[:, :], ALU.mult, ALU.mult, [rpq, rgm, R_c], [R_sp[h]])
                    for h in range(4):
                        s_ = dr * 4 + h
                        act(cbf[:, h, :, :], Caug[:, s_, :, :], AF.Copy, [RCs[s_], RB[7], rgm], [R_cb[h]], scale=CSC[:, col0 + h:col0 + h + 1])
                    for h in range(4):
                        s_ = dr * 4 + h
                        pp, rpp = psum()
                        mm(pp[0:64, 0:257], spT[:, h, :], vaug[:, h, :], True, False, [R_sp[h], R_va], [rpp])
                        mm(pp[0:64, 0:257], qT[:, h * 2, tsl], cbf[:, h, 0, :], False, False, [Rq, R_cb[h]], [rpp])
                        mm(pp[0:64, 0:257], qT[:, h * 2 + 1, tsl], cbf[:, h, 1, :], False, True, [Rq, R_cb[h]], [rpp])
                        rr = rsm_t[:, h * 2:h * 2 + 1]
                        act(rr, pp[0:64, 256:257], AF.Abs, [rpp], [R_rs[h]])
                        ts(rr, rr, THR[0:64, col0 + h:col0 + h + 1], None, ALU.max, None, [R_rs[h], rgm], [R_rs[h]])
                        S.add("dve", (lambda o: (lambda e: e.reciprocal(o, o)))(rr), [R_rs[h]], [R_rs[h]])
                        hv_ = hs(c, h)
                        stt(hv_, pp[0:64, 0:256], rr, hv_, ALU.mult, ALU.add, [rpp, R_rs[h], Rhc[c], Rh[c // 8]], [Rhc[c]])
                        for kt2 in range(2):
                            pc, rpc = psum()
                            mm(pc[:, 0:257], kp[:, h, kt2 * 128:(kt2 + 1) * 128], vaug[:, h, :], True, True, [R_kp[h], R_va], [rpc])
                            stt(Caug[:, s_, kt2, :], Caug[:, s_, kt2, :], DEC[:, col0 + h:col0 + h + 1], pc[:, 0:257], ALU.mult, ALU.add, [RCs[s_], RB[7], rgm, rpc], [RCs[s_]])
            if g == 1:
                for dr in range(2):
                    for h in range(4):
                        dma("sp", stC[q, l, dr, h].rearrange("(k p) e -> p k e", p=128), Caug[:, dr * 4 + h, :, 0:256], [RCs[dr * 4 + h], RB[7]], [Res()])
                        dma("sp", stn[q, l, dr, h].rearrange("(k p o) -> p k o", p=128, o=1), Caug[:, dr * 4 + h, :, 256:257], [RCs[dr * 4 + h], RB[7]], [Res()], nonc=True)

        hcT = bf(BIG[0])[:, 0:8 * T].rearrange("p (f t) -> p f t", t=T)
        mnw = prm[:, P_MNW:P_MNW + 8]
        for c in range(16):
            hb = hsum if c < 8 else hsum2
            hc_ = hb[:, (c % 8) * 1024:(c % 8 + 1) * 1024]
            sqc = SC[c % 2][0:64, :]
            rq = RS[c % 2]
            act(sqc, hc_, AF.Square, [Rhc[c], Rh[c // 8]], [rq])
            S.add("dve", (lambda o, a: (lambda e: e.tensor_reduce(o, a, AX.X, ALU.add)))(hn_s[:, c, :], sqc.rearrange("p (h e) -> p h e", e=256)), [rq], [R_hn])
            ts(hn_s[:, c, :], hn_s[:, c, :], 1.0 / DH, EPS, ALU.mult, ALU.add, [R_hn], [R_hn])
            act(hn_s[:, c, :], hn_s[:, c, :], AF.Sqrt, [R_hn], [R_hn])
            S.add("dve", (lambda o: (lambda e: e.reciprocal(o, o)))(hn_s[:, c, :]), [R_hn], [R_hn])
            tt(sqc.rearrange("p (h e) -> p h e", e=256), hc_.rearrange("p (h e) -> p h e", e=256),
               hn_s[:, c, :].unsqueeze(2).to_broadcast([64, 4, 256]), ALU.mult, [Rhc[c], Rh[c // 8], R_hn], [rq])
            p, rp = psum()
            for ft in range(8):
                tr(p[:, ft * 64:(ft + 1) * 64], sqc[:, ft * 128:(ft + 1) * 128], ident[0:64, 0:64], [rq, R_c], [rp])
            tt(hcT[:, :, c * 64:(c + 1) * 64], p[:, :].rearrange("p (f t) -> p f t", t=64),
               mnw.unsqueeze(2).to_broadcast([128, 8, 64]), ALU.mult, [rp, R_prm], [RB[0]])

    def ffn(l, g):
        hv, hr = hT_views(0, 1, fine=True)
        rhs = lambda kt, tb: (hv[kt][:, tb * 512:(tb + 1) * 512], [hr[kt]])

        def actv(j):
            b = 2 + j // 8
            return bf(BIG[b])[:, (j % 8) * T:(j % 8 + 1) * T], RB[b]
        fcw = prm[:, P_FCW:P_FCW + 132].rearrange("p (f k) -> p f k", k=3)
        u_sb, g_sb, gc, t1 = SC[0], SC[1], SC[2], SC[3]
        for qb in range(11):
            def ev_u(ft, tb, p, rp):
                if tb == 0:
                    cpy(u_sb[:, 0:512], p[:, :], [rp], [RS[0]])
                else:
                    act(u_sb[:, 512:1024], p[:, :], AF.Copy, [rp], [RS[0]])

            def ev_g(ft, tb, p, rp):
                if tb == 0:
                    act(g_sb[:, 0:512], p[:, :], AF.Copy, [rp], [RS[1]])
                else:
                    cpy(g_sb[:, 512:1024], p[:, :], [rp], [RS[1]])
            vu, ru = load_panel(w_ffn_up[l][:, qb * 512:(qb + 1) * 512], 16, 512)
            vg, rg_ = load_panel(w_ffn_up[l][:, DFF + qb * 512:DFF + (qb + 1) * 512], 16, 512)
            for ft in range(4):
                j = qb * 4 + ft
                for (vv, rv, evf) in ((vu, ru, ev_u), (vg, rg_, ev_g)):
                    for tb in range(2):
                        p, rp = psum()
                        for kt in range(16):
                            mm(p[:, :], vv[:, kt, ft * 128:(ft + 1) * 128], hv[kt][:, tb * 512:(tb + 1) * 512], kt == 0, kt == 15, [rv, hr[kt]], [rp])
                        evf(ft, tb, p, rp)
                ts(gc[:, :], g_sb[:, :], fcw[:, j, 1:2], prm[:, P_FCB + j:P_FCB + j + 1], ALU.mult, ALU.add, [RS[1], R_prm], [RS[2]])
                for k in (0, 2):
                    d = k - 1
                    dv, _ = shifted(gc[:, :], g, "col", d)
                    _, sv = shifted(g_sb[:, :], g, "col", d)
                    stt(dv, sv, fcw[:, j, k:k + 1], dv, ALU.mult, ALU.add, [RS[1], RS[2], R_prm], [RS[2]])
                act(t1[:, :], gc[:, :], AF.Square, [RS[2]], [RS[3]])
                ts(t1[:, :], t1[:, :], 0.044715, 1.0, ALU.mult, ALU.add, [RS[3]], [RS[3]])
                tt(t1[:, :], t1[:, :], gc[:, :], ALU.mult, [RS[3], RS[2]], [RS[3]])
                act(t1[:, :], t1[:, :], AF.Sigmoid, [RS[3]], [RS[3]], scale=1.5957691216057308)
                tt(gc[:, :], gc[:, :], u_sb[:, :], ALU.mult, [RS[2], RS[0]], [RS[2]])
                av, ar = actv(j)
                tt(av, gc[:, :], t1[:, :], ALU.mult, [RS[2], RS[3]], [ar])
        claim(0)
        claim(1)
        for db in range(4):
            stage = BIG[0][:, 0:4 * T].rearrange("p (f t) -> p f t", t=T)
            tstage = BIG[1][:, 0:8 * 512].rearrange("p (i n) -> p i n", n=512)
            for dj in range(4):
                dt_ = db * 4 + dj

                def ev(ft, tb, p, rp, dj=dj):
                    if tb == 0:
                        cpy(stage[:, dj, 0:512], p[:, :], [rp], [RB[0]])
                    else:
                        act(stage[:, dj, 512:1024], p[:, :], AF.Copy, [rp], [RB[0]])
                linear(w_ffn_down[l][:, dt_ * 128:(dt_ + 1) * 128], 44, 128, lambda kt, tb: (actv(kt)[0][:, tb * 512:(tb + 1) * 512], [actv(kt)[1]]), ev)
            out_block_to_fd(stage, RB[0], tstage, RB[1], db * 4)

    for l in range(depth):
        mod_phase(l)
    for g in GORDER:
        load_params(0)
        fused_pass(g, None, (0, 0))
        for l in range(depth):
            mixer(l, g)
            fused_pass(g, (l, 0), (l, 1))
            ffn(l, g)
            if l + 1 < depth:
                load_params(l + 1)
                fused_pass(g, (l, 1), (l + 1, 0))
            else:
                fused_pass(g, (l, 1), None)
    S.emit(nc, es)
    es.close()
    return nc


_CACHE = {}


def _consts():
    ident = np.eye(128, dtype=np.float32)
    s = np.arange(64)
    triU = (s[:, None] <= s[None, :]).astype(np.float32)
    triL = (s[:, None] >= s[None, :]).astype(np.float32)
    inv = np.zeros((2, 4, T), np.float32)
    for gi, L in enumerate((16, 256)):
        t = np.arange(L)
        for wi, w in enumerate(POOLW):
            lo = np.clip(t - w // 2, 0, L)
            hi = np.clip(t - w // 2 + w, 0, L)
            ic = (1.0 / (hi - lo)).astype(np.float32)
            if gi == 0:
                inv[gi, wi] = np.repeat(ic, 64)
            else:
                inv[gi, wi] = np.tile(ic, 4)
    return ident, triU, triL, inv


def kernel(x_prompt, x_sample, state_C, state_n, state_m, c, c_ctx, w_ada, b_ada, norm_mix_pre, norm_mix_post,
           norm_ffn_pre, norm_ffn_post, w_in, b_in, conv_a_w, conv_a_b, ln_a_w, ln_a_b, w_a_out, w_pool,
           pool_scale, mlstm_norm_w, w_c_out, w_out, w_ffn_up, ffn_conv_w, ffn_conv_b, w_ffn_down, _depth=None,
           _cores=8):
    f = lambda a: np.ascontiguousarray(np.asarray(a, dtype=np.float32))
    depth = _depth or w_ada.shape[0]
    if depth not in _CACHE:
        _CACHE[depth] = build(depth)
    nc = _CACHE[depth]
    ident, triU, triL, inv = _consts()
    shared = {"w_ada": f(w_ada)[:depth], "b_ada": f(b_ada)[:depth], "norm_mix_pre": f(norm_mix_pre)[:depth],
              "norm_mix_post": f(norm_mix_post)[:depth], "norm_ffn_pre": f(norm_ffn_pre)[:depth],
              "norm_ffn_post": f(norm_ffn_post)[:depth], "w_in": f(w_in)[:depth], "b_in": f(b_in)[:depth],
              "conv_a_w": f(conv_a_w)[:depth], "conv_a_b": f(conv_a_b)[:depth], "ln_a_w": f(ln_a_w)[:depth],
              "ln_a_b": f(ln_a_b)[:depth], "w_a_out": f(w_a_out)[:depth], "w_pool": f(w_pool)[:depth],
              "pool_scale": f(pool_scale)[:depth], "mlstm_norm_w": f(mlstm_norm_w)[:depth],
              "w_c_out": f(w_c_out)[:depth], "w_out": f(w_out)[:depth], "w_ffn_up": f(w_ffn_up)[:depth],
              "ffn_conv_w": f(ffn_conv_w)[:depth], "ffn_conv_b": f(ffn_conv_b)[:depth],
              "w_ffn_down": f(w_ffn_down)[:depth], "c_ident": ident, "c_triU": triU, "c_triL": triL,
              "c_invcnt": inv}
    xp = f(x_prompt)
    xsm = f(x_sample)
    sC, sn, sm = f(state_C), f(state_n), f(state_m)
    cc, cctx = f(c), f(c_ctx)
    in_maps = []
    for core in range(_cores):
        b = core % 4
        m = dict(shared)
        m["xs"] = np.ascontiguousarray(np.concatenate([xsm[b], xp[core * 4:(core + 1) * 4].reshape(T, D)], axis=0))
        m["cond"] = np.ascontiguousarray(np.stack([cc[b], cctx], axis=0))
        m["C0"] = np.ascontiguousarray(sC[b][:depth])
        m["n0"] = np.ascontiguousarray(sn[b][:depth])
        m["m0"] = np.ascontiguousarray(sm[b][:depth].reshape(depth, 8))
        in_maps.append(m)
    res = run_bass_kernel_spmd(nc, in_maps, core_ids=list(range(_cores)))
    R = res.results
    B = x_prompt.shape[0]
    yp = np.zeros((B, 256, D), np.float32)
    ys = np.zeros((4, T, D), np.float32)
    nC = np.zeros((B, depth, 2, NH, DH, DH), np.float32)
    nn = np.zeros((B, depth, 2, NH, DH), np.float32)
    nm = np.zeros((B, depth, 2, NH), np.float32)
    for core in range(_cores):
        r = R[core]
        yy = np.asarray(r["y"])
        if core < 4:
            ys[core] = yy[0:T]
        yp[core * 4:(core + 1) * 4] = yy[T:2 * T].reshape(4, 256, D)
        nC[core * 4:(core + 1) * 4] = np.asarray(r["stC"])
        nn[core * 4:(core + 1) * 4] = np.asarray(r["stn"])
        nm[core * 4:(core + 1) * 4] = np.asarray(r["stm"]).reshape(4, depth, 2, NH)
    return yp, ys, nC, nn, nm
```

```python
import numpy as np
from contextlib import ExitStack
import concourse.bass as bass
import concourse.mybir as mybir
from concourse.bass_utils import run_bass_kernel_spmd

F32 = mybir.dt.float32
BF16 = mybir.dt.bfloat16
AF = mybir.ActivationFunctionType
ALU = mybir.AluOpType
AX = mybir.AxisListType

D = 2048
T = 1024
NT = 8
WA = 512
WB = 512
WC = 1024
NH = 4
DH = 256
DFF = 5632
NIN = 11792
OFF_POOL = 1024
OFF_QKV = 1536
OFF_OG = 4608
OFF_GATES = 5632
OFF_MERGE = 5648
EPS = 1e-6
POOLW = (2, 4, 8, 16)
BIGN = 4112


def _flat(x):
    out = []
    for a in x:
        if a is None:
            continue
        if isinstance(a, (list, tuple)):
            out.extend(_flat(a))
        else:
            out.append(a)
    return out


SAME_ENG_WAIT = True
GORDER = (0, 1)


class Res:
    __slots__ = ("lw", "rd", "dr")

    def __init__(self):
        self.lw = None
        self.rd = {}
        self.dr = []


class Op:
    __slots__ = ("eng", "fn", "deps", "dma", "sig", "sigval", "semk")

    def __init__(self, eng, fn, deps, dma):
        self.eng = eng
        self.fn = fn
        self.deps = deps
        self.dma = dma
        self.sig = False
        self.sigval = 0
        self.semk = None


class Sched:
    ENGS = ("pe", "act", "dve", "pool", "sp")
    NDS = 12

    def __init__(self):
        self.ops = []

    def add(self, eng, fn, reads=(), writes=(), dma=False):
        oid = len(self.ops)
        deps = set()
        reads = _flat(reads)
        writes = _flat(writes)
        for r in reads:
            if r.lw is not None:
                deps.add(r.lw)
        for w in writes:
            if w.lw is not None:
                deps.add(w.lw)
            deps.update(w.rd.values())
            deps.update(w.dr)
        self.ops.append(Op(eng, fn, deps, dma))
        for r in reads:
            if dma:
                r.dr.append(oid)
            else:
                r.rd[eng] = oid
        for w in writes:
            w.lw = oid
            w.rd = {}
            w.dr = []
        return oid

    def emit(self, nc, es):
        ops = self.ops
        for o in ops:
            if o.dma:
                o.sig = True
            for d in o.deps:
                od = ops[d]
                if od.dma:
                    continue
                if od.eng == o.eng and not o.dma and (o.eng == "pe" or not SAME_ENG_WAIT):
                    continue
                od.sig = True
        cnt = {e: 0 for e in self.ENGS}
        dcnt = {}
        dn = {e: 0 for e in self.ENGS}
        for o in ops:
            if o.dma:
                k = (o.eng, dn[o.eng] % self.NDS)
                dn[o.eng] += 1
                dcnt[k] = dcnt.get(k, 0) + 1
                o.semk = k
                o.sigval = 16 * dcnt[k]
            elif o.sig:
                cnt[o.eng] += 1
                o.sigval = cnt[o.eng]
        sems = {}
        for e in self.ENGS:
            sems[e] = es.enter_context(nc.semaphore("s_" + e))
        for k in dcnt:
            sems[k] = es.enter_context(nc.semaphore("d_%s_%d" % k))
        block = es.enter_context(nc.Block())
        handles = {"pe": block.tensor, "act": block.scalar, "dve": block.vector, "pool": block.gpsimd,
                   "sp": block.sync}

        def run_engine(ename):
            def body(h):
                seen = {}
                for o in ops:
                    if o.eng != ename:
                        continue
                    need = {}
                    for d in o.deps:
                        od = ops[d]
                        if od.dma:
                            key = od.semk
                        else:
                            if od.eng == ename and not o.dma and (ename == "pe" or not SAME_ENG_WAIT):
                                continue
                            key = od.eng
                        if need.get(key, 0) < od.sigval:
                            need[key] = od.sigval
                    if o.dma and o.sigval > 16:
                        if need.get(o.semk, 0) < o.sigval - 16:
                            need[o.semk] = o.sigval - 16
                    for key, v in need.items():
                        if seen.get(key, 0) < v:
                            h.wait_ge(sems[key], v)
                            seen[key] = v
                    ins = o.fn(h)
                    if o.dma:
                        ins.then_inc(sems[o.semk], 16)
                    elif o.sig:
                        ins.then_inc(sems[ename], 1)
                for k, c in dcnt.items():
                    if k[0] == ename:
                        h.wait_ge(sems[k], 16 * c)
            return body

        for e in self.ENGS:
            handles[e](run_engine(e))


def build(depth=4):
    nc = bass.Bass("TRN2", target_bir_lowering=False)
    S = Sched()
    es = ExitStack()

    def din(name, shape):
        return nc.dram_tensor(name, list(shape), F32, kind="ExternalInput").ap()

    def dout(name, shape):
        return nc.dram_tensor(name, list(shape), F32, kind="ExternalOutput").ap()

    xs = din("xs", [2 * T, D])
    cond = din("cond", [2, D])
    C0 = din("C0", [depth, 2, NH, DH, DH])
    n0 = din("n0", [depth, 2, NH, DH])
    m0 = din("m0", [depth, 8])
    w_ada = din("w_ada", [depth, D, 6 * D])
    b_ada = din("b_ada", [depth, 6 * D])
    nrm = {k: din(k, [depth, D]) for k in ("norm_mix_pre", "norm_mix_post", "norm_ffn_pre", "norm_ffn_post")}
    w_in = din("w_in", [depth, D, NIN])
    b_in = din("b_in", [depth, NIN])
    conv_a_w = din("conv_a_w", [depth, 31, WA])
    conv_a_b = din("conv_a_b", [depth, WA])
    ln_a_w = din("ln_a_w", [depth, WA])
    ln_a_b = din("ln_a_b", [depth, WA])
    w_a_out = din("w_a_out", [depth, WA, D])
    w_pool = din("w_pool", [depth, 4, 128, 512])
    pool_scale = din("pool_scale", [depth, D])
    mlstm_norm_w = din("mlstm_norm_w", [depth, WC])
    w_c_out = din("w_c_out", [depth, WC, D])
    w_out = din("w_out", [depth, D, D])
    w_ffn_up = din("w_ffn_up", [depth, D, 2 * DFF])
    ffn_conv_w = din("ffn_conv_w", [depth, 3, DFF])
    ffn_conv_b = din("ffn_conv_b", [depth, DFF])
    w_ffn_down = din("w_ffn_down", [depth, DFF, D])
    c_ident = din("c_ident", [128, 128])
    c_triU = din("c_triU", [64, 64])
    c_triL = din("c_triL", [64, 64])
    c_invcnt = din("c_invcnt", [2, 4, T])

    y = dout("y", [2 * T, D])
    stC = dout("stC", [4, depth, 2, NH, DH, DH])
    stn = dout("stn", [4, depth, 2, NH, DH])
    stm = dout("stm", [4, depth, 8])
    mod_d = dout("mod_d", [depth, 2, 6 * D])
    hT_d = dout("hT_d", [2, 128, 4096])
    R_hTd = Res()
    f_d = dout("f_d", [T, D])
    R_y = [[Res() for _ in range(NT)] for _ in range(2)]
    R_f = [Res() for _ in range(NT)]
    R_modp = {(l_, g_, pn_): Res() for l_ in range(depth) for g_ in range(2) for pn_ in range(24)}
    R_st = Res()

    def sb(name, shape, dt=F32):
        return es.enter_context(nc.sbuf_tensor(name, list(shape), dt))

    BIG = [sb("big%d" % i, [128, BIGN]) for i in range(8)]
    RBh = [[Res(), Res()] for _ in range(8)]
    RB = [tuple(h) for h in RBh]
    WBUF = [sb("wb%d" % i, [128, 8192], BF16) for i in range(2)]
    RW = [Res() for _ in range(2)]
    SC = [sb("sc%d" % i, [128, T]) for i in range(4)]
    RS = [Res() for _ in range(4)]
    junk = sb("junk", [128, 2048], BF16)
    R_junk = Res()
    PS = [es.enter_context(nc.psum_tensor("ps%d" % i, [128, 512], F32)) for i in range(8)]
    RP = [Res() for _ in range(8)]
    psn = [0]

    def psum():
        i = psn[0] % 8
        psn[0] += 1
        return PS[i], RP[i]

    wbn = [0]

    def bf(t, n=None):
        a = t[:].bitcast(BF16)
        return a

    def mm(out, lhsT, rhs, start, stop, R, W):
        S.add("pe", lambda e: e.matmul(out, lhsT, rhs, start=start, stop=stop), R, W)

    def tr(out, in_, ident, R, W):
        S.add("pe", lambda e: e.transpose(out, in_, ident), R, W)

    def act(out, in_, func, R, W, bias=None, scale=None, accum=None):
        kw = {}
        if bias is not None:
            kw["bias"] = bias
        if scale is not None:
            kw["scale"] = scale
        if accum is not None:
            kw["accum_out"] = accum
        S.add("act", lambda e: e.activation(out, in_, func, **kw), R, W)

    def tt(out, a, b, op, R, W, eng="dve"):
        S.add(eng, lambda e: e.tensor_tensor(out, a, b, op), R, W)

    def ts(out, a, s1, s2, op0, op1, R, W, eng="dve"):
        if op1 is None:
            S.add(eng, lambda e: e.tensor_scalar(out, a, s1, None, op0), R, W)
        else:
            S.add(eng, lambda e: e.tensor_scalar(out, a, s1, s2, op0, op1), R, W)

    def stt(out, a, s, b, op0, op1, R, W):
        S.add("dve", lambda e: e.scalar_tensor_tensor(out, a, s, b, op0, op1), R, W)

    def cpy(out, in_, R, W, eng="dve"):
        S.add(eng, lambda e: e.tensor_copy(out, in_), R, W)

    def mset(ap, v, W, eng="dve"):
        S.add(eng, lambda e: e.memset(ap, v), (), W)

    def dma(q, out, in_, R, W, nonc=False):
        if nonc:
            S.add(q, lambda e: e.dma_start(out=out, in_=in_, allow_slow_non_contiguous=True), R, W, dma=True)
        else:
            S.add(q, lambda e: e.dma_start(out=out, in_=in_), R, W, dma=True)

    gm = sb("gm", [128, 14, 128]); rgm = Res()
    G = sb("G", [64, 256]); rG = Res()
    acol = sb("acol", [128, 1])
    mcur_all = sb("mcur", [128, 2, 4, 4])
    rsm_t = sb("rsm_t", [64, 8])
    hn_s = sb("hn_s", [64, 16, 4]); R_hn = Res()
    wpool_sb = sb("wpool", [128, 4, 512], BF16); r_wp = Res()
    small = sb("small", [128, 8, 8]); rsm = [Res() for _ in range(8)]
    R_kp = [Res() for _ in range(4)]; R_va = Res(); R_sp = [Res() for _ in range(4)]
    R_cb = [Res() for _ in range(4)]; R_rs = [Res() for _ in range(4)]
    RCs = [Res() for _ in range(8)]; Rhc = [Res() for _ in range(16)]
    ident = sb("ident", [128, 128])
    identb = sb("identb", [128, 128], BF16)
    ones = sb("ones", [128, 128])
    triU = sb("triU", [64, 64])
    triL = sb("triL", [64, 64])
    R_c = Res()
    dma("sp", ident[:], c_ident, (), [R_c])
    dma("sp", triU[:], c_triU, (), [R_c])
    dma("sp", triL[:], c_triL, (), [R_c])
    cpy(identb[:], ident[:], [R_c], [R_c])
    mset(ones[:], 1.0, [R_c])

    scT = sb("scT", [128, 16, 64], BF16)
    R_scT = Res()
    cnd = SC[0]
    mset(BIG[0][0:64, 0:D], 0.0, [RB[0]])
    dma("sp", BIG[0][0:1, 0:D], cond[0:1, :], (), [RB[0]])
    dma("sp", BIG[0][32:33, 0:D], cond[1:2, :], (), [RB[0]])
    act(BIG[0][0:64, 0:D], BIG[0][0:64, 0:D], AF.Silu, [RB[0]], [RB[0]])
    for kt in range(16):
        p, rp = psum()
        tr(p[:, 0:64], BIG[0][0:64, kt * 128:(kt + 1) * 128], ident[0:64, 0:64], [RB[0], R_c], [rp])
        cpy(scT[:, kt, :], p[:, 0:64], [rp], [R_scT])

    prm = sb("prm", [128, 640])
    R_prm = Res()
    P_BIN, P_CAW, P_CAB, P_LNW, P_LNB, P_PSC, P_MNW, P_FCW, P_FCB, P_BQ = 0, 92, 216, 220, 224, 228, 244, 252, 384, 428
    gbias = sb("gbias", [16, 1])

    def load_rows_T(dst_ap_fn, src2d, nrows, ncolt, stage, rstage):
        for c0 in range(0, ncolt, 8):
            nc_ = min(8, ncolt - c0)
            stage, rstage = SC[2 + (c0 // 8) % 2], RS[2 + (c0 // 8) % 2]
            dma("sp", stage[0:nrows, 0:nc_ * 128], src2d[:, c0 * 128:(c0 + nc_) * 128], (), [rstage])
            for c in range(nc_):
                p, rp = psum()
                tr(p[:, 0:nrows], stage[0:nrows, c * 128:(c + 1) * 128], ident[0:nrows, 0:nrows], [rstage, R_c], [rp])
                cpy(dst_ap_fn(c0 + c), p[:, 0:nrows], [rp], [R_prm])

    def load_params(l):
        st, rs = None, None
        load_rows_T(lambda c: prm[:, P_BIN:P_BIN + 44], b_in[l, 0:5632].rearrange("(r c) -> r c", c=128), 44, 1, st, rs)
        load_rows_T(lambda c: prm[:, P_BIN + 44:P_BIN + 92], b_in[l, OFF_MERGE:NIN].rearrange("(r c) -> r c", c=128), 48, 1, st, rs)
        dma("sp", gbias[:], b_in[l, OFF_GATES:OFF_MERGE].rearrange("(p o) -> p o", o=1), (), [R_prm])
        pv = prm[:, P_CAW:P_CAW + 124].rearrange("p (f k) -> p f k", k=31)
        load_rows_T(lambda c: pv[:, c, :], conv_a_w[l], 31, 4, st, rs)
        load_rows_T(lambda c: prm[:, P_CAB:P_CAB + 4], conv_a_b[l].rearrange("(r c) -> r c", c=128), 4, 1, st, rs)
        load_rows_T(lambda c: prm[:, P_LNW:P_LNW + 4], ln_a_w[l].rearrange("(r c) -> r c", c=128), 4, 1, st, rs)
        load_rows_T(lambda c: prm[:, P_LNB:P_LNB + 4], ln_a_b[l].rearrange("(r c) -> r c", c=128), 4, 1, st, rs)
        load_rows_T(lambda c: prm[:, P_PSC:P_PSC + 16], pool_scale[l].rearrange("(r c) -> r c", c=128), 16, 1, st, rs)
        load_rows_T(lambda c: prm[:, P_MNW:P_MNW + 8], mlstm_norm_w[l].rearrange("(r c) -> r c", c=128), 8, 1, st, rs)
        fv = prm[:, P_FCW:P_FCW + 132].rearrange("p (f k) -> p f k", k=3)
        load_rows_T(lambda c: fv[:, c, :], ffn_conv_w[l][:, 0:2816], 3, 22, st, rs)
        load_rows_T(lambda c: fv[:, 22 + c, :], ffn_conv_w[l][:, 2816:5632], 3, 22, st, rs)
        load_rows_T(lambda c: prm[:, P_FCB:P_FCB + 44], ffn_conv_b[l].rearrange("(r c) -> r c", c=128), 44, 1, st, rs)
        ts(prm[:, P_BQ:P_BQ + 8], prm[:, P_BIN + 12:P_BIN + 20], 1.0 / 16.0, None, ALU.mult, None, [R_prm], [R_prm])

    def load_panel(src, KT, ncols):
        i = wbn[0] % 2
        wbn[0] += 1
        v = WBUF[i][:, 0:KT * ncols].rearrange("p (k n) -> p k n", n=ncols)
        dma("pool", v, src.rearrange("(k p) n -> p k n", p=128), (), [RW[i]])
        return v, RW[i]

    def linear(src, KT, ncols, rhs_fn, evac_fn, ft0=0):
        v, rw = load_panel(src, KT, ncols)
        for ft in range(ncols // 128):
            for tb in range(2):
                p, rp = psum()
                for kt in range(KT):
                    ra, rr = rhs_fn(kt, tb)
                    mm(p[:, :], v[:, kt, ft * 128:(ft + 1) * 128], ra, kt == 0, kt == KT - 1, [rw] + rr, [rp])
                evac_fn(ft0 + ft, tb, p, rp)

    def mod_phase(l):
        for pn in range(24):
            stg, rsg = SC[(pn % 2) * 2], RS[(pn % 2) * 2]
            bst, rbs = SC[(pn % 2) * 2 + 1], RS[(pn % 2) * 2 + 1]
            v, rw = load_panel(w_ada[l, :, pn * 512:(pn + 1) * 512], 16, 512)
            p, rp = psum()
            for kt in range(16):
                mm(p[0:64, :], scT[:, kt, :], v[:, kt, :], kt == 0, kt == 15, [rw, R_scT], [rp])
            dma("sp", bst[0:1, 0:512], b_ada[l, pn * 512:(pn + 1) * 512].rearrange("(o n) -> o n", o=1), (), [rbs])
            dma("sp", bst[32:33, 0:512], b_ada[l, pn * 512:(pn + 1) * 512].rearrange("(o n) -> o n", o=1), (), [rbs])
            tt(stg[0:1, 0:512], p[0:1, :], bst[0:1, 0:512], ALU.add, [rp, rbs], [rsg])
            tt(stg[32:33, 0:512], p[32:33, :], bst[32:33, 0:512], ALU.add, [rp, rbs], [rsg])
            dma("sp", mod_d[l, 0:1, pn * 512:(pn + 1) * 512], stg[0:1, 0:512], [rsg], [R_modp[(l, 0, pn)]])
            dma("sp", mod_d[l, 1:2, pn * 512:(pn + 1) * 512], stg[32:33, 0:512], [rsg], [R_modp[(l, 1, pn)]])

    def bcast_load(dst, rdst, vec, modsel=None):
        rr_ = [] if modsel is None else [R_modp[(modsel[0], modsel[1], modsel[2] * 4 + k_)] for k_ in range(4)]
        dma("sp", dst, vec.partition_broadcast(128), rr_, [rdst])

    RH = {0: [Res() for _ in range(8)], 1: [Res() for _ in range(8)]}
    clm = sb("clm", [128, 2])

    def claim(b):
        mset(clm[:, 0:1], 0.0, [RB[b], RH[b]])

    def hT_views(b0, b1, fine=False):
        hv, hr = [], []
        for kt in range(16):
            b = b0 if kt < 8 else b1
            hv.append(bf(BIG[b])[:, (kt % 8) * T:(kt % 8 + 1) * T])
            hr.append(RH[b][kt % 8] if fine else RB[b])
        return hv, hr

    def fused_pass(g, fin, nrmsel):
        gam = BIG[7][:, 0:D]
        shf = BIG[7][:, D:2 * D]
        pg = BIG[2][:, 0:D]
        tmpb = BIG[2][:, D:2 * D]
        rb = RB[7]
        rpg = RB[2]
        if fin is not None:
            lf, wf = fin
            gi = 2 if wf == 0 else 5
            nwf = nrm["norm_mix_post" if wf == 0 else "norm_ffn_post"]
            bcast_load(pg, rpg, mod_d[lf, g, gi * D:(gi + 1) * D], (lf, g, gi))
            bcast_load(tmpb, rpg, nwf[lf])
            tt(pg, pg, tmpb, ALU.mult, [rpg], [rpg])
            if lf == 0 and wf == 0:
                xsrc = [(xs[g * T + i * 128:g * T + (i + 1) * 128, :], []) for i in range(NT)]
            else:
                xsrc = [(y[g * T + i * 128:g * T + (i + 1) * 128, :], [R_y[g][i]]) for i in range(NT)]
        else:
            xsrc = [(xs[g * T + i * 128:g * T + (i + 1) * 128, :], []) for i in range(NT)]
        if nrmsel is not None:
            ln_, wn = nrmsel
            si, ci = (0, 1) if wn == 0 else (3, 4)
            nw = nrm["norm_mix_pre" if wn == 0 else "norm_ffn_pre"]
            bcast_load(gam, rb, mod_d[ln_, g, ci * D:(ci + 1) * D], (ln_, g, ci))
            bcast_load(shf, rb, nw[ln_])
            stt(gam, gam, 1.0, shf, ALU.add, ALU.mult, [rb], [rb])
            bcast_load(shf, rb, mod_d[ln_, g, si * D:(si + 1) * D], (ln_, g, si))
            claim(0)
            claim(1)
            hviews, hres = hT_views(0, 1, fine=True)
        xbuf = [(BIG[6][:, 0:D], RBh[6][0]), (BIG[6][:, D:2 * D], RBh[6][1]), (BIG[5][:, 0:D], RBh[5][0]), (BIG[5][:, D:2 * D], RBh[5][1])]
        fbuf = [(BIG[4][:, 0:D], RBh[4][0]), (BIG[4][:, D:2 * D], RBh[4][1]), (BIG[3][:, 0:D], RBh[3][0]), (BIG[3][:, D:2 * D], RBh[3][1])]
        def issue_loads(i):
            x_t, rx = xbuf[i % 4]
            f_t, rf = fbuf[i % 4]
            dma("sp", x_t, xsrc[i][0], xsrc[i][1], [rx])
            if fin is not None:
                dma("sp", f_t, f_d[i * 128:(i + 1) * 128, :], [R_f[i]], [rf])
        for i in range(4):
            issue_loads(i)
        for i in range(NT):
            x_t, rx = xbuf[i % 4]
            f_t, rf = fbuf[i % 4]
            ri = rsm[i]
            if fin is not None:
                act(junk[:, :], f_t, AF.Square, [rf], [ri], accum=small[:, i, 0:1])
                act(small[:, i, 1:2], small[:, i, 0:1], AF.Sqrt, [ri], [ri], scale=1.0 / D, bias=EPS)
                S.add("dve", (lambda o, a: (lambda e: e.reciprocal(o, a)))(small[:, i, 2:3], small[:, i, 1:2]), [ri], [ri])
                stt(f_t, f_t, small[:, i, 2:3], pg, ALU.mult, ALU.mult, [rf, ri, rpg], [rf])
                tt(f_t, f_t, x_t, ALU.add, [rf, rx], [rf])
                dma("sp", y[g * T + i * 128:g * T + (i + 1) * 128, :], f_t, [rf], [R_y[g][i]])
                cur, rcur = f_t, rf
            else:
                cur, rcur = x_t, rx
            if nrmsel is not None:
                act(junk[:, :], cur, AF.Square, [rcur], [ri], accum=small[:, i, 4:5])
                act(small[:, i, 5:6], small[:, i, 4:5], AF.Sqrt, [ri], [ri], scale=1.0 / D, bias=EPS)
                S.add("dve", (lambda o, a: (lambda e: e.reciprocal(o, a)))(small[:, i, 6:7], small[:, i, 5:6]), [ri], [ri])
                stt(x_t, cur, small[:, i, 6:7], gam, ALU.mult, ALU.mult, [rcur, ri, rb], [rx])
                tt(x_t, x_t, shf, ALU.add, [rx, rb], [rx])
                for q4 in range(4):
                    p, rp = psum()
                    for j in range(4):
                        kt = q4 * 4 + j
                        tr(p[:, j * 128:(j + 1) * 128], x_t[:, kt * 128:(kt + 1) * 128], ident[:, :], [rx, R_c], [rp])
                    b_ = q4 // 2
                    k0 = (q4 % 2) * 4
                    dst = bf(BIG[b_])[:, 0:8 * T].rearrange("p (k t) -> p k t", t=T)[:, k0:k0 + 4, i * 128:(i + 1) * 128]
                    src = p[:, :].rearrange("p (k t) -> p k t", t=128)
                    if q4 % 2 == 0:
                        cpy(dst, src, [rp], [RH[b_][k0:k0 + 4]])
                    else:
                        act(dst, src, AF.Copy, [rp], [RH[b_][k0:k0 + 4]])
            if i + 4 < NT:
                issue_loads(i + 4)
        if nrmsel is not None and nrmsel[1] == 0:
            dma("sp", hT_d[0], BIG[0][:, 0:4096], [RH[0]], [R_hTd])
            dma("sp", hT_d[1], BIG[1][:, 0:4096], [RH[1]], [R_hTd])

    def out_block_to_fd(stage, rstage, tstage, rtstage, d0):
        sv = stage
        tv = tstage
        for i in range(NT):
            p, rp = psum()
            for j in range(4):
                tr(p[:, j * 128:(j + 1) * 128], sv[:, j, i * 128:(i + 1) * 128], ident[:, :], [rstage, R_c], [rp])
            if i % 2 == 0:
                cpy(tv[:, i, :], p[:, :], [rp], [rtstage])
            else:
                act(tv[:, i, :], p[:, :], AF.Copy, [rp], [rtstage])
            dma("sp", f_d[i * 128:(i + 1) * 128, d0 * 128:d0 * 128 + 512], tv[:, i, :], [rtstage], [R_f[i]])

    def shifted(a, g, kind, d):
        if g == 0:
            a3 = a.rearrange("p (r c) -> p r c", c=64)
            if kind == "row":
                lo, hi = max(0, -d), 64 - max(0, d)
                return a3[:, :, lo:hi], a3[:, :, lo + d:hi + d]
            lo, hi = max(0, -d), 16 - max(0, d)
            return a3[:, lo:hi, :], a3[:, lo + d:hi + d, :]
        a3 = a.rearrange("p (r c) -> p r c", c=256)
        lo, hi = max(0, -d), 256 - max(0, d)
        return a3[:, :, lo:hi], a3[:, :, lo + d:hi + d]

    def mixer(l, g):
        wl = w_in[l]
        hv, hr = hT_views(0, 1, fine=True)

        def rhs_h(hv, hr):
            return lambda kt, tb: (hv[kt][:, tb * 512:(tb + 1) * 512], [hr[kt]])

        qkvT = [bf(BIG[2 + j])[:, 0:8 * T].rearrange("p (f t) -> p f t", t=T) for j in range(3)]
        for j in range(3):
            for pn in range(2):
                c0 = OFF_QKV + j * WC + pn * 512

                def ev(ft, tb, p, rp, j=j):
                    bt = OFF_QKV // 128 + j * 8 + ft
                    dst = qkvT[j][:, ft, tb * 512:(tb + 1) * 512]
                    if j == 0:
                        act(dst, p[:, :], AF.Identity, [rp, R_prm], [RB[2]], bias=prm[:, P_BQ + ft:P_BQ + ft + 1], scale=1.0 / 16.0)
                    elif (ft + tb) % 2 == 0:
                        act(dst, p[:, :], AF.Identity, [rp, R_prm], [RB[2 + j]], bias=prm[:, P_BIN + bt:P_BIN + bt + 1])
                    else:
                        ts(dst, p[:, :], prm[:, P_BIN + bt:P_BIN + bt + 1], None, ALU.add, None, [rp, R_prm], [RB[2 + j]])
                linear(wl[:, c0:c0 + 512], 16, 512, rhs_h(hv, hr), ev, ft0=pn * 4)
        gT = SC[3]
        i = wbn[0] % 2
        wbn[0] += 1
        gv = WBUF[i][:, 0:256].rearrange("p (k n) -> p k n", n=16)
        dma("pool", gv, wl[:, OFF_GATES:OFF_MERGE].rearrange("(k p) n -> p k n", p=128), (), [RW[i]], nonc=True)
        for tb in range(2):
            p, rp = psum()
            for kt in range(16):
                mm(p[0:16, :], gv[:, kt, :], hv[kt][:, tb * 512:(tb + 1) * 512], kt == 0, kt == 15, [RW[i], hr[kt]], [rp])
            act(gT[0:16, tb * 512:(tb + 1) * 512], p[0:16, :], AF.Identity, [rp, R_prm], [RS[3]], bias=gbias[:, 0:1])

        claim(0)
        claim(1)
        mlstm(l, g, qkvT, gT)

        hv, hr = hT_views(1, 2)
        dma("sp", BIG[1][:, 0:4096], hT_d[0], [R_hTd], [RB[1], R_kp, R_va, R_sp, R_cb])
        dma("sp", BIG[2][:, 0:4096], hT_d[1], [R_hTd], [RB[2]])
        hcT = bf(BIG[0])[:, 0:8 * T].rearrange("p (f t) -> p f t", t=T)

        for pn in range(2):
            c0 = OFF_OG + pn * 512

            def ev(ft, tb, p, rp):
                bt = OFF_OG // 128 + ft
                tmp = SC[(ft + tb) % 2][:, 0:512]
                rt = RS[(ft + tb) % 2]
                act(tmp, p[:, :], AF.Sigmoid, [rp, R_prm], [rt], bias=prm[:, P_BIN + bt:P_BIN + bt + 1])
                tt(hcT[:, ft, tb * 512:(tb + 1) * 512], hcT[:, ft, tb * 512:(tb + 1) * 512], tmp, ALU.mult, [rt, RB[0]], [RB[0]])
            linear(wl[:, c0:c0 + 512], 16, 512, rhs_h(hv, hr), ev, ft0=pn * 4)

        ga = BIG[3][:, 0:4 * T].rearrange("p (f t) -> p f t", t=T)
        gb = BIG[4][:, 0:4 * T].rearrange("p (f t) -> p f t", t=T)
        acc = BIG[5][:, 0:4 * T].rearrange("p (f t) -> p f t", t=T)
        sq = BIG[6][:, 0:4 * T].rearrange("p (f t) -> p f t", t=T)
        aT = bf(BIG[7])[:, 0:4 * T].rearrange("p (f t) -> p f t", t=T)
        pT = bf(BIG[7])[:, 4 * T:8 * T].rearrange("p (f t) -> p f t", t=T)
        for pn in range(2):
            def ev(ft, tb, p, rp):
                if ft < 4:
                    act(ga[:, ft, tb * 512:(tb + 1) * 512], p[:, :], AF.Identity, [rp, R_prm], [RB[3]], bias=prm[:, P_BIN + ft:P_BIN + ft + 1])
                else:
                    act(gb[:, ft - 4, tb * 512:(tb + 1) * 512], p[:, :], AF.Sigmoid, [rp, R_prm], [RB[4]], bias=prm[:, P_BIN + ft:P_BIN + ft + 1])
            linear(wl[:, pn * 512:(pn + 1) * 512], 16, 512, rhs_h(hv, hr), ev, ft0=pn * 4)
        caw = prm[:, P_CAW:P_CAW + 124].rearrange("p (f k) -> p f k", k=31)
        for ft in range(4):
            tt(ga[:, ft, :], ga[:, ft, :], gb[:, ft, :], ALU.mult, [RB[3], RB[4]], [RB[3]])
            ts(acc[:, ft, :], ga[:, ft, :], caw[:, ft, 15:16], prm[:, P_CAB + ft:P_CAB + ft + 1], ALU.mult, ALU.add, [RB[3], R_prm], [RB[5]])
            for k in range(31):
                d = k - 15
                if d == 0:
                    continue
                dv, _ = shifted(acc[:, ft, :], g, "row", d)
                _, sv = shifted(ga[:, ft, :], g, "row", d)
                stt(dv, sv, caw[:, ft, k:k + 1], dv, ALU.mult, ALU.add, [RB[3], RB[5], R_prm], [RB[5]])
            act(sq[:, ft, :], acc[:, ft, :], AF.Square, [RB[5]], [RB[6]])
        for tb in range(2):
            p1, r1 = psum()
            p2, r2 = psum()
            for ft in range(4):
                mm(p1[:, :], ones[:, :], acc[:, ft, tb * 512:(tb + 1) * 512], ft == 0, ft == 3, [R_c, RB[5]], [r1])
            for ft in range(4):
                mm(p2[:, :], ones[:, :], sq[:, ft, tb * 512:(tb + 1) * 512], ft == 0, ft == 3, [R_c, RB[6]], [r2])
            mean = SC[0][:, 0:512]
            var = SC[1][:, 0:512]
            act(mean, p1[:, :], AF.Copy, [r1], [RS[0]], scale=1.0 / WA)
            tt(var, mean, mean, ALU.mult, [RS[0]], [RS[1]])
            stt(var, p2[:, :], 1.0 / WA, var, ALU.mult, ALU.subtract, [r2, RS[1]], [RS[1]])
            ts(var, var, EPS, None, ALU.add, None, [RS[1]], [RS[1]])
            act(var, var, AF.Sqrt, [RS[1]], [RS[1]])
            S.add("dve", (lambda o: (lambda e: e.reciprocal(o, o)))(var), [RS[1]], [RS[1]])
            for ft in range(4):
                tmp = SC[2][:, 0:512]
                tt(tmp, acc[:, ft, tb * 512:(tb + 1) * 512], mean, ALU.subtract, [RB[5], RS[0]], [RS[2]])
                tt(tmp, tmp, var, ALU.mult, [RS[2], RS[1]], [RS[2]])
                act(aT[:, ft, tb * 512:(tb + 1) * 512], tmp, AF.Silu, [RS[2], R_prm], [RB[7]],
                    bias=prm[:, P_LNB + ft:P_LNB + ft + 1], scale=prm[:, P_LNW + ft:P_LNW + ft + 1])

        zp = BIG[3][:, 0:4 * T].rearrange("p (f t) -> p f t", t=T)
        pacc = BIG[4][:, 0:4 * T].rearrange("p (f t) -> p f t", t=T)

        def ev(ft, tb, p, rp):
            bt = OFF_POOL // 128 + ft
            act(zp[:, ft, tb * 512:(tb + 1) * 512], p[:, :], AF.Identity, [rp, R_prm], [RB[3]], bias=prm[:, P_BIN + bt:P_BIN + bt + 1])
        linear(wl[:, OFF_POOL:OFF_POOL + 512], 16, 512, rhs_h(hv, hr), ev)
        for ft in range(4):
            w = POOLW[ft]
            cpy(pacc[:, ft, :], zp[:, ft, :], [RB[3]], [RB[4]])
            for d in range(-(w // 2), w // 2):
                if d == 0:
                    continue
                dv, _ = shifted(pacc[:, ft, :], g, "col", d)
                _, sv = shifted(zp[:, ft, :], g, "col", d)
                tt(dv, dv, sv, ALU.add, [RB[3], RB[4]], [RB[4]])
            ic = SC[0]
            dma("sp", ic[:, :], c_invcnt[g, ft].partition_broadcast(128), (), [RS[0]])
            tt(pacc[:, ft, :], pacc[:, ft, :], ic[:, :], ALU.mult, [RB[4], RS[0]], [RB[4]])
            tt(pT[:, ft, :], pacc[:, ft, :], zp[:, ft, :], ALU.subtract, [RB[4], RB[3]], [RB[7]])

        def mixv(dt_):
            b = 3 if dt_ < 8 else 4
            return bf(BIG[b])[:, (dt_ % 8) * T:(dt_ % 8 + 1) * T], RB[b]
        dma("pool", wpool_sb[:, :, :], w_pool[l].rearrange("g c d -> c g d"), (), [r_wp])
        for db in range(4):
            wa_v, wa_r = load_panel(w_a_out[l][:, db * 512:(db + 1) * 512], 4, 512)
            wc_v, wc_r = load_panel(w_c_out[l][:, db * 512:(db + 1) * 512], 8, 512)
            ya = BIG[5][:, 0:4 * T].rearrange("p (f t) -> p f t", t=T)
            yc = BIG[6][:, 0:4 * T].rearrange("p (f t) -> p f t", t=T)
            for dj in range(4):
                for tb in range(2):
                    p, rp = psum()
                    for kt in range(4):
                        mm(p[:, :], wa_v[:, kt, dj * 128:(dj + 1) * 128], aT[:, kt, tb * 512:(tb + 1) * 512], kt == 0, kt == 3, [wa_r, RB[7]], [rp])
                    act(ya[:, dj, tb * 512:(tb + 1) * 512], p[:, :], AF.Copy, [rp], [RB[5]])
                    p, rp = psum()
                    for kt in range(8):
                        mm(p[:, :], wc_v[:, kt, dj * 128:(dj + 1) * 128], hcT[:, kt, tb * 512:(tb + 1) * 512], kt == 0, kt == 7, [wc_r, RB[0]], [rp])
                    cpy(yc[:, dj, tb * 512:(tb + 1) * 512], p[:, :], [rp], [RB[6]])
            for br in range(3):
                c0 = OFF_MERGE + br * D + db * 512

                def ev(ft, tb, p, rp, br=br, db=db):
                    dt_ = db * 4 + ft
                    bt = 44 + br * 16 + dt_
                    gt = SC[(ft + tb) % 2][:, 0:512]
                    rg = RS[(ft + tb) % 2]
                    act(gt, p[:, :], AF.Sigmoid, [rp, R_prm], [rg], bias=prm[:, P_BIN + bt:P_BIN + bt + 1])
                    mv, mr = mixv(dt_)
                    msl = mv[:, tb * 512:(tb + 1) * 512]
                    if br == 0:
                        tt(ya[:, ft, tb * 512:(tb + 1) * 512], ya[:, ft, tb * 512:(tb + 1) * 512], gt, ALU.mult, [rg, RB[5]], [RB[5]])
                    elif br == 1:
                        gidx = dt_ // 4
                        p2, rp2 = psum()
                        mm(p2[:, :], wpool_sb[:, gidx, (dt_ % 4) * 128:(dt_ % 4 + 1) * 128], pT[:, gidx, tb * 512:(tb + 1) * 512], True, True, [r_wp, RB[7]], [rp2])
                        tmp = SC[2][:, 0:512]
                        stt(tmp, p2[:, :], prm[:, P_PSC + dt_:P_PSC + dt_ + 1], gt, ALU.mult, ALU.mult, [rp2, rg, R_prm], [RS[2]])
                        tt(ya[:, ft, tb * 512:(tb + 1) * 512], ya[:, ft, tb * 512:(tb + 1) * 512], tmp, ALU.add, [RS[2], RB[5]], [RB[5]])
                    else:
                        tt(yc[:, ft, tb * 512:(tb + 1) * 512], yc[:, ft, tb * 512:(tb + 1) * 512], gt, ALU.mult, [rg, RB[6]], [RB[6]])
                        tt(msl, ya[:, ft, tb * 512:(tb + 1) * 512], yc[:, ft, tb * 512:(tb + 1) * 512], ALU.add, [RB[5], RB[6]], [mr])
                linear(wl[:, c0:c0 + 512], 16, 512, rhs_h(hv, hr), ev)

        for db in range(4):
            stage = BIG[5][:, 0:4 * T].rearrange("p (f t) -> p f t", t=T)
            tstage = BIG[6][:, 0:8 * 512].rearrange("p (i n) -> p i n", n=512)

            def ev(ft, tb, p, rp):
                if (ft + tb) % 2 == 0:
                    cpy(stage[:, ft, tb * 512:(tb + 1) * 512], p[:, :], [rp], [RB[5]])
                else:
                    act(stage[:, ft, tb * 512:(tb + 1) * 512], p[:, :], AF.Copy, [rp], [RB[5]])
            linear(w_out[l][:, db * 512:(db + 1) * 512], 16, 512, lambda kt, tb: (mixv(kt)[0][:, tb * 512:(tb + 1) * 512], [mixv(kt)[1]]), ev)
            out_block_to_fd(stage, RB[5], tstage, RB[6], db * 4)

    def mlstm(l, g, qkvT, gT):
        qT, kT, vT = qkvT
        Rq, Rk, Rv = RB[2], RB[3], RB[4]
        nseq, nch = (1, 16) if g == 0 else (4, 4)
        LF, LI, BC_, UU, ABC, BL, MALL, MPREV, DEC, CSC, KFAC, ATOK, KSC, THR = [gm[:, i, :] for i in range(14)]
        p, rp = psum()
        for c in range(16):
            tr(p[0:64, c * 16:(c + 1) * 16], gT[0:16, c * 64:(c + 1) * 64], ident[0:16, 0:16], [RS[3], R_c], [rp])
        cpy(G[:, :], p[0:64, 0:256], [rp], [rG])
        G5 = G[:, :].rearrange("p (c d k h) -> p c d k h", d=2, k=2, h=4)
        for dr in range(2):
            lfv = LF[0:64, dr * 64:(dr + 1) * 64].rearrange("p (c h) -> p c h", h=4)
            liv = LI[0:64, dr * 64:(dr + 1) * 64].rearrange("p (c h) -> p c h", h=4)
            act(lfv, G5[:, :, dr, 1, :], AF.Exp, [rG], [rgm], scale=-1.0)
            cpy(liv, G5[:, :, dr, 0, :], [rG], [rgm])
        act(LF[0:64, :], LF[0:64, :], AF.Ln, [rgm], [rgm], bias=1.0)
        ts(LF[0:64, :], LF[0:64, :], -1.0, None, ALU.mult, None, [rgm], [rgm])
        p, rp = psum()
        mm(p[0:64, 0:64], triU[:, :], LF[0:64, 0:64], True, True, [R_c, rgm], [rp])
        mm(p[0:64, 64:128], triL[:, :], LF[0:64, 64:128], True, True, [R_c, rgm], [rp])
        cpy(BC_[0:64, :], p[0:64, 0:128], [rp], [rgm])
        tt(UU[0:64, :], LI[0:64, :], BC_[0:64, :], ALU.subtract, [rgm], [rgm])
        p, rp = psum()
        tr(p[:, 0:64], UU[0:64, :], ident[0:64, 0:64], [rgm, R_c], [rp])
        S.add("dve", (lambda o, a: (lambda e: e.tensor_reduce(o, a, AX.X, ALU.max)))(acol[:, :], p[:, 0:64]), [rp], [rgm])
        diag = SC[0][:, 0:128]
        ts(diag, ident[:, :], acol[:, 0:1], None, ALU.mult, None, [rgm, R_c], [RS[0]])
        p2, rp2 = psum()
        mm(p2[:, 0:128], ones[:, :], diag, True, True, [R_c, RS[0]], [rp2])
        cpy(ABC, p2[:, 0:128], [rp2], [rgm])
        p3, rp3 = psum()
        mm(p3[:, 0:128], ones[0:64, :], LF[0:64, :], True, True, [R_c, rgm], [rp3])
        cpy(BL, p3[:, 0:128], [rp3], [rgm])
        mcur = mcur_all[:, :, 0:nseq, :]
        if g == 0:
            dma("sp", mcur[:, :, 0, :], m0[l].rearrange("(d h) -> d h", h=4).partition_broadcast(128), (), [rgm])
        else:
            mset(mcur[:, :, :, :], 0.0, [rgm])

        def colv(arr, dr, cc):
            return arr[:, dr * 64:(dr + 1) * 64].rearrange("p (q c h) -> p q c h", q=nseq, h=4)[:, :, cc, :]
        for j in range(nch):
            for dr in range(2):
                cc = j if dr == 0 else nch - 1 - j
                cpy(colv(MPREV, dr, cc), mcur[:, dr, :, :], [rgm], [rgm])
                tt(colv(MALL, dr, cc), mcur[:, dr, :, :], colv(ABC, dr, cc), ALU.max, [rgm], [rgm])
                tt(mcur[:, dr, :, :], colv(MALL, dr, cc), colv(BL, dr, cc), ALU.add, [rgm], [rgm])
                if g == 1 and j == nch - 1:
                    pass
        if g == 1:
            for q in range(4):
                dma("sp", stm[q, l].rearrange("(o d h) -> o d h", o=1, h=4), mcur[0:1, :, q, :], [rgm], [Res()])
        tt(DEC, MPREV, MALL, ALU.subtract, [rgm], [rgm])
        act(DEC, DEC, AF.Exp, [rgm], [rgm])
        tt(CSC, MPREV, ABC, ALU.subtract, [rgm], [rgm])
        act(CSC, CSC, AF.Exp, [rgm], [rgm])
        tt(KFAC, ABC, MALL, ALU.subtract, [rgm], [rgm])
        act(KFAC, KFAC, AF.Exp, [rgm], [rgm])
        tt(ATOK[0:64, :], UU[0:64, :], ABC[0:64, :], ALU.subtract, [rgm], [rgm])
        act(ATOK[0:64, :], ATOK[0:64, :], AF.Exp, [rgm], [rgm])
        tt(KSC[0:64, :], ATOK[0:64, :], KFAC[0:64, :], ALU.mult, [rgm], [rgm])
        tt(THR[0:64, :], BC_[0:64, :], ABC[0:64, :], ALU.add, [rgm], [rgm])
        act(THR[0:64, :], THR[0:64, :], AF.Exp, [rgm], [rgm], scale=-1.0)

        Caug = BIG[7][:, 0:8 * 2 * 257].rearrange("p (s k e) -> p s k e", k=2, e=257)
        RC = RB[7]
        hsum = bf(BIG[5])[0:64, :]
        hsum2 = bf(BIG[6])[0:64, :]

        def hs(c, h):
            b = hsum if c < 8 else hsum2
            return b[:, (c % 8) * 1024 + h * 256:(c % 8) * 1024 + (h + 1) * 256]
        Rh = [RB[5], RB[6]]
        mset(bf(BIG[5])[0:64, 0:8192], 0.0, [RB[5], Rhc[0:8]])
        mset(bf(BIG[6])[0:64, 0:8192], 0.0, [RB[6], Rhc[8:16]])
        wk = bf(BIG[1])
        kp = wk[0:64, 0:1024].rearrange("p (h e) -> p h e", e=256)
        vaug = wk[0:64, 1024:1024 + 4 * 257].rearrange("p (h e) -> p h e", e=257)
        spT = wk[0:64, 2304:2304 + 256].rearrange("p (h t) -> p h t", t=64)
        cbf = wk[:, 2560:2560 + 4 * 514].rearrange("p (h k e) -> p h k e", k=2, e=257)
        mset(wk[:, 0:4624], 0.0, [RB[1], R_kp, R_va, R_sp, R_cb])
        mset(vaug[:, :, 256:257], 1.0, [R_va])

        for q in range(nseq):
            if g == 0:
                for dr in range(2):
                    for h in range(4):
                        dma("sp", Caug[:, dr * 4 + h, :, 0:256], C0[l, dr, h].rearrange("(k p) e -> p k e", p=128), (), [RCs[dr * 4 + h], RB[7]])
                        dma("sp", Caug[:, dr * 4 + h, :, 256:257], n0[l, dr, h].rearrange("(k p o) -> p k o", p=128, o=1), (), [RCs[dr * 4 + h], RB[7]], nonc=True)
            else:
                mset(Caug[:, :, :, :], 0.0, [RCs, RB[7]])
            for j in range(nch):
                for dr in range(2):
                    cc = j if dr == 0 else nch - 1 - j
                    c = q * nch + cc
                    tsl = slice(c * 64, (c + 1) * 64)
                    col0 = dr * 64 + c * 4
                    pk, rpk = psum()
                    pv_, rpv = psum()
                    pkb = pk[:].bitcast(BF16)
                    pvb = pv_[:].bitcast(BF16)
                    for h in range(4):
                        for kt2 in range(2):
                            tr(pkb[0:64, h * 256 + kt2 * 128:h * 256 + (kt2 + 1) * 128], kT[:, h * 2 + kt2, tsl], identb[:, :], [Rk, R_c], [rpk])
                            tr(pvb[0:64, h * 256 + kt2 * 128:h * 256 + (kt2 + 1) * 128], vT[:, h * 2 + kt2, tsl], identb[:, :], [Rv, R_c], [rpv])
                    for h in range(4):
                        act(kp[:, h, :], pkb[0:64, h * 256:(h + 1) * 256], AF.Copy, [rpk, rgm], [R_kp[h]], scale=KSC[0:64, col0 + h:col0 + h + 1])
                    cpy(vaug[:, :, 0:256], pvb[0:64, 0:1024].rearrange("p (h e) -> p h e", e=256), [rpv], [R_va])
                    pq, rpq = psum()
                    for h in range(4):
                        for kt2 in range(2):
                            mm(pq[0:64, h * 64:(h + 1) * 64], kT[:, h * 2 + kt2, tsl], qT[:, h * 2 + kt2, tsl], kt2 == 0, kt2 == 1, [Rk, Rq], [rpq])
                    msk = triU if dr == 0 else triL
                    for h in range(4):
                        stt(spT[:, h, :], pq[0:64, h * 64:(h + 1) * 64], ATOK[0:64, col0 + h:col0 + h + 1], msk[:, :], ALU.mult, ALU.mult, [rpq, rgm, R_c], [R_sp[h]])
                    for h in range(4):
                        s_ = dr * 4 + h
                        act(cbf[:, h, :, :], Caug[:, s_, :, :], AF.Copy, [RCs[s_], RB[7], rgm], [R_cb[h]], scale=CSC[:, col0 + h:col0 + h + 1])
                    for h in range(4):
                        s_ = dr * 4 + h
                        pp, rpp = psum()
                        mm(pp[0:64, 0:257], spT[:, h, :], vaug[:, h, :], True, False, [R_sp[h], R_va], [rpp])
                        mm(pp[0:64, 0:257], qT[:, h * 2, tsl], cbf[:, h, 0, :], False, False, [Rq, R_cb[h]], [rpp])
                        mm(pp[0:64, 0:257], qT[:, h * 2 + 1, tsl], cbf[:, h, 1, :], False, True, [Rq, R_cb[h]], [rpp])
                        rr = rsm_t[:, h * 2:h * 2 + 1]
                        act(rr, pp[0:64, 256:257], AF.Abs, [rpp], [R_rs[h]])
                        ts(rr, rr, THR[0:64, col0 + h:col0 + h + 1], None, ALU.max, None, [R_rs[h], rgm], [R_rs[h]])
                        S.add("dve", (lambda o: (lambda e: e.reciprocal(o, o)))(rr), [R_rs[h]], [R_rs[h]])
                        hv_ = hs(c, h)
                        stt(hv_, pp[0:64, 0:256], rr, hv_, ALU.mult, ALU.add, [rpp, R_rs[h], Rhc[c], Rh[c // 8]], [Rhc[c]])
                        for kt2 in range(2):
                            pc, rpc = psum()
                            mm(pc[:, 0:257], kp[:, h, kt2 * 128:(kt2 + 1) * 128], vaug[:, h, :], True, True, [R_kp[h], R_va], [rpc])
                            stt(Caug[:, s_, kt2, :], Caug[:, s_, kt2, :], DEC[:, col0 + h:col0 + h + 1], pc[:, 0:257], ALU.mult, ALU.add, [RCs[s_], RB[7], rgm, rpc], [RCs[s_]])
            if g == 1:
                for dr in range(2):
                    for h in range(4):
                        dma("sp", stC[q, l, dr, h].rearrange("(k p) e -> p k e", p=128), Caug[:, dr * 4 + h, :, 0:256], [RCs[dr * 4 + h], RB[7]], [Res()])
                        dma("sp", stn[q, l, dr, h].rearrange("(k p o) -> p k o", p=128, o=1), Caug[:, dr * 4 + h, :, 256:257], [RCs[dr * 4 + h], RB[7]], [Res()], nonc=True)

        hcT = bf(BIG[0])[:, 0:8 * T].rearrange("p (f t) -> p f t", t=T)
        mnw = prm[:, P_MNW:P_MNW + 8]
        for c in range(16):
            hb = hsum if c < 8 else hsum2
            hc_ = hb[:, (c % 8) * 1024:(c % 8 + 1) * 1024]
            sqc = SC[c % 2][0:64, :]
            rq = RS[c % 2]
            act(sqc, hc_, AF.Square, [Rhc[c], Rh[c // 8]], [rq])
            S.add("dve", (lambda o, a: (lambda e: e.tensor_reduce(o, a, AX.X, ALU.add)))(hn_s[:, c, :], sqc.rearrange("p (h e) -> p h e", e=256)), [rq], [R_hn])
            ts(hn_s[:, c, :], hn_s[:, c, :], 1.0 / DH, EPS, ALU.mult, ALU.add, [R_hn], [R_hn])
            act(hn_s[:, c, :], hn_s[:, c, :], AF.Sqrt, [R_hn], [R_hn])
            S.add("dve", (lambda o: (lambda e: e.reciprocal(o, o)))(hn_s[:, c, :]), [R_hn], [R_hn])
            tt(sqc.rearrange("p (h e) -> p h e", e=256), hc_.rearrange("p (h e) -> p h e", e=256),
               hn_s[:, c, :].unsqueeze(2).to_broadcast([64, 4, 256]), ALU.mult, [Rhc[c], Rh[c // 8], R_hn], [rq])
            p, rp = psum()
            for ft in range(8):
                tr(p[:, ft * 64:(ft + 1) * 64], sqc[:, ft * 128:(ft + 1) * 128], ident[0:64, 0:64], [rq, R_c], [rp])
            tt(hcT[:, :, c * 64:(c + 1) * 64], p[:, :].rearrange("p (f t) -> p f t", t=64),
               mnw.unsqueeze(2).to_broadcast([128, 8, 64]), ALU.mult, [rp, R_prm], [RB[0]])

    def ffn(l, g):
        hv, hr = hT_views(0, 1, fine=True)
        rhs = lambda kt, tb: (hv[kt][:, tb * 512:(tb + 1) * 512], [hr[kt]])

        def actv(j):
            b = 2 + j // 8
            return bf(BIG[b])[:, (j % 8) * T:(j % 8 + 1) * T], RB[b]
        fcw = prm[:, P_FCW:P_FCW + 132].rearrange("p (f k) -> p f k", k=3)
        u_sb, g_sb, gc, t1 = SC[0], SC[1], SC[2], SC[3]
        for qb in range(11):
            def ev_u(ft, tb, p, rp):
                if tb == 0:
                    cpy(u_sb[:, 0:512], p[:, :], [rp], [RS[0]])
                else:
                    act(u_sb[:, 512:1024], p[:, :], AF.Copy, [rp], [RS[0]])

            def ev_g(ft, tb, p, rp):
                if tb == 0:
                    act(g_sb[:, 0:512], p[:, :], AF.Copy, [rp], [RS[1]])
                else:
                    cpy(g_sb[:, 512:1024], p[:, :], [rp], [RS[1]])
            vu, ru = load_panel(w_ffn_up[l][:, qb * 512:(qb + 1) * 512], 16, 512)
            vg, rg_ = load_panel(w_ffn_up[l][:, DFF + qb * 512:DFF + (qb + 1) * 512], 16, 512)
            for ft in range(4):
                j = qb * 4 + ft
                for (vv, rv, evf) in ((vu, ru, ev_u), (vg, rg_, ev_g)):
                    for tb in range(2):
                        p, rp = psum()
                        for kt in range(16):
                            mm(p[:, :], vv[:, kt, ft * 128:(ft + 1) * 128], hv[kt][:, tb * 512:(tb + 1) * 512], kt == 0, kt == 15, [rv, hr[kt]], [rp])
                        evf(ft, tb, p, rp)
                ts(gc[:, :], g_sb[:, :], fcw[:, j, 1:2], prm[:, P_FCB + j:P_FCB + j + 1], ALU.mult, ALU.add, [RS[1], R_prm], [RS[2]])
                for k in (0, 2):
                    d = k - 1
                    dv, _ = shifted(gc[:, :], g, "col", d)
                    _, sv = shifted(g_sb[:, :], g, "col", d)
                    stt(dv, sv, fcw[:, j, k:k + 1], dv, ALU.mult, ALU.add, [RS[1], RS[2], R_prm], [RS[2]])
                act(t1[:, :], gc[:, :], AF.Square, [RS[2]], [RS[3]])
                ts(t1[:, :], t1[:, :], 0.044715, 1.0, ALU.mult, ALU.add, [RS[3]], [RS[3]])
                tt(t1[:, :], t1[:, :], gc[:, :], ALU.mult, [RS[3], RS[2]], [RS[3]])
                act(t1[:, :], t1[:, :], AF.Sigmoid, [RS[3]], [RS[3]], scale=1.5957691216057308)
                tt(gc[:, :], gc[:, :], u_sb[:, :], ALU.mult, [RS[2], RS[0]], [RS[2]])
                av, ar = actv(j)
                tt(av, gc[:, :], t1[:, :], ALU.mult, [RS[2], RS[3]], [ar])
        claim(0)
        claim(1)
        for db in range(4):
            stage = BIG[0][:, 0:4 * T].rearrange("p (f t) -> p f t", t=T)
            tstage = BIG[1][:, 0:8 * 512].rearrange("p (i n) -> p i n", n=512)
            for dj in range(4):
                dt_ = db * 4 + dj

                def ev(ft, tb, p, rp, dj=dj):
                    if tb == 0:
                        cpy(stage[:, dj, 0:512], p[:, :], [rp], [RB[0]])
                    else:
                        act(stage[:, dj, 512:1024], p[:, :], AF.Copy, [rp], [RB[0]])
                linear(w_ffn_down[l][:, dt_ * 128:(dt_ + 1) * 128], 44, 128, lambda kt, tb: (actv(kt)[0][:, tb * 512:(tb + 1) * 512], [actv(kt)[1]]), ev)
            out_block_to_fd(stage, RB[0], tstage, RB[1], db * 4)

    for l in range(depth):
        mod_phase(l)
    for g in GORDER:
        load_params(0)
        fused_pass(g, None, (0, 0))
        for l in range(depth):
            mixer(l, g)
            fused_pass(g, (l, 0), (l, 1))
            ffn(l, g)
            if l + 1 < depth:
                load_params(l + 1)
                fused_pass(g, (l, 1), (l + 1, 0))
            else:
                fused_pass(g, (l, 1), None)
    S.emit(nc, es)
    es.close()
    return nc


_CACHE = {}


def _consts():
    ident = np.eye(128, dtype=np.float32)
    s = np.arange(64)
    triU = (s[:, None] <= s[None, :]).astype(np.float32)
    triL = (s[:, None] >= s[None, :]).astype(np.float32)
    inv = np.zeros((2, 4, T), np.float32)
    for gi, L in enumerate((16, 256)):
        t = np.arange(L)
        for wi, w in enumerate(POOLW):
            lo = np.clip(t - w // 2, 0, L)
            hi = np.clip(t - w // 2 + w, 0, L)
            ic = (1.0 / (hi - lo)).astype(np.float32)
            if gi == 0:
                inv[gi, wi] = np.repeat(ic, 64)
            else:
                inv[gi, wi] = np.tile(ic, 4)
    return ident, triU, triL, inv


def kernel(x_prompt, x_sample, state_C, state_n, state_m, c, c_ctx, w_ada, b_ada, norm_mix_pre, norm_mix_post,
           norm_ffn_pre, norm_ffn_post, w_in, b_in, conv_a_w, conv_a_b, ln_a_w, ln_a_b, w_a_out, w_pool,
           pool_scale, mlstm_norm_w, w_c_out, w_out, w_ffn_up, ffn_conv_w, ffn_conv_b, w_ffn_down, _depth=None,
           _cores=8):
    f = lambda a: np.ascontiguousarray(np.asarray(a, dtype=np.float32))
    depth = _depth or w_ada.shape[0]
    if depth not in _CACHE:
        _CACHE[depth] = build(depth)
    nc = _CACHE[depth]
    ident, triU, triL, inv = _consts()
    shared = {"w_ada": f(w_ada)[:depth], "b_ada": f(b_ada)[:depth], "norm_mix_pre": f(norm_mix_pre)[:depth],
              "norm_mix_post": f(norm_mix_post)[:depth], "norm_ffn_pre": f(norm_ffn_pre)[:depth],
              "norm_ffn_post": f(norm_ffn_post)[:depth], "w_in": f(w_in)[:depth], "b_in": f(b_in)[:depth],
              "conv_a_w": f(conv_a_w)[:depth], "conv_a_b": f(conv_a_b)[:depth], "ln_a_w": f(ln_a_w)[:depth],
              "ln_a_b": f(ln_a_b)[:depth], "w_a_out": f(w_a_out)[:depth], "w_pool": f(w_pool)[:depth],
              "pool_scale": f(pool_scale)[:depth], "mlstm_norm_w": f(mlstm_norm_w)[:depth],
              "w_c_out": f(w_c_out)[:depth], "w_out": f(w_out)[:depth], "w_ffn_up": f(w_ffn_up)[:depth],
              "ffn_conv_w": f(ffn_conv_w)[:depth], "ffn_conv_b": f(ffn_conv_b)[:depth],
              "w_ffn_down": f(w_ffn_down)[:depth], "c_ident": ident, "c_triU": triU, "c_triL": triL,
              "c_invcnt": inv}
    xp = f(x_prompt)
    xsm = f(x_sample)
    sC, sn, sm = f(state_C), f(state_n), f(state_m)
    cc, cctx = f(c), f(c_ctx)
    in_maps = []
    for core in range(_cores):
        b = core % 4
        m = dict(shared)
        m["xs"] = np.ascontiguousarray(np.concatenate([xsm[b], xp[core * 4:(core + 1) * 4].reshape(T, D)], axis=0))
        m["cond"] = np.ascontiguousarray(np.stack([cc[b], cctx], axis=0))
        m["C0"] = np.ascontiguousarray(sC[b][:depth])
        m["n0"] = np.ascontiguousarray(sn[b][:depth])
        m["m0"] = np.ascontiguousarray(sm[b][:depth].reshape(depth, 8))
        in_maps.append(m)
    res = run_bass_kernel_spmd(nc, in_maps, core_ids=list(range(_cores)))
    R = res.results
    B = x_prompt.shape[0]
    yp = np.zeros((B, 256, D), np.float32)
    ys = np.zeros((4, T, D), np.float32)
    nC = np.zeros((B, depth, 2, NH, DH, DH), np.float32)
    nn = np.zeros((B, depth, 2, NH, DH), np.float32)
    nm = np.zeros((B, depth, 2, NH), np.float32)
    for core in range(_cores):
        r = R[core]
        yy = np.asarray(r["y"])
        if core < 4:
            ys[core] = yy[0:T]
        yp[core * 4:(core + 1) * 4] = yy[T:2 * T].reshape(4, 256, D)
        nC[core * 4:(core + 1) * 4] = np.asarray(r["stC"])
        nn[core * 4:(core + 1) * 4] = np.asarray(r["stn"])
        nm[core * 4:(core + 1) * 4] = np.asarray(r["stm"]).reshape(4, depth, 2, NH)
    return yp, ys, nC, nn, nm
```

```python
import numpy as np
from contextlib import ExitStack
import concourse.bass as bass
import concourse.mybir as mybir
from concourse.bass_utils import run_bass_kernel_spmd

F32 = mybir.dt.float32
BF16 = mybir.dt.bfloat16
AF = mybir.ActivationFunctionType
ALU = mybir.AluOpType
AX = mybir.AxisListType

D = 2048
T = 1024
NT = 8
WA = 512
WB = 512
WC = 1024
NH = 4
DH = 256
DFF = 5632
NIN = 11792
OFF_POOL = 1024
OFF_QKV = 1536
OFF_OG = 4608
OFF_GATES = 5632
OFF_MERGE = 5648
EPS = 1e-6
POOLW = (2, 4, 8, 16)
BIGN = 4112


def _flat(x):
    out = []
    for a in x:
        if a is None:
            continue
        if isinstance(a, (list, tuple)):
            out.extend(_flat(a))
        else:
            out.append(a)
    return out


SAME_ENG_WAIT = True
GORDER = (0, 1)


class Res:
    __slots__ = ("lw", "rd", "dr")

    def __init__(self):
        self.lw = None
        self.rd = {}
        self.dr = []


class Op:
    __slots__ = ("eng", "fn", "deps", "dma", "sig", "sigval", "semk")

    def __init__(self, eng, fn, deps, dma):
        self.eng = eng
        self.fn = fn
        self.deps = deps
        self.dma = dma
        self.sig = False
        self.sigval = 0
        self.semk = None


class Sched:
    ENGS = ("pe", "act", "dve", "pool", "sp")
    NDS = 12

    def __init__(self):
        self.ops = []

    def add(self, eng, fn, reads=(), writes=(), dma=False):
        oid = len(self.ops)
        deps = set()
        reads = _flat(reads)
        writes = _flat(writes)
        for r in reads:
            if r.lw is not None:
                deps.add(r.lw)
        for w in writes:
            if w.lw is not None:
                deps.add(w.lw)
            deps.update(w.rd.values())
            deps.update(w.dr)
        self.ops.append(Op(eng, fn, deps, dma))
        for r in reads:
            if dma:
                r.dr.append(oid)
            else:
                r.rd[eng] = oid
        for w in writes:
            w.lw = oid
            w.rd = {}
            w.dr = []
        return oid

    def emit(self, nc, es):
        ops = self.ops
        for o in ops:
            if o.dma:
                o.sig = True
            for d in o.deps:
                od = ops[d]
                if od.dma:
                    continue
                if od.eng == o.eng and not o.dma and (o.eng == "pe" or not SAME_ENG_WAIT):
                    continue
                od.sig = True
        cnt = {e: 0 for e in self.ENGS}
        dcnt = {}
        dn = {e: 0 for e in self.ENGS}
        for o in ops:
            if o.dma:
                k = (o.eng, dn[o.eng] % self.NDS)
                dn[o.eng] += 1
                dcnt[k] = dcnt.get(k, 0) + 1
                o.semk = k
                o.sigval = 16 * dcnt[k]
            elif o.sig:
                cnt[o.eng] += 1
                o.sigval = cnt[o.eng]
        sems = {}
        for e in self.ENGS:
            sems[e] = es.enter_context(nc.semaphore("s_" + e))
        for k in dcnt:
            sems[k] = es.enter_context(nc.semaphore("d_%s_%d" % k))
        block = es.enter_context(nc.Block())
        handles = {"pe": block.tensor, "act": block.scalar, "dve": block.vector, "pool": block.gpsimd,
                   "sp": block.sync}

        def run_engine(ename):
            def body(h):
                seen = {}
                for o in ops:
                    if o.eng != ename:
                        continue
                    need = {}
                    for d in o.deps:
                        od = ops[d]
                        if od.dma:
                            key = od.semk
                        else:
                            if od.eng == ename and not o.dma and (ename == "pe" or not SAME_ENG_WAIT):
                                continue
                            key = od.eng
                        if need.get(key, 0) < od.sigval:
                            need[key] = od.sigval
                    if o.dma and o.sigval > 16:
                        if need.get(o.semk, 0) < o.sigval - 16:
                            need[o.semk] = o.sigval - 16
                    for key, v in need.items():
                        if seen.get(key, 0) < v:
                            h.wait_ge(sems[key], v)
                            seen[key] = v
                    ins = o.fn(h)
                    if o.dma:
                        ins.then_inc(sems[o.semk], 16)
                    elif o.sig:
                        ins.then_inc(sems[ename], 1)
                for k, c in dcnt.items():
                    if k[0] == ename:
                        h.wait_ge(sems[k], 16 * c)
            return body

        for e in self.ENGS:
            handles[e](run_engine(e))


def build(depth=4):
    nc = bass.Bass("TRN2", target_bir_lowering=False)
    S = Sched()
    es = ExitStack()

    def din(name, shape):
        return nc.dram_tensor(name, list(shape), F32, kind="ExternalInput").ap()

    def dout(name, shape):
        return nc.dram_tensor(name, list(shape), F32, kind="ExternalOutput").ap()

    xs = din("xs", [2 * T, D])
    cond = din("cond", [2, D])
    C0 = din("C0", [depth, 2, NH, DH, DH])
    n0 = din("n0", [depth, 2, NH, DH])
    m0 = din("m0", [depth, 8])
    w_ada = din("w_ada", [depth, D, 6 * D])
    b_ada = din("b_ada", [depth, 6 * D])
    nrm = {k: din(k, [depth, D]) for k in ("norm_mix_pre", "norm_mix_post", "norm_ffn_pre", "norm_ffn_post")}
    w_in = din("w_in", [depth, D, NIN])
    b_in = din("b_in", [depth, NIN])
    conv_a_w = din("conv_a_w", [depth, 31, WA])
    conv_a_b = din("conv_a_b", [depth, WA])
    ln_a_w = din("ln_a_w", [depth, WA])
    ln_a_b = din("ln_a_b", [depth, WA])
    w_a_out = din("w_a_out", [depth, WA, D])
    w_pool = din("w_pool", [depth, 4, 128, 512])
    pool_scale = din("pool_scale", [depth, D])
    mlstm_norm_w = din("mlstm_norm_w", [depth, WC])
    w_c_out = din("w_c_out", [depth, WC, D])
    w_out = din("w_out", [depth, D, D])
    w_ffn_up = din("w_ffn_up", [depth, D, 2 * DFF])
    ffn_conv_w = din("ffn_conv_w", [depth, 3, DFF])
    ffn_conv_b = din("ffn_conv_b", [depth, DFF])
    w_ffn_down = din("w_ffn_down", [depth, DFF, D])
    c_ident = din("c_ident", [128, 128])
    c_triU = din("c_triU", [64, 64])
    c_triL = din("c_triL", [64, 64])
    c_invcnt = din("c_invcnt", [2, 4, T])

    y = dout("y", [2 * T, D])
    stC = dout("stC", [4, depth, 2, NH, DH, DH])
    stn = dout("stn", [4, depth, 2, NH, DH])
    stm = dout("stm", [4, depth, 8])
    mod_d = dout("mod_d", [depth, 2, 6 * D])
    hT_d = dout("hT_d", [2, 128, 4096])
    R_hTd = Res()
    f_d = dout("f_d", [T, D])
    R_y = [[Res() for _ in range(NT)] for _ in range(2)]
    R_f = [Res() for _ in range(NT)]
    R_modp = {(l_, g_, pn_): Res() for l_ in range(depth) for g_ in range(2) for pn_ in range(24)}
    R_st = Res()

    def sb(name, shape, dt=F32):
        return es.enter_context(nc.sbuf_tensor(name, list(shape), dt))

    BIG = [sb("big%d" % i, [128, BIGN]) for i in range(8)]
    RBh = [[Res(), Res()] for _ in range(8)]
    RB = [tuple(h) for h in RBh]
    WBUF = [sb("wb%d" % i, [128, 8192], BF16) for i in range(2)]
    RW = [Res() for _ in range(2)]
    SC = [sb("sc%d" % i, [128, T]) for i in range(4)]
    RS = [Res() for _ in range(4)]
    junk = sb("junk", [128, 2048], BF16)
    R_junk = Res()
    PS = [es.enter_context(nc.psum_tensor("ps%d" % i, [128, 512], F32)) for i in range(8)]
    RP = [Res() for _ in range(8)]
    psn = [0]

    def psum():
        i = psn[0] % 8
        psn[0] += 1
        return PS[i], RP[i]

    wbn = [0]

    def bf(t, n=None):
        a = t[:].bitcast(BF16)
        return a

    def mm(out, lhsT, rhs, start, stop, R, W):
        S.add("pe", lambda e: e.matmul(out, lhsT, rhs, start=start, stop=stop), R, W)

    def tr(out, in_, ident, R, W):
        S.add("pe", lambda e: e.transpose(out, in_, ident), R, W)

    def act(out, in_, func, R, W, bias=None, scale=None, accum=None):
        kw = {}
        if bias is not None:
            kw["bias"] = bias
        if scale is not None:
            kw["scale"] = scale
        if accum is not None:
            kw["accum_out"] = accum
        S.add("act", lambda e: e.activation(out, in_, func, **kw), R, W)

    def tt(out, a, b, op, R, W, eng="dve"):
        S.add(eng, lambda e: e.tensor_tensor(out, a, b, op), R, W)

    def ts(out, a, s1, s2, op0, op1, R, W, eng="dve"):
        if op1 is None:
            S.add(eng, lambda e: e.tensor_scalar(out, a, s1, None, op0), R, W)
        else:
            S.add(eng, lambda e: e.tensor_scalar(out, a, s1, s2, op0, op1), R, W)

    def stt(out, a, s, b, op0, op1, R, W):
        S.add("dve", lambda e: e.scalar_tensor_tensor(out, a, s, b, op0, op1), R, W)

    def cpy(out, in_, R, W, eng="dve"):
        S.add(eng, lambda e: e.tensor_copy(out, in_), R, W)

    def mset(ap, v, W, eng="dve"):
        S.add(eng, lambda e: e.memset(ap, v), (), W)

    def dma(q, out, in_, R, W, nonc=False):
        if nonc:
            S.add(q, lambda e: e.dma_start(out=out, in_=in_, allow_slow_non_contiguous=True), R, W, dma=True)
        else:
            S.add(q, lambda e: e.dma_start(out=out, in_=in_), R, W, dma=True)

    gm = sb("gm", [128, 14, 128]); rgm = Res()
    G = sb("G", [64, 256]); rG = Res()
    acol = sb("acol", [128, 1])
    mcur_all = sb("mcur", [128, 2, 4, 4])
    rsm_t = sb("rsm_t", [64, 8])
    hn_s = sb("hn_s", [64, 16, 4]); R_hn = Res()
    wpool_sb = sb("wpool", [128, 4, 512], BF16); r_wp = Res()
    small = sb("small", [128, 8, 8]); rsm = [Res() for _ in range(8)]
    R_kp = [Res() for _ in range(4)]; R_va = Res(); R_sp = [Res() for _ in range(4)]
    R_cb = [Res() for _ in range(4)]; R_rs = [Res() for _ in range(4)]
    RCs = [Res() for _ in range(8)]; Rhc = [Res() for _ in range(16)]
    ident = sb("ident", [128, 128])
    identb = sb("identb", [128, 128], BF16)
    ones = sb("ones", [128, 128])
    triU = sb("triU", [64, 64])
    triL = sb("triL", [64, 64])
    R_c = Res()
    dma("sp", ident[:], c_ident, (), [R_c])
    dma("sp", triU[:], c_triU, (), [R_c])
    dma("sp", triL[:], c_triL, (), [R_c])
    cpy(identb[:], ident[:], [R_c], [R_c])
    mset(ones[:], 1.0, [R_c])

    scT = sb("scT", [128, 16, 64], BF16)
    R_scT = Res()
    cnd = SC[0]
    mset(BIG[0][0:64, 0:D], 0.0, [RB[0]])
    dma("sp", BIG[0][0:1, 0:D], cond[0:1, :], (), [RB[0]])
    dma("sp", BIG[0][32:33, 0:D], cond[1:2, :], (), [RB[0]])
    act(BIG[0][0:64, 0:D], BIG[0][0:64, 0:D], AF.Silu, [RB[0]], [RB[0]])
    for kt in range(16):
        p, rp = psum()
        tr(p[:, 0:64], BIG[0][0:64, kt * 128:(kt + 1) * 128], ident[0:64, 0:64], [RB[0], R_c], [rp])
        cpy(scT[:, kt, :], p[:, 0:64], [rp], [R_scT])

    prm = sb("prm", [128, 640])
    R_prm = Res()
    P_BIN, P_CAW, P_CAB, P_LNW, P_LNB, P_PSC, P_MNW, P_FCW, P_FCB, P_BQ = 0, 92, 216, 220, 224, 228, 244, 252, 384, 428
    gbias = sb("gbias", [16, 1])

    def load_rows_T(dst_ap_fn, src2d, nrows, ncolt, stage, rstage):
        for c0 in range(0, ncolt, 8):
            nc_ = min(8, ncolt - c0)
            stage, rstage = SC[2 + (c0 // 8) % 2], RS[2 + (c0 // 8) % 2]
            dma("sp", stage[0:nrows, 0:nc_ * 128], src2d[:, c0 * 128:(c0 + nc_) * 128], (), [rstage])
            for c in range(nc_):
                p, rp = psum()
                tr(p[:, 0:nrows], stage[0:nrows, c * 128:(c + 1) * 128], ident[0:nrows, 0:nrows], [rstage, R_c], [rp])
                cpy(dst_ap_fn(c0 + c), p[:, 0:nrows], [rp], [R_prm])

    def load_params(l):
        st, rs = None, None
        load_rows_T(lambda c: prm[:, P_BIN:P_BIN + 44], b_in[l, 0:5632].rearrange("(r c) -> r c", c=128), 44, 1, st, rs)
        load_rows_T(lambda c: prm[:, P_BIN + 44:P_BIN + 92], b_in[l, OFF_MERGE:NIN].rearrange("(r c) -> r c", c=128), 48, 1, st, rs)
        dma("sp", gbias[:], b_in[l, OFF_GATES:OFF_MERGE].rearrange("(p o) -> p o", o=1), (), [R_prm])
        pv = prm[:, P_CAW:P_CAW + 124].rearrange("p (f k) -> p f k", k=31)
        load_rows_T(lambda c: pv[:, c, :], conv_a_w[l], 31, 4, st, rs)
        load_rows_T(lambda c: prm[:, P_CAB:P_CAB + 4], conv_a_b[l].rearrange("(r c) -> r c", c=128), 4, 1, st, rs)
        load_rows_T(lambda c: prm[:, P_LNW:P_LNW + 4], ln_a_w[l].rearrange("(r c) -> r c", c=128), 4, 1, st, rs)
        load_rows_T(lambda c: prm[:, P_LNB:P_LNB + 4], ln_a_b[l].rearrange("(r c) -> r c", c=128), 4, 1, st, rs)
        load_rows_T(lambda c: prm[:, P_PSC:P_PSC + 16], pool_scale[l].rearrange("(r c) -> r c", c=128), 16, 1, st, rs)
        load_rows_T(lambda c: prm[:, P_MNW:P_MNW + 8], mlstm_norm_w[l].rearrange("(r c) -> r c", c=128), 8, 1, st, rs)
        fv = prm[:, P_FCW:P_FCW + 132].rearrange("p (f k) -> p f k", k=3)
        load_rows_T(lambda c: fv[:, c, :], ffn_conv_w[l][:, 0:2816], 3, 22, st, rs)
        load_rows_T(lambda c: fv[:, 22 + c, :], ffn_conv_w[l][:, 2816:5632], 3, 22, st, rs)
        load_rows_T(lambda c: prm[:, P_FCB:P_FCB + 44], ffn_conv_b[l].rearrange("(r c) -> r c", c=128), 44, 1, st, rs)
        ts(prm[:, P_BQ:P_BQ + 8], prm[:, P_BIN + 12:P_BIN + 20], 1.0 / 16.0, None, ALU.mult, None, [R_prm], [R_prm])

    def load_panel(src, KT, ncols):
        i = wbn[0] % 2
        wbn[0] += 1
        v = WBUF[i][:, 0:KT * ncols].rearrange("p (k n) -> p k n", n=ncols)
        dma("pool", v, src.rearrange("(k p) n -> p k n", p=128), (), [RW[i]])
        return v, RW[i]

    def linear(src, KT, ncols, rhs_fn, evac_fn, ft0=0):
        v, rw = load_panel(src, KT, ncols)
        for ft in range(ncols // 128):
            for tb in range(2):
                p, rp = psum()
                for kt in range(KT):
                    ra, rr = rhs_fn(kt, tb)
                    mm(p[:, :], v[:, kt, ft * 128:(ft + 1) * 128], ra, kt == 0, kt == KT - 1, [rw] + rr, [rp])
                evac_fn(ft0 + ft, tb, p, rp)

    def mod_phase(l):
        for pn in range(24):
            stg, rsg = SC[(pn % 2) * 2], RS[(pn % 2) * 2]
            bst, rbs = SC[(pn % 2) * 2 + 1], RS[(pn % 2) * 2 + 1]
            v, rw = load_panel(w_ada[l, :, pn * 512:(pn + 1) * 512], 16, 512)
            p, rp = psum()
            for kt in range(16):
                mm(p[0:64, :], scT[:, kt, :], v[:, kt, :], kt == 0, kt == 15, [rw, R_scT], [rp])
            dma("sp", bst[0:1, 0:512], b_ada[l, pn * 512:(pn + 1) * 512].rearrange("(o n) -> o n", o=1), (), [rbs])
            dma("sp", bst[32:33, 0:512], b_ada[l, pn * 512:(pn + 1) * 512].rearrange("(o n) -> o n", o=1), (), [rbs])
            tt(stg[0:1, 0:512], p[0:1, :], bst[0:1, 0:512], ALU.add, [rp, rbs], [rsg])
            tt(stg[32:33, 0:512], p[32:33, :], bst[32:33, 0:512], ALU.add, [rp, rbs], [rsg])
            dma("sp", mod_d[l, 0:1, pn * 512:(pn + 1) * 512], stg[0:1, 0:512], [rsg], [R_modp[(l, 0, pn)]])
            dma("sp", mod_d[l, 1:2, pn * 512:(pn + 1) * 512], stg[32:33, 0:512], [rsg], [R_modp[(l, 1, pn)]])

    def bcast_load(dst, rdst, vec, modsel=None):
        rr_ = [] if modsel is None else [R_modp[(modsel[0], modsel[1], modsel[2] * 4 + k_)] for k_ in range(4)]
        dma("sp", dst, vec.partition_broadcast(128), rr_, [rdst])

    RH = {0: [Res() for _ in range(8)], 1: [Res() for _ in range(8)]}
    clm = sb("clm", [128, 2])

    def claim(b):
        mset(clm[:, 0:1], 0.0, [RB[b], RH[b]])

    def hT_views(b0, b1, fine=False):
        hv, hr = [], []
        for kt in range(16):
            b = b0 if kt < 8 else b1
            hv.append(bf(BIG[b])[:, (kt % 8) * T:(kt % 8 + 1) * T])
            hr.append(RH[b][kt % 8] if fine else RB[b])
        return hv, hr

    def fused_pass(g, fin, nrmsel):
        gam = BIG[7][:, 0:D]
        shf = BIG[7][:, D:2 * D]
        pg = BIG[2][:, 0:D]
        tmpb = BIG[2][:, D:2 * D]
        rb = RB[7]
        rpg = RB[2]
        if fin is not None:
            lf, wf = fin
            gi = 2 if wf == 0 else 5
            nwf = nrm["norm_mix_post" if wf == 0 else "norm_ffn_post"]
            bcast_load(pg, rpg, mod_d[lf, g, gi * D:(gi + 1) * D], (lf, g, gi))
            bcast_load(tmpb, rpg, nwf[lf])
            tt(pg, pg, tmpb, ALU.mult, [rpg], [rpg])
            if lf == 0 and wf == 0:
                xsrc = [(xs[g * T + i * 128:g * T + (i + 1) * 128, :], []) for i in range(NT)]
            else:
                xsrc = [(y[g * T + i * 128:g * T + (i + 1) * 128, :], [R_y[g][i]]) for i in range(NT)]
        else:
            xsrc = [(xs[g * T + i * 128:g * T + (i + 1) * 128, :], []) for i in range(NT)]
        if nrmsel is not None:
            ln_, wn = nrmsel
            si, ci = (0, 1) if wn == 0 else (3, 4)
            nw = nrm["norm_mix_pre" if wn == 0 else "norm_ffn_pre"]
            bcast_load(gam, rb, mod_d[ln_, g, ci * D:(ci + 1) * D], (ln_, g, ci))
            bcast_load(shf, rb, nw[ln_])
            stt(gam, gam, 1.0, shf, ALU.add, ALU.mult, [rb], [rb])
            bcast_load(shf, rb, mod_d[ln_, g, si * D:(si + 1) * D], (ln_, g, si))
            claim(0)
            claim(1)
            hviews, hres = hT_views(0, 1, fine=True)
        xbuf = [(BIG[6][:, 0:D], RBh[6][0]), (BIG[6][:, D:2 * D], RBh[6][1]), (BIG[5][:, 0:D], RBh[5][0]), (BIG[5][:, D:2 * D], RBh[5][1])]
        fbuf = [(BIG[4][:, 0:D], RBh[4][0]), (BIG[4][:, D:2 * D], RBh[4][1]), (BIG[3][:, 0:D], RBh[3][0]), (BIG[3][:, D:2 * D], RBh[3][1])]
        def issue_loads(i):
            x_t, rx = xbuf[i % 4]
            f_t, rf = fbuf[i % 4]
            dma("sp", x_t, xsrc[i][0], xsrc[i][1], [rx])
            if fin is not None:
                dma("sp", f_t, f_d[i * 128:(i + 1) * 128, :], [R_f[i]], [rf])

        def stage_a(i):
            x_t, rx = xbuf[i % 4]
            f_t, rf = fbuf[i % 4]
            ri = rsm[i]
            if fin is not None:
                act(junk[:, :], f_t, AF.Square, [rf], [ri], accum=small[:, i, 0:1])
                act(small[:, i, 1:2], small[:, i, 0:1], AF.Sqrt, [ri], [ri], scale=1.0 / D, bias=EPS)
                S.add("dve", (lambda o, a: (lambda e: e.reciprocal(o, a)))(small[:, i, 2:3], small[:, i, 1:2]), [ri], [ri])
                stt(f_t, f_t, small[:, i, 2:3], pg, ALU.mult, ALU.mult, [rf, ri, rpg], [rf])
                tt(f_t, f_t, x_t, ALU.add, [rf, rx], [rf])
                dma("sp", y[g * T + i * 128:g * T + (i + 1) * 128, :], f_t, [rf], [R_y[g][i]])

        def stage_b(i):
            x_t, rx = xbuf[i % 4]
            f_t, rf = fbuf[i % 4]
            ri = rsm[i]
            cur, rcur = (f_t, rf) if fin is not None else (x_t, rx)
            if nrmsel is not None:
                act(junk[:, :], cur, AF.Square, [rcur], [ri], accum=small[:, i, 4:5])
                act(small[:, i, 5:6], small[:, i, 4:5], AF.Sqrt, [ri], [ri], scale=1.0 / D, bias=EPS)
                S.add("dve", (lambda o, a: (lambda e: e.reciprocal(o, a)))(small[:, i, 6:7], small[:, i, 5:6]), [ri], [ri])
                stt(x_t, cur, small[:, i, 6:7], gam, ALU.mult, ALU.mult, [rcur, ri, rb], [rx])
                tt(x_t, x_t, shf, ALU.add, [rx, rb], [rx])

        def stage_c(i):
            x_t, rx = xbuf[i % 4]
            if nrmsel is not None:
                for q4 in range(4):
                    p, rp = psum()
                    for j in range(4):
                        kt = q4 * 4 + j
                        tr(p[:, j * 128:(j + 1) * 128], x_t[:, kt * 128:(kt + 1) * 128], ident[:, :], [rx, R_c], [rp])
                    b_ = q4 // 2
                    k0 = (q4 % 2) * 4
                    dst = bf(BIG[b_])[:, 0:8 * T].rearrange("p (k t) -> p k t", t=T)[:, k0:k0 + 4, i * 128:(i + 1) * 128]
                    src = p[:, :].rearrange("p (k t) -> p k t", t=128)
                    act(dst, src, AF.Copy, [rp], [RH[b_][k0:k0 + 4]])

        issue_loads(0)
        for s_ in range(NT + 2):
            if 0 <= s_ - 2 < NT:
                stage_c(s_ - 2)
            if 0 <= s_ - 1 < NT:
                stage_b(s_ - 1)
            if s_ + 1 < NT:
                issue_loads(s_ + 1)
            if s_ < NT:
                stage_a(s_)
        if nrmsel is not None and nrmsel[1] == 0:
            dma("sp", hT_d[0], BIG[0][:, 0:4096], [RH[0]], [R_hTd])
            dma("sp", hT_d[1], BIG[1][:, 0:4096], [RH[1]], [R_hTd])

    def out_block_to_fd(stage, rstage, tstage, rtstage, d0):
        sv = stage
        tv = tstage
        for i in range(NT):
            p, rp = psum()
            for j in range(4):
                tr(p[:, j * 128:(j + 1) * 128], sv[:, j, i * 128:(i + 1) * 128], ident[:, :], [rstage, R_c], [rp])
            if i % 2 == 0:
                cpy(tv[:, i, :], p[:, :], [rp], [rtstage])
            else:
                act(tv[:, i, :], p[:, :], AF.Copy, [rp], [rtstage])
            dma("sp", f_d[i * 128:(i + 1) * 128, d0 * 128:d0 * 128 + 512], tv[:, i, :], [rtstage], [R_f[i]])

    def shifted(a, g, kind, d):
        if g == 0:
            a3 = a.rearrange("p (r c) -> p r c", c=64)
            if kind == "row":
                lo, hi = max(0, -d), 64 - max(0, d)
                return a3[:, :, lo:hi], a3[:, :, lo + d:hi + d]
            lo, hi = max(0, -d), 16 - max(0, d)
            return a3[:, lo:hi, :], a3[:, lo + d:hi + d, :]
        a3 = a.rearrange("p (r c) -> p r c", c=256)
        lo, hi = max(0, -d), 256 - max(0, d)
        return a3[:, :, lo:hi], a3[:, :, lo + d:hi + d]

    def mixer(l, g):
        wl = w_in[l]
        hv, hr = hT_views(0, 1, fine=True)

        def rhs_h(hv, hr):
            return lambda kt, tb: (hv[kt][:, tb * 512:(tb + 1) * 512], [hr[kt]])

        qkvT = [bf(BIG[2 + j])[:, 0:8 * T].rearrange("p (f t) -> p f t", t=T) for j in range(3)]
        for j in range(3):
            for pn in range(2):
                c0 = OFF_QKV + j * WC + pn * 512

                def ev(ft, tb, p, rp, j=j):
                    bt = OFF_QKV // 128 + j * 8 + ft
                    dst = qkvT[j][:, ft, tb * 512:(tb + 1) * 512]
                    if j == 0:
                        act(dst, p[:, :], AF.Identity, [rp, R_prm], [RB[2]], bias=prm[:, P_BQ + ft:P_BQ + ft + 1], scale=1.0 / 16.0)
                    elif (ft + tb) % 2 == 0:
                        act(dst, p[:, :], AF.Identity, [rp, R_prm], [RB[2 + j]], bias=prm[:, P_BIN + bt:P_BIN + bt + 1])
                    else:
                        ts(dst, p[:, :], prm[:, P_BIN + bt:P_BIN + bt + 1], None, ALU.add, None, [rp, R_prm], [RB[2 + j]])
                linear(wl[:, c0:c0 + 512], 16, 512, rhs_h(hv, hr), ev, ft0=pn * 4)
        gT = SC[3]
        i = wbn[0] % 2
        wbn[0] += 1
        gv = WBUF[i][:, 0:256].rearrange("p (k n) -> p k n", n=16)
        dma("pool", gv, wl[:, OFF_GATES:OFF_MERGE].rearrange("(k p) n -> p k n", p=128), (), [RW[i]], nonc=True)
        for tb in range(2):
            p, rp = psum()
            for kt in range(16):
                mm(p[0:16, :], gv[:, kt, :], hv[kt][:, tb * 512:(tb + 1) * 512], kt == 0, kt == 15, [RW[i], hr[kt]], [rp])
            act(gT[0:16, tb * 512:(tb + 1) * 512], p[0:16, :], AF.Identity, [rp, R_prm], [RS[3]], bias=gbias[:, 0:1])

        claim(0)
        claim(1)
        mlstm(l, g, qkvT, gT)

        hv, hr = hT_views(1, 2)
        dma("sp", BIG[1][:, 0:4096], hT_d[0], [R_hTd], [RB[1], R_kp, R_va, R_sp, R_cb])
        dma("sp", BIG[2][:, 0:4096], hT_d[1], [R_hTd], [RB[2]])
        hcT = bf(BIG[0])[:, 0:8 * T].rearrange("p (f t) -> p f t", t=T)

        for pn in range(2):
            c0 = OFF_OG + pn * 512

            def ev(ft, tb, p, rp):
                bt = OFF_OG // 128 + ft
                tmp = SC[(ft + tb) % 2][:, 0:512]
                rt = RS[(ft + tb) % 2]
                act(tmp, p[:, :], AF.Sigmoid, [rp, R_prm], [rt], bias=prm[:, P_BIN + bt:P_BIN + bt + 1])
                tt(hcT[:, ft, tb * 512:(tb + 1) * 512], hcT[:, ft, tb * 512:(tb + 1) * 512], tmp, ALU.mult, [rt, RB[0]], [RB[0]])
            linear(wl[:, c0:c0 + 512], 16, 512, rhs_h(hv, hr), ev, ft0=pn * 4)

        ga = BIG[3][:, 0:4 * T].rearrange("p (f t) -> p f t", t=T)
        gb = BIG[4][:, 0:4 * T].rearrange("p (f t) -> p f t", t=T)
        acc = BIG[5][:, 0:4 * T].rearrange("p (f t) -> p f t", t=T)
        sq = BIG[6][:, 0:4 * T].rearrange("p (f t) -> p f t", t=T)
        aT = bf(BIG[7])[:, 0:4 * T].rearrange("p (f t) -> p f t", t=T)
        pT = bf(BIG[7])[:, 4 * T:8 * T].rearrange("p (f t) -> p f t", t=T)
        for pn in range(2):
            def ev(ft, tb, p, rp):
                if ft < 4:
                    act(ga[:, ft, tb * 512:(tb + 1) * 512], p[:, :], AF.Identity, [rp, R_prm], [RB[3]], bias=prm[:, P_BIN + ft:P_BIN + ft + 1])
                else:
                    act(gb[:, ft - 4, tb * 512:(tb + 1) * 512], p[:, :], AF.Sigmoid, [rp, R_prm], [RB[4]], bias=prm[:, P_BIN + ft:P_BIN + ft + 1])
            linear(wl[:, pn * 512:(pn + 1) * 512], 16, 512, rhs_h(hv, hr), ev, ft0=pn * 4)
        caw = prm[:, P_CAW:P_CAW + 124].rearrange("p (f k) -> p f k", k=31)
        for ft in range(4):
            tt(ga[:, ft, :], ga[:, ft, :], gb[:, ft, :], ALU.mult, [RB[3], RB[4]], [RB[3]])
            ts(acc[:, ft, :], ga[:, ft, :], caw[:, ft, 15:16], prm[:, P_CAB + ft:P_CAB + ft + 1], ALU.mult, ALU.add, [RB[3], R_prm], [RB[5]])
            for k in range(31):
                d = k - 15
                if d == 0:
                    continue
                dv, _ = shifted(acc[:, ft, :], g, "row", d)
                _, sv = shifted(ga[:, ft, :], g, "row", d)
                stt(dv, sv, caw[:, ft, k:k + 1], dv, ALU.mult, ALU.add, [RB[3], RB[5], R_prm], [RB[5]])
            act(sq[:, ft, :], acc[:, ft, :], AF.Square, [RB[5]], [RB[6]])
        for tb in range(2):
            p1, r1 = psum()
            p2, r2 = psum()
            for ft in range(4):
                mm(p1[:, :], ones[:, :], acc[:, ft, tb * 512:(tb + 1) * 512], ft == 0, ft == 3, [R_c, RB[5]], [r1])
            for ft in range(4):
                mm(p2[:, :], ones[:, :], sq[:, ft, tb * 512:(tb + 1) * 512], ft == 0, ft == 3, [R_c, RB[6]], [r2])
            mean = SC[0][:, 0:512]
            var = SC[1][:, 0:512]
            act(mean, p1[:, :], AF.Copy, [r1], [RS[0]], scale=1.0 / WA)
            tt(var, mean, mean, ALU.mult, [RS[0]], [RS[1]])
            stt(var, p2[:, :], 1.0 / WA, var, ALU.mult, ALU.subtract, [r2, RS[1]], [RS[1]])
            ts(var, var, EPS, None, ALU.add, None, [RS[1]], [RS[1]])
            act(var, var, AF.Sqrt, [RS[1]], [RS[1]])
            S.add("dve", (lambda o: (lambda e: e.reciprocal(o, o)))(var), [RS[1]], [RS[1]])
            for ft in range(4):
                tmp = SC[2][:, 0:512]
                tt(tmp, acc[:, ft, tb * 512:(tb + 1) * 512], mean, ALU.subtract, [RB[5], RS[0]], [RS[2]])
                tt(tmp, tmp, var, ALU.mult, [RS[2], RS[1]], [RS[2]])
                act(aT[:, ft, tb * 512:(tb + 1) * 512], tmp, AF.Silu, [RS[2], R_prm], [RB[7]],
                    bias=prm[:, P_LNB + ft:P_LNB + ft + 1], scale=prm[:, P_LNW + ft:P_LNW + ft + 1])

        zp = BIG[3][:, 0:4 * T].rearrange("p (f t) -> p f t", t=T)
        pacc = BIG[4][:, 0:4 * T].rearrange("p (f t) -> p f t", t=T)

        def ev(ft, tb, p, rp):
            bt = OFF_POOL // 128 + ft
            act(zp[:, ft, tb * 512:(tb + 1) * 512], p[:, :], AF.Identity, [rp, R_prm], [RB[3]], bias=prm[:, P_BIN + bt:P_BIN + bt + 1])
        linear(wl[:, OFF_POOL:OFF_POOL + 512], 16, 512, rhs_h(hv, hr), ev)
        for ft in range(4):
            w = POOLW[ft]
            cpy(pacc[:, ft, :], zp[:, ft, :], [RB[3]], [RB[4]])
            for d in range(-(w // 2), w // 2):
                if d == 0:
                    continue
                dv, _ = shifted(pacc[:, ft, :], g, "col", d)
                _, sv = shifted(zp[:, ft, :], g, "col", d)
                tt(dv, dv, sv, ALU.add, [RB[3], RB[4]], [RB[4]])
            ic = SC[0]
            dma("sp", ic[:, :], c_invcnt[g, ft].partition_broadcast(128), (), [RS[0]])
            tt(pacc[:, ft, :], pacc[:, ft, :], ic[:, :], ALU.mult, [RB[4], RS[0]], [RB[4]])
            tt(pT[:, ft, :], pacc[:, ft, :], zp[:, ft, :], ALU.subtract, [RB[4], RB[3]], [RB[7]])

        def mixv(dt_):
            b = 3 if dt_ < 8 else 4
            return bf(BIG[b])[:, (dt_ % 8) * T:(dt_ % 8 + 1) * T], RB[b]
        dma("pool", wpool_sb[:, :, :], w_pool[l].rearrange("g c d -> c g d"), (), [r_wp])
        for db in range(4):
            wa_v, wa_r = load_panel(w_a_out[l][:, db * 512:(db + 1) * 512], 4, 512)
            wc_v, wc_r = load_panel(w_c_out[l][:, db * 512:(db + 1) * 512], 8, 512)
            ya = BIG[5][:, 0:4 * T].rearrange("p (f t) -> p f t", t=T)
            yc = BIG[6][:, 0:4 * T].rearrange("p (f t) -> p f t", t=T)
            for dj in range(4):
                for tb in range(2):
                    p, rp = psum()
                    for kt in range(4):
                        mm(p[:, :], wa_v[:, kt, dj * 128:(dj + 1) * 128], aT[:, kt, tb * 512:(tb + 1) * 512], kt == 0, kt == 3, [wa_r, RB[7]], [rp])
                    act(ya[:, dj, tb * 512:(tb + 1) * 512], p[:, :], AF.Copy, [rp], [RB[5]])
                    p, rp = psum()
                    for kt in range(8):
                        mm(p[:, :], wc_v[:, kt, dj * 128:(dj + 1) * 128], hcT[:, kt, tb * 512:(tb + 1) * 512], kt == 0, kt == 7, [wc_r, RB[0]], [rp])
                    cpy(yc[:, dj, tb * 512:(tb + 1) * 512], p[:, :], [rp], [RB[6]])
            for br in range(3):
                c0 = OFF_MERGE + br * D + db * 512

                def ev(ft, tb, p, rp, br=br, db=db):
                    dt_ = db * 4 + ft
                    bt = 44 + br * 16 + dt_
                    gt = SC[(ft + tb) % 2][:, 0:512]
                    rg = RS[(ft + tb) % 2]
                    act(gt, p[:, :], AF.Sigmoid, [rp, R_prm], [rg], bias=prm[:, P_BIN + bt:P_BIN + bt + 1])
                    mv, mr = mixv(dt_)
                    msl = mv[:, tb * 512:(tb + 1) * 512]
                    if br == 0:
                        tt(ya[:, ft, tb * 512:(tb + 1) * 512], ya[:, ft, tb * 512:(tb + 1) * 512], gt, ALU.mult, [rg, RB[5]], [RB[5]])
                    elif br == 1:
                        gidx = dt_ // 4
                        p2, rp2 = psum()
                        mm(p2[:, :], wpool_sb[:, gidx, (dt_ % 4) * 128:(dt_ % 4 + 1) * 128], pT[:, gidx, tb * 512:(tb + 1) * 512], True, True, [r_wp, RB[7]], [rp2])
                        tmp = SC[2][:, 0:512]
                        stt(tmp, p2[:, :], prm[:, P_PSC + dt_:P_PSC + dt_ + 1], gt, ALU.mult, ALU.mult, [rp2, rg, R_prm], [RS[2]])
                        tt(ya[:, ft, tb * 512:(tb + 1) * 512], ya[:, ft, tb * 512:(tb + 1) * 512], tmp, ALU.add, [RS[2], RB[5]], [RB[5]])
                    else:
                        tt(yc[:, ft, tb * 512:(tb + 1) * 512], yc[:, ft, tb * 512:(tb + 1) * 512], gt, ALU.mult, [rg, RB[6]], [RB[6]])
                        tt(msl, ya[:, ft, tb * 512:(tb + 1) * 512], yc[:, ft, tb * 512:(tb + 1) * 512], ALU.add, [RB[5], RB[6]], [mr])
                linear(wl[:, c0:c0 + 512], 16, 512, rhs_h(hv, hr), ev)

        for db in range(4):
            stage = BIG[5][:, 0:4 * T].rearrange("p (f t) -> p f t", t=T)
            tstage = BIG[6][:, 0:8 * 512].rearrange("p (i n) -> p i n", n=512)

            def ev(ft, tb, p, rp):
                if (ft + tb) % 2 == 0:
                    cpy(stage[:, ft, tb * 512:(tb + 1) * 512], p[:, :], [rp], [RB[5]])
                else:
                    act(stage[:, ft, tb * 512:(tb + 1) * 512], p[:, :], AF.Copy, [rp], [RB[5]])
            linear(w_out[l][:, db * 512:(db + 1) * 512], 16, 512, lambda kt, tb: (mixv(kt)[0][:, tb * 512:(tb + 1) * 512], [mixv(kt)[1]]), ev)
            out_block_to_fd(stage, RB[5], tstage, RB[6], db * 4)

    def mlstm(l, g, qkvT, gT):
        qT, kT, vT = qkvT
        Rq, Rk, Rv = RB[2], RB[3], RB[4]
        nseq, nch = (1, 16) if g == 0 else (4, 4)
        LF, LI, BC_, UU, ABC, BL, MALL, MPREV, DEC, CSC, KFAC, ATOK, KSC, THR = [gm[:, i, :] for i in range(14)]
        p, rp = psum()
        for c in range(16):
            tr(p[0:64, c * 16:(c + 1) * 16], gT[0:16, c * 64:(c + 1) * 64], ident[0:16, 0:16], [RS[3], R_c], [rp])
        cpy(G[:, :], p[0:64, 0:256], [rp], [rG])
        G5 = G[:, :].rearrange("p (c d k h) -> p c d k h", d=2, k=2, h=4)
        for dr in range(2):
            lfv = LF[0:64, dr * 64:(dr + 1) * 64].rearrange("p (c h) -> p c h", h=4)
            liv = LI[0:64, dr * 64:(dr + 1) * 64].rearrange("p (c h) -> p c h", h=4)
            act(lfv, G5[:, :, dr, 1, :], AF.Exp, [rG], [rgm], scale=-1.0)
            cpy(liv, G5[:, :, dr, 0, :], [rG], [rgm])
        act(LF[0:64, :], LF[0:64, :], AF.Ln, [rgm], [rgm], bias=1.0)
        ts(LF[0:64, :], LF[0:64, :], -1.0, None, ALU.mult, None, [rgm], [rgm])
        p, rp = psum()
        mm(p[0:64, 0:64], triU[:, :], LF[0:64, 0:64], True, True, [R_c, rgm], [rp])
        mm(p[0:64, 64:128], triL[:, :], LF[0:64, 64:128], True, True, [R_c, rgm], [rp])
        cpy(BC_[0:64, :], p[0:64, 0:128], [rp], [rgm])
        tt(UU[0:64, :], LI[0:64, :], BC_[0:64, :], ALU.subtract, [rgm], [rgm])
        p, rp = psum()
        tr(p[:, 0:64], UU[0:64, :], ident[0:64, 0:64], [rgm, R_c], [rp])
        S.add("dve", (lambda o, a: (lambda e: e.tensor_reduce(o, a, AX.X, ALU.max)))(acol[:, :], p[:, 0:64]), [rp], [rgm])
        diag = SC[0][:, 0:128]
        ts(diag, ident[:, :], acol[:, 0:1], None, ALU.mult, None, [rgm, R_c], [RS[0]])
        p2, rp2 = psum()
        mm(p2[:, 0:128], ones[:, :], diag, True, True, [R_c, RS[0]], [rp2])
        cpy(ABC, p2[:, 0:128], [rp2], [rgm])
        p3, rp3 = psum()
        mm(p3[:, 0:128], ones[0:64, :], LF[0:64, :], True, True, [R_c, rgm], [rp3])
        cpy(BL, p3[:, 0:128], [rp3], [rgm])
        mcur = mcur_all[:, :, 0:nseq, :]
        if g == 0:
            dma("sp", mcur[:, :, 0, :], m0[l].rearrange("(d h) -> d h", h=4).partition_broadcast(128), (), [rgm])
        else:
            mset(mcur[:, :, :, :], 0.0, [rgm])

        def colv(arr, dr, cc):
            return arr[:, dr * 64:(dr + 1) * 64].rearrange("p (q c h) -> p q c h", q=nseq, h=4)[:, :, cc, :]
        for j in range(nch):
            for dr in range(2):
                cc = j if dr == 0 else nch - 1 - j
                cpy(colv(MPREV, dr, cc), mcur[:, dr, :, :], [rgm], [rgm])
                tt(colv(MALL, dr, cc), mcur[:, dr, :, :], colv(ABC, dr, cc), ALU.max, [rgm], [rgm])
                tt(mcur[:, dr, :, :], colv(MALL, dr, cc), colv(BL, dr, cc), ALU.add, [rgm], [rgm])
                if g == 1 and j == nch - 1:
                    pass
        if g == 1:
            for q in range(4):
                dma("sp", stm[q, l].rearrange("(o d h) -> o d h", o=1, h=4), mcur[0:1, :, q, :], [rgm], [Res()])
        tt(DEC, MPREV, MALL, ALU.subtract, [rgm], [rgm])
        act(DEC, DEC, AF.Exp, [rgm], [rgm])
        tt(CSC, MPREV, ABC, ALU.subtract, [rgm], [rgm])
        act(CSC, CSC, AF.Exp, [rgm], [rgm])
        tt(KFAC, ABC, MALL, ALU.subtract, [rgm], [rgm])
        act(KFAC, KFAC, AF.Exp, [rgm], [rgm])
        tt(ATOK[0:64, :], UU[0:64, :], ABC[0:64, :], ALU.subtract, [rgm], [rgm])
        act(ATOK[0:64, :], ATOK[0:64, :], AF.Exp, [rgm], [rgm])
        tt(KSC[0:64, :], ATOK[0:64, :], KFAC[0:64, :], ALU.mult, [rgm], [rgm])
        tt(THR[0:64, :], BC_[0:64, :], ABC[0:64, :], ALU.add, [rgm], [rgm])
        act(THR[0:64, :], THR[0:64, :], AF.Exp, [rgm], [rgm], scale=-1.0)

        Caug = BIG[7][:, 0:8 * 2 * 257].rearrange("p (s k e) -> p s k e", k=2, e=257)
        RC = RB[7]
        hsum = bf(BIG[5])[0:64, :]
        hsum2 = bf(BIG[6])[0:64, :]

        def hs(c, h):
            b = hsum if c < 8 else hsum2
            return b[:, (c % 8) * 1024 + h * 256:(c % 8) * 1024 + (h + 1) * 256]
        Rh = [RB[5], RB[6]]
        mset(bf(BIG[5])[0:64, 0:8192], 0.0, [RB[5], Rhc[0:8]])
        mset(bf(BIG[6])[0:64, 0:8192], 0.0, [RB[6], Rhc[8:16]])
        wk = bf(BIG[1])
        kp = wk[0:64, 0:1024].rearrange("p (h e) -> p h e", e=256)
        vaug = wk[0:64, 1024:1024 + 4 * 257].rearrange("p (h e) -> p h e", e=257)
        spT = wk[0:64, 2304:2304 + 256].rearrange("p (h t) -> p h t", t=64)
        cbf = wk[:, 2560:2560 + 4 * 514].rearrange("p (h k e) -> p h k e", k=2, e=257)
        mset(wk[:, 0:4624], 0.0, [RB[1], R_kp, R_va, R_sp, R_cb])
        mset(vaug[:, :, 256:257], 1.0, [R_va])

        for q in range(nseq):
            if g == 0:
                for dr in range(2):
                    for h in range(4):
                        dma("sp", Caug[:, dr * 4 + h, :, 0:256], C0[l, dr, h].rearrange("(k p) e -> p k e", p=128), (), [RCs[dr * 4 + h], RB[7]])
                        dma("sp", Caug[:, dr * 4 + h, :, 256:257], n0[l, dr, h].rearrange("(k p o) -> p k o", p=128, o=1), (), [RCs[dr * 4 + h], RB[7]], nonc=True)
            else:
                mset(Caug[:, :, :, :], 0.0, [RCs, RB[7]])
            for j in range(nch):
                for dr in range(2):
                    cc = j if dr == 0 else nch - 1 - j
                    c = q * nch + cc
                    tsl = slice(c * 64, (c + 1) * 64)
                    col0 = dr * 64 + c * 4
                    pk, rpk = psum()
                    pv_, rpv = psum()
                    pkb = pk[:].bitcast(BF16)
                    pvb = pv_[:].bitcast(BF16)
                    for h in range(4):
                        for kt2 in range(2):
                            tr(pkb[0:64, h * 256 + kt2 * 128:h * 256 + (kt2 + 1) * 128], kT[:, h * 2 + kt2, tsl], identb[:, :], [Rk, R_c], [rpk])
                            tr(pvb[0:64, h * 256 + kt2 * 128:h * 256 + (kt2 + 1) * 128], vT[:, h * 2 + kt2, tsl], identb[:, :], [Rv, R_c], [rpv])
                    for h in range(4):
                        act(kp[:, h, :], pkb[0:64, h * 256:(h + 1) * 256], AF.Copy, [rpk, rgm], [R_kp[h]], scale=KSC[0:64, col0 + h:col0 + h + 1])
                    cpy(vaug[:, :, 0:256], pvb[0:64, 0:1024].rearrange("p (h e) -> p h e", e=256), [rpv], [R_va])
                    pq, rpq = psum()
                    for h in range(4):
                        for kt2 in range(2):
                            mm(pq[0:64, h * 64:(h + 1) * 64], kT[:, h * 2 + kt2, tsl], qT[:, h * 2 + kt2, tsl], kt2 == 0, kt2 == 1, [Rk, Rq], [rpq])
                    msk = triU if dr == 0 else triL
                    for h in range(4):
                        stt(spT[:, h, :], pq[0:64, h * 64:(h + 1) * 64], ATOK[0:64, col0 + h:col0 + h + 1], msk[:, :], ALU.mult, ALU.mult, [rpq, rgm, R_c], [R_sp[h]])
                    for h in range(4):
                        s_ = dr * 4 + h
                        act(cbf[:, h, :, :], Caug[:, s_, :, :], AF.Copy, [RCs[s_], RB[7], rgm], [R_cb[h]], scale=CSC[:, col0 + h:col0 + h + 1])
                    for h in range(4):
                        s_ = dr * 4 + h
                        pp, rpp = psum()
                        mm(pp[0:64, 0:257], spT[:, h, :], vaug[:, h, :], True, False, [R_sp[h], R_va], [rpp])
                        mm(pp[0:64, 0:257], qT[:, h * 2, tsl], cbf[:, h, 0, :], False, False, [Rq, R_cb[h]], [rpp])
                        mm(pp[0:64, 0:257], qT[:, h * 2 + 1, tsl], cbf[:, h, 1, :], False, True, [Rq, R_cb[h]], [rpp])
                        rr = rsm_t[:, h * 2:h * 2 + 1]
                        act(rr, pp[0:64, 256:257], AF.Abs, [rpp], [R_rs[h]])
                        ts(rr, rr, THR[0:64, col0 + h:col0 + h + 1], None, ALU.max, None, [R_rs[h], rgm], [R_rs[h]])
                        S.add("dve", (lambda o: (lambda e: e.reciprocal(o, o)))(rr), [R_rs[h]], [R_rs[h]])
                        hv_ = hs(c, h)
                        stt(hv_, pp[0:64, 0:256], rr, hv_, ALU.mult, ALU.add, [rpp, R_rs[h], Rhc[c], Rh[c // 8]], [Rhc[c]])
                        for kt2 in range(2):
                            pc, rpc = psum()
                            mm(pc[:, 0:257], kp[:, h, kt2 * 128:(kt2 + 1) * 128], vaug[:, h, :], True, True, [R_kp[h], R_va], [rpc])
                            stt(Caug[:, s_, kt2, :], Caug[:, s_, kt2, :], DEC[:, col0 + h:col0 + h + 1], pc[:, 0:257], ALU.mult, ALU.add, [RCs[s_], RB[7], rgm, rpc], [RCs[s_]])
            if g == 1:
                for dr in range(2):
                    for h in range(4):
                        dma("sp", stC[q, l, dr, h].rearrange("(k p) e -> p k e", p=128), Caug[:, dr * 4 + h, :, 0:256], [RCs[dr * 4 + h], RB[7]], [Res()])
                        dma("sp", stn[q, l, dr, h].rearrange("(k p o) -> p k o", p=128, o=1), Caug[:, dr * 4 + h, :, 256:257], [RCs[dr * 4 + h], RB[7]], [Res()], nonc=True)

        hcT = bf(BIG[0])[:, 0:8 * T].rearrange("p (f t) -> p f t", t=T)
        mnw = prm[:, P_MNW:P_MNW + 8]
        for c in range(16):
            hb = hsum if c < 8 else hsum2
            hc_ = hb[:, (c % 8) * 1024:(c % 8 + 1) * 1024]
            sqc = SC[c % 2][0:64, :]
            rq = RS[c % 2]
            act(sqc, hc_, AF.Square, [Rhc[c], Rh[c // 8]], [rq])
            S.add("dve", (lambda o, a: (lambda e: e.tensor_reduce(o, a, AX.X, ALU.add)))(hn_s[:, c, :], sqc.rearrange("p (h e) -> p h e", e=256)), [rq], [R_hn])
            ts(hn_s[:, c, :], hn_s[:, c, :], 1.0 / DH, EPS, ALU.mult, ALU.add, [R_hn], [R_hn])
            act(hn_s[:, c, :], hn_s[:, c, :], AF.Sqrt, [R_hn], [R_hn])
            S.add("dve", (lambda o: (lambda e: e.reciprocal(o, o)))(hn_s[:, c, :]), [R_hn], [R_hn])
            tt(sqc.rearrange("p (h e) -> p h e", e=256), hc_.rearrange("p (h e) -> p h e", e=256),
               hn_s[:, c, :].unsqueeze(2).to_broadcast([64, 4, 256]), ALU.mult, [Rhc[c], Rh[c // 8], R_hn], [rq])
            p, rp = psum()
            for ft in range(8):
                tr(p[:, ft * 64:(ft + 1) * 64], sqc[:, ft * 128:(ft + 1) * 128], ident[0:64, 0:64], [rq, R_c], [rp])
            tt(hcT[:, :, c * 64:(c + 1) * 64], p[:, :].rearrange("p (f t) -> p f t", t=64),
               mnw.unsqueeze(2).to_broadcast([128, 8, 64]), ALU.mult, [rp, R_prm], [RB[0]])

    def ffn(l, g):
        hv, hr = hT_views(0, 1, fine=True)
        rhs = lambda kt, tb: (hv[kt][:, tb * 512:(tb + 1) * 512], [hr[kt]])

        def actv(j):
            b = 2 + j // 8
            return bf(BIG[b])[:, (j % 8) * T:(j % 8 + 1) * T], RB[b]
        fcw = prm[:, P_FCW:P_FCW + 132].rearrange("p (f k) -> p f k", k=3)
        u_sb, g_sb, gc, t1 = SC[0], SC[1], SC[2], SC[3]
        for qb in range(11):
            def ev_u(ft, tb, p, rp):
                if tb == 0:
                    cpy(u_sb[:, 0:512], p[:, :], [rp], [RS[0]])
                else:
                    act(u_sb[:, 512:1024], p[:, :], AF.Copy, [rp], [RS[0]])

            def ev_g(ft, tb, p, rp):
                if tb == 0:
                    act(g_sb[:, 0:512], p[:, :], AF.Copy, [rp], [RS[1]])
                else:
                    cpy(g_sb[:, 512:1024], p[:, :], [rp], [RS[1]])
            vu, ru = load_panel(w_ffn_up[l][:, qb * 512:(qb + 1) * 512], 16, 512)
            vg, rg_ = load_panel(w_ffn_up[l][:, DFF + qb * 512:DFF + (qb + 1) * 512], 16, 512)
            for ft in range(4):
                j = qb * 4 + ft
                for (vv, rv, evf) in ((vu, ru, ev_u), (vg, rg_, ev_g)):
                    for tb in range(2):
                        p, rp = psum()
                        for kt in range(16):
                            mm(p[:, :], vv[:, kt, ft * 128:(ft + 1) * 128], hv[kt][:, tb * 512:(tb + 1) * 512], kt == 0, kt == 15, [rv, hr[kt]], [rp])
                        evf(ft, tb, p, rp)
                ts(gc[:, :], g_sb[:, :], fcw[:, j, 1:2], prm[:, P_FCB + j:P_FCB + j + 1], ALU.mult, ALU.add, [RS[1], R_prm], [RS[2]])
                for k in (0, 2):
                    d = k - 1
                    dv, _ = shifted(gc[:, :], g, "col", d)
                    _, sv = shifted(g_sb[:, :], g, "col", d)
                    stt(dv, sv, fcw[:, j, k:k + 1], dv, ALU.mult, ALU.add, [RS[1], RS[2], R_prm], [RS[2]])
                act(t1[:, :], gc[:, :], AF.Square, [RS[2]], [RS[3]])
                ts(t1[:, :], t1[:, :], 0.044715, 1.0, ALU.mult, ALU.add, [RS[3]], [RS[3]])
                tt(t1[:, :], t1[:, :], gc[:, :], ALU.mult, [RS[3], RS[2]], [RS[3]])
                act(t1[:, :], t1[:, :], AF.Sigmoid, [RS[3]], [RS[3]], scale=1.5957691216057308)
                tt(gc[:, :], gc[:, :], u_sb[:, :], ALU.mult, [RS[2], RS[0]], [RS[2]])
                av, ar = actv(j)
                tt(av, gc[:, :], t1[:, :], ALU.mult, [RS[2], RS[3]], [ar])
        claim(0)
        claim(1)
        for db in range(4):
            stage = BIG[0][:, 0:4 * T].rearrange("p (f t) -> p f t", t=T)
            tstage = BIG[1][:, 0:8 * 512].rearrange("p (i n) -> p i n", n=512)
            for dj in range(4):
                dt_ = db * 4 + dj

                def ev(ft, tb, p, rp, dj=dj):
                    if tb == 0:
                        cpy(stage[:, dj, 0:512], p[:, :], [rp], [RB[0]])
                    else:
                        act(stage[:, dj, 512:1024], p[:, :], AF.Copy, [rp], [RB[0]])
                linear(w_ffn_down[l][:, dt_ * 128:(dt_ + 1) * 128], 44, 128, lambda kt, tb: (actv(kt)[0][:, tb * 512:(tb + 1) * 512], [actv(kt)[1]]), ev)
            out_block_to_fd(stage, RB[0], tstage, RB[1], db * 4)

    for l in range(depth):
        mod_phase(l)
    for g in GORDER:
        load_params(0)
        fused_pass(g, None, (0, 0))
        for l in range(depth):
            mixer(l, g)
            fused_pass(g, (l, 0), (l, 1))
            ffn(l, g)
            if l + 1 < depth:
                load_params(l + 1)
                fused_pass(g, (l, 1), (l + 1, 0))
            else:
                fused_pass(g, (l, 1), None)
    S.emit(nc, es)
    es.close()
    return nc


_CACHE = {}


def _consts():
    ident = np.eye(128, dtype=np.float32)
    s = np.arange(64)
    triU = (s[:, None] <= s[None, :]).astype(np.float32)
    triL = (s[:, None] >= s[None, :]).astype(np.float32)
    inv = np.zeros((2, 4, T), np.float32)
    for gi, L in enumerate((16, 256)):
        t = np.arange(L)
        for wi, w in enumerate(POOLW):
            lo = np.clip(t - w // 2, 0, L)
            hi = np.clip(t - w // 2 + w, 0, L)
            ic = (1.0 / (hi - lo)).astype(np.float32)
            if gi == 0:
                inv[gi, wi] = np.repeat(ic, 64)
            else:
                inv[gi, wi] = np.tile(ic, 4)
    return ident, triU, triL, inv


def kernel(x_prompt, x_sample, state_C, state_n, state_m, c, c_ctx, w_ada, b_ada, norm_mix_pre, norm_mix_post,
           norm_ffn_pre, norm_ffn_post, w_in, b_in, conv_a_w, conv_a_b, ln_a_w, ln_a_b, w_a_out, w_pool,
           pool_scale, mlstm_norm_w, w_c_out, w_out, w_ffn_up, ffn_conv_w, ffn_conv_b, w_ffn_down, _depth=None,
           _cores=8):
    f = lambda a: np.ascontiguousarray(np.asarray(a, dtype=np.float32))
    depth = _depth or w_ada.shape[0]
    if depth not in _CACHE:
        _CACHE[depth] = build(depth)
    nc = _CACHE[depth]
    ident, triU, triL, inv = _consts()
    shared = {"w_ada": f(w_ada)[:depth], "b_ada": f(b_ada)[:depth], "norm_mix_pre": f(norm_mix_pre)[:depth],
              "norm_mix_post": f(norm_mix_post)[:depth], "norm_ffn_pre": f(norm_ffn_pre)[:depth],
              "norm_ffn_post": f(norm_ffn_post)[:depth], "w_in": f(w_in)[:depth], "b_in": f(b_in)[:depth],
              "conv_a_w": f(conv_a_w)[:depth], "conv_a_b": f(conv_a_b)[:depth], "ln_a_w": f(ln_a_w)[:depth],
              "ln_a_b": f(ln_a_b)[:depth], "w_a_out": f(w_a_out)[:depth], "w_pool": f(w_pool)[:depth],
              "pool_scale": f(pool_scale)[:depth], "mlstm_norm_w": f(mlstm_norm_w)[:depth],
              "w_c_out": f(w_c_out)[:depth], "w_out": f(w_out)[:depth], "w_ffn_up": f(w_ffn_up)[:depth],
              "ffn_conv_w": f(ffn_conv_w)[:depth], "ffn_conv_b": f(ffn_conv_b)[:depth],
              "w_ffn_down": f(w_ffn_down)[:depth], "c_ident": ident, "c_triU": triU, "c_triL": triL,
              "c_invcnt": inv}
    xp = f(x_prompt)
    xsm = f(x_sample)
    sC, sn, sm = f(state_C), f(state_n), f(state_m)
    cc, cctx = f(c), f(c_ctx)
    in_maps = []
    for core in range(_cores):
        b = core % 4
        m = dict(shared)
        m["xs"] = np.ascontiguousarray(np.concatenate([xsm[b], xp[core * 4:(core + 1) * 4].reshape(T, D)], axis=0))
        m["cond"] = np.ascontiguousarray(np.stack([cc[b], cctx], axis=0))
        m["C0"] = np.ascontiguousarray(sC[b][:depth])
        m["n0"] = np.ascontiguousarray(sn[b][:depth])
        m["m0"] = np.ascontiguousarray(sm[b][:depth].reshape(depth, 8))
        in_maps.append(m)
    res = run_bass_kernel_spmd(nc, in_maps, core_ids=list(range(_cores)))
    R = res.results
    B = x_prompt.shape[0]
    yp = np.zeros((B, 256, D), np.float32)
    ys = np.zeros((4, T, D), np.float32)
    nC = np.zeros((B, depth, 2, NH, DH, DH), np.float32)
    nn = np.zeros((B, depth, 2, NH, DH), np.float32)
    nm = np.zeros((B, depth, 2, NH), np.float32)
    for core in range(_cores):
        r = R[core]
        yy = np.asarray(r["y"])
        if core < 4:
            ys[core] = yy[0:T]
        yp[core * 4:(core + 1) * 4] = yy[T:2 * T].reshape(4, 256, D)
        nC[core * 4:(core + 1) * 4] = np.asarray(r["stC"])
        nn[core * 4:(core + 1) * 4] = np.asarray(r["stn"])
        nm[core * 4:(core + 1) * 4] = np.asarray(r["stm"]).reshape(4, depth, 2, NH)
    return yp, ys, nC, nn, nm
```
